# Optimizing a Trainium2 kernel written in Bass

```python
import math
import jax, jax.numpy as jnp
from jax import lax
import numpy as np

D_MODEL = 1024
BATCH = 8
SEQ = 2048
DEPTH = 4

GRID_W = 64
CTX_LEN = 256

N_MIXERS = 3
MIXER_HYENA = 0
MIXER_MLA = 1
MIXER_GDN = 2
N_HYENA = (DEPTH + 2) // 3
N_MLA = (DEPTH + 1) // 3
N_GDN = DEPTH // 3

DEEPNORM_ALPHA = (2 * DEPTH) ** 0.25
DEEPNORM_BETA = (8 * DEPTH) ** -0.25
LN_EPS = 1e-5
RMS_EPS = 1e-6

D_FF = -(-8 * D_MODEL // (3 * 256)) * 256

HYENA_SHORT = 3
HYENA_EMB = 33
HYENA_FILTER_HIDDEN = 64
HYENA_DECAY_TARGET = 1e-2
HYENA_FAST_DECAY_PCT = 0.3
HYENA_SLOW_DECAY_PCT = 1.5

MLA_HEADS = 8
MLA_Q_LORA = 384
MLA_KV_LORA = 256
MLA_NOPE = 128
MLA_ROPE = 64
MLA_V = 128
MLA_SCALE = (MLA_NOPE + MLA_ROPE) ** -0.5
ROPE_THETA = 10000.0
Q_BLOCK = 128

GDN_HEADS = 8
GDN_DK = 128
GDN_DV = 128
GDN_SHORT = 3
GDN_CHUNK = 64

kernel_name = "hybrid_hyena_mla_gdn_flow_backbone"


def layer_norm(x, g, b):
    xf = x.astype(jnp.float32)
    mu = jnp.mean(xf, axis=-1, keepdims=True)
    var = jnp.mean(jnp.square(xf - mu), axis=-1, keepdims=True)
    return ((xf - mu) * lax.rsqrt(var + LN_EPS) * g + b).astype(x.dtype)


def rms_norm(x, g):
    xf = x.astype(jnp.float32)
    return (xf * lax.rsqrt(jnp.mean(jnp.square(xf), axis=-1, keepdims=True) + RMS_EPS) * g).astype(x.dtype)


def l2_normalize(x):
    xf = x.astype(jnp.float32)
    return (xf * lax.rsqrt(jnp.sum(xf * xf, axis=-1, keepdims=True) + RMS_EPS)).astype(x.dtype)


def centred_depthwise_conv(u, w):
    width, chans = w.shape
    return lax.conv_general_dilated(
        u, w[:, None, :].astype(u.dtype), window_strides=(1,),
        padding=[((width - 1) // 2, width // 2)],
        dimension_numbers=("NWC", "WIO", "NWC"), feature_group_count=chans)


def adaln(cvec, w, b):
    return jnp.split(jax.nn.silu(cvec) @ w + b, 6, axis=-1)


def swiglu(h, w_in, w_out):
    gate, up = jnp.split(h @ w_in, 2, axis=-1)
    return (jax.nn.silu(gate) * up) @ w_out


def mixer_residual_and_ffn(s, y_mix, m, g, b, w_in, w_out):
    s = layer_norm(DEEPNORM_ALPHA * s + m[2] * y_mix, g[0], b[0])
    f = swiglu(s * (1.0 + m[4]) + m[3], w_in, w_out)
    return layer_norm(DEEPNORM_ALPHA * s + m[5] * f, g[1], b[1])


def hyena_position_features(length):
    t01 = jnp.linspace(0.0, 1.0, length, dtype=jnp.float32)[:, None]
    bands = (HYENA_EMB - 1) // 2
    w = (2.0 * math.pi / length) * jnp.arange(length, dtype=jnp.float32)
    f = jnp.linspace(1e-4, bands - 1, bands, dtype=jnp.float32)
    ang = w[:, None] * f[None, :]
    return jnp.concatenate([t01, jnp.cos(ang), -jnp.sin(ang)], axis=-1)


def hyena_decay_window(length):
    max_decay = math.log(HYENA_DECAY_TARGET) / HYENA_FAST_DECAY_PCT
    min_decay = math.log(HYENA_DECAY_TARGET) / HYENA_SLOW_DECAY_PCT
    deltas = jnp.abs(jnp.linspace(min_decay, max_decay, D_MODEL, dtype=jnp.float32))
    t = jnp.linspace(0.0, 1.0, length, dtype=jnp.float32)
    return jnp.exp(-t[:, None] * deltas[None, :])


def hyena_filters(length, p):
    f32 = jnp.float32
    freq = p["freq"].astype(f32)
    z = hyena_position_features(length)
    z = jnp.sin(freq * (z @ p["fw1"].astype(f32) + p["fb1"].astype(f32)))
    z = jnp.sin(freq * (z @ p["fw2"].astype(f32) + p["fb2"].astype(f32)))
    z = jnp.sin(freq * (z @ p["fw3"].astype(f32) + p["fb3"].astype(f32)))
    k_fwd, k_bwd = jnp.split(z @ p["fw4"].astype(f32), 2, axis=-1)
    window = hyena_decay_window(length)
    return k_fwd * window, k_bwd * window


def two_sided_fft_conv(u, k_fwd, k_bwd):
    length = u.shape[1]
    pad = ((0, length), (0, 0))
    taps = jnp.pad(k_fwd, pad) + jnp.roll(jnp.pad(k_bwd, pad)[::-1], 1, axis=0)
    spec = jnp.fft.rfft(u, n=2 * length, axis=1) * jnp.fft.rfft(taps, axis=0)[None]
    return jnp.fft.irfft(spec, n=2 * length, axis=1)[:, :length]


def hyena_mixer(h, p):
    length = h.shape[1]
    z = centred_depthwise_conv(h @ p["w_in"] + p["b_in"], p["conv_w"]) + p["conv_b"]
    x0, x1, v = jnp.split(z, 3, axis=-1)
    k_fwd, k_bwd = hyena_filters(length, p)
    u = (x1 * v).astype(jnp.float32)
    y = two_sided_fft_conv(u, k_fwd, k_bwd) + u * p["skip"].astype(jnp.float32)
    return (x0 * y.astype(h.dtype)) @ p["w_out"] + p["b_out"]


def axial_rope(t):
    length = t.shape[1]
    rows = length // GRID_W
    row, col = jnp.meshgrid(jnp.arange(rows, dtype=jnp.float32), jnp.arange(GRID_W, dtype=jnp.float32), indexing="ij")
    half = MLA_ROPE // 2
    quarter = half // 2
    inv_freq = ROPE_THETA ** (-jnp.arange(quarter, dtype=jnp.float32) / quarter)

    def rotate(seg, pos):
        ang = (pos[:, None] * inv_freq[None, :])[:, None, :]
        cos, sin = jnp.cos(ang), jnp.sin(ang)
        a, b = seg[..., :quarter], seg[..., quarter:]
        return jnp.concatenate([a * cos - b * sin, b * cos + a * sin], axis=-1)

    tf = t.astype(jnp.float32)
    out = jnp.concatenate([rotate(tf[..., :half], row.reshape(-1)), rotate(tf[..., half:], col.reshape(-1))], axis=-1)
    return out.astype(t.dtype)


def mla_project(h, p, with_pos):
    bsz, length, _ = h.shape
    cq, ckv, k_rope = jnp.split(h @ p["w_in"], [MLA_Q_LORA, MLA_Q_LORA + MLA_KV_LORA], axis=-1)
    q = (rms_norm(cq, p["q_norm"]) @ p["w_uq"]).reshape(bsz, length, MLA_HEADS, MLA_NOPE + MLA_ROPE)
    kv = (rms_norm(ckv, p["kv_norm"]) @ p["w_ukv"]).reshape(bsz, length, MLA_HEADS, MLA_NOPE + MLA_V)
    q_nope, q_rope = jnp.split(q, [MLA_NOPE], axis=-1)
    k_nope, v = jnp.split(kv, [MLA_NOPE], axis=-1)
    k_rope = k_rope[:, :, None, :]
    if with_pos:
        q_rope = axial_rope(q_rope)
        k_rope = axial_rope(k_rope)
    q = jnp.concatenate([q_nope, q_rope], axis=-1)
    k = jnp.concatenate([k_nope, jnp.broadcast_to(k_rope, (bsz, length, MLA_HEADS, MLA_ROPE))], axis=-1)
    to_heads = lambda a: a.transpose(0, 2, 1, 3)
    return to_heads(q), to_heads(k), to_heads(v)


def blocked_softmax_attention(q, k, v):
    bsz, heads, lq, dqk = q.shape
    dv = v.shape[-1]
    nblk = lq // Q_BLOCK
    q_blocks = q.reshape(bsz, heads, nblk, Q_BLOCK, dqk).transpose(2, 0, 1, 3, 4)

    def attend(qb):
        s = jnp.einsum("bhqd,bhkd->bhqk", qb, k).astype(jnp.float32) * MLA_SCALE
        probs = jax.nn.softmax(s, axis=-1).astype(v.dtype)
        return jnp.einsum("bhqk,bhkd->bhqd", probs, v)

    o = lax.map(attend, q_blocks)
    return o.transpose(1, 2, 0, 3, 4).reshape(bsz, heads, lq, dv)


def merge_heads(o):
    bsz, heads, length, dv = o.shape
    return o.transpose(0, 2, 1, 3).reshape(bsz, length, heads * dv)


def mla_mixer(h_lat, h_ctx, p, with_ctx_out):
    qc, kc, vc = mla_project(h_ctx, p, with_pos=False)
    ql, kl, vl = mla_project(h_lat, p, with_pos=True)
    o_lat = blocked_softmax_attention(ql, jnp.concatenate([kc, kl], axis=2), jnp.concatenate([vc, vl], axis=2))
    y_lat = merge_heads(o_lat) @ p["w_out"]
    y_ctx = merge_heads(blocked_softmax_attention(qc, kc, vc)) @ p["w_out"] if with_ctx_out else None
    return y_lat, y_ctx


def chunk_gated_delta(q, k, v, g, beta, s0):
    bsz, heads, length, dk = q.shape
    dv = v.shape[-1]
    n = length // GDN_CHUNK
    out_dtype = v.dtype

    def chunked(t):
        return t.astype(jnp.float32).reshape(bsz, heads, n, GDN_CHUNK, *t.shape[3:])

    q, k, v, g, beta = chunked(q), chunked(k), chunked(v), chunked(g), chunked(beta)
    cum = jnp.cumsum(g, axis=-1)
    pos = jnp.arange(GDN_CHUNK)
    incl = pos[:, None] >= pos[None, :]
    strict = pos[:, None] > pos[None, :]
    gamma = jnp.exp(jnp.where(incl, cum[..., :, None] - cum[..., None, :], -jnp.inf))
    k_beta = k * beta[..., None]
    a_mat = jnp.where(strict, jnp.einsum("bhncd,bhnsd->bhncs", k_beta, k) * gamma, 0.0)
    rhs = jnp.concatenate([v * beta[..., None], k_beta * jnp.exp(cum)[..., None]], axis=-1)
    sol = lax.linalg.triangular_solve(a_mat + jnp.eye(GDN_CHUNK, dtype=jnp.float32), rhs, left_side=True, lower=True)
    u, w = sol[..., :dv], sol[..., dv:]
    qk = jnp.where(incl, jnp.einsum("bhncd,bhnsd->bhncs", q, k) * gamma, 0.0)
    q_dec = q * jnp.exp(cum)[..., None]
    g_last = cum[..., -1]
    k_dec = k * jnp.exp(g_last[..., None] - cum)[..., None]

    def step(state, xs):
        u_i, w_i, qd_i, qk_i, kd_i, gl_i = xs
        v_new = u_i - jnp.einsum("bhck,bhkv->bhcv", w_i, state)
        o_i = jnp.einsum("bhck,bhkv->bhcv", qd_i, state) + jnp.einsum("bhcs,bhsv->bhcv", qk_i, v_new)
        state = state * jnp.exp(gl_i)[..., None, None] + jnp.einsum("bhck,bhcv->bhkv", kd_i, v_new)
        return state, o_i

    xs = tuple(jnp.moveaxis(t, 2, 0) for t in (u, w, q_dec, qk, k_dec, g_last))
    state, o = lax.scan(step, s0.astype(jnp.float32), xs)
    o = jnp.moveaxis(o, 0, 2).reshape(bsz, heads, length, dv)
    return o.astype(out_dtype), state


def gdn_project(h, p):
    bsz, length, _ = h.shape
    hk, hv, nh = GDN_HEADS * GDN_DK, GDN_HEADS * GDN_DV, GDN_HEADS
    qkv, gate, a, b = jnp.split(h @ p["w_in"], [2 * hk + hv, 2 * hk + 2 * hv, 2 * hk + 2 * hv + 2 * nh], axis=-1)
    qkv = jax.nn.silu(centred_depthwise_conv(qkv, p["conv_w"]))
    q, k, v = jnp.split(qkv, [hk, 2 * hk], axis=-1)
    q = l2_normalize(q.reshape(bsz, length, nh, GDN_DK)).transpose(0, 2, 1, 3) * GDN_DK ** -0.5
    k = l2_normalize(k.reshape(bsz, length, nh, GDN_DK)).transpose(0, 2, 1, 3)
    v = v.reshape(bsz, length, nh, GDN_DV).transpose(0, 2, 1, 3)
    a = a.astype(jnp.float32).reshape(bsz, length, 2, nh).transpose(2, 0, 3, 1)
    beta = jax.nn.sigmoid(b.astype(jnp.float32).reshape(bsz, length, 2, nh).transpose(2, 0, 3, 1))
    g = -jnp.exp(p["a_log"].astype(jnp.float32))[:, None, :, None] * jax.nn.softplus(
        a + p["dt_bias"].astype(jnp.float32)[:, None, :, None])
    return q, k, v, g, beta, gate


def gdn_bidirectional(q, k, v, g, beta, s_fwd, s_bwd):
    o_f, s_f = chunk_gated_delta(q, k, v, g[0], beta[0], s_fwd)
    flip = lambda t: jnp.flip(t, axis=2)
    o_b, s_b = chunk_gated_delta(flip(q), flip(k), flip(v), flip(g[1]), flip(beta[1]), s_bwd)
    return o_f + flip(o_b), s_f, s_b


def gdn_output(o, gate, p):
    bsz, heads, length, _ = o.shape
    o = rms_norm(o.transpose(0, 2, 1, 3), p["o_norm"]) * jax.nn.silu(gate.reshape(bsz, length, heads, GDN_DV))
    return o.reshape(bsz, length, heads * GDN_DV) @ p["w_out"]


def gdn_mixer(h_lat, h_ctx, p, with_ctx_out):
    qc, kc, vc, gc, bc, zc = gdn_project(h_ctx, p)
    s0 = jnp.zeros((h_ctx.shape[0], GDN_HEADS, GDN_DK, GDN_DV), jnp.float32)
    o_ctx, s_fwd, s_bwd = gdn_bidirectional(qc, kc, vc, gc, bc, s0, s0)
    ql, kl, vl, gl, bl, zl = gdn_project(h_lat, p)
    o_lat, _, _ = gdn_bidirectional(ql, kl, vl, gl, bl, s_fwd, s_bwd)
    y_lat = gdn_output(o_lat, zl, p)
    y_ctx = gdn_output(o_ctx, zc, p) if with_ctx_out else None
    return y_lat, y_ctx


def setup_inputs(seed: int = 0) -> dict:
    key = jax.random.key(seed)
    ks = iter(jax.random.split(key, 48))
    f32 = jnp.float32
    nrm = lambda shape, scale: scale * jax.random.normal(next(ks), shape, f32)
    d = D_MODEL
    hf = HYENA_FILTER_HIDDEN
    gdn_in = 2 * GDN_HEADS * GDN_DK + 2 * GDN_HEADS * GDN_DV + 4 * GDN_HEADS
    gdn_conv = 2 * GDN_HEADS * GDN_DK + GDN_HEADS * GDN_DV
    a_init = jax.random.uniform(next(ks), (N_GDN, 2, GDN_HEADS), f32, 1.0, 16.0)
    dt = jnp.exp(jax.random.uniform(next(ks), (N_GDN, 2, GDN_HEADS), f32, math.log(1e-3), math.log(1e-1)))
    return {
        "x": nrm((BATCH, SEQ, d), 1.0),
        "c": nrm((BATCH, d), 1.0),
        "ctx": nrm((BATCH, CTX_LEN, d), 1.0),
        "c_ctx": nrm((d,), 1.0),
        "mod_w": nrm((DEPTH, d, 6 * d), d ** -0.5),
        "mod_b": nrm((DEPTH, 6 * d), 0.01),
        "ln_g": 1.0 + nrm((DEPTH, 2, d), 0.02),
        "ln_b": nrm((DEPTH, 2, d), 0.02),
        "ffn_w_in": nrm((DEPTH, d, 2 * D_FF), d ** -0.5),
        "ffn_w_out": nrm((DEPTH, D_FF, d), DEEPNORM_BETA * D_FF ** -0.5),
        "hy_w_in": nrm((N_HYENA, d, 3 * d), d ** -0.5),
        "hy_b_in": nrm((N_HYENA, 3 * d), 0.02),
        "hy_conv_w": nrm((N_HYENA, HYENA_SHORT, 3 * d), HYENA_SHORT ** -0.5),
        "hy_conv_b": nrm((N_HYENA, 3 * d), 0.02),
        "hy_fw1": nrm((N_HYENA, HYENA_EMB, hf), HYENA_EMB ** -0.5),
        "hy_fb1": nrm((N_HYENA, hf), 0.02),
        "hy_fw2": nrm((N_HYENA, hf, hf), hf ** -0.5),
        "hy_fb2": nrm((N_HYENA, hf), 0.02),
        "hy_fw3": nrm((N_HYENA, hf, hf), hf ** -0.5),
        "hy_fb3": nrm((N_HYENA, hf), 0.02),
        "hy_fw4": nrm((N_HYENA, hf, 2 * d), 0.05 * hf ** -0.5),
        "hy_freq": 1.0 + nrm((N_HYENA, hf), 0.02),
        "hy_skip": nrm((N_HYENA, d), 0.1),
        "hy_w_out": nrm((N_HYENA, d, d), DEEPNORM_BETA * d ** -0.5),
        "hy_b_out": nrm((N_HYENA, d), 0.02),
        "mla_w_in": nrm((N_MLA, d, MLA_Q_LORA + MLA_KV_LORA + MLA_ROPE), d ** -0.5),
        "mla_q_norm": 1.0 + nrm((N_MLA, MLA_Q_LORA), 0.02),
        "mla_kv_norm": 1.0 + nrm((N_MLA, MLA_KV_LORA), 0.02),
        "mla_w_uq": nrm((N_MLA, MLA_Q_LORA, MLA_HEADS * (MLA_NOPE + MLA_ROPE)), MLA_Q_LORA ** -0.5),
        "mla_w_ukv": nrm((N_MLA, MLA_KV_LORA, MLA_HEADS * (MLA_NOPE + MLA_V)), MLA_KV_LORA ** -0.5),
        "mla_w_out": nrm((N_MLA, MLA_HEADS * MLA_V, d), DEEPNORM_BETA * (MLA_HEADS * MLA_V) ** -0.5),
        "gdn_w_in": nrm((N_GDN, d, gdn_in), d ** -0.5),
        "gdn_conv_w": nrm((N_GDN, GDN_SHORT, gdn_conv), GDN_SHORT ** -0.5),
        "gdn_a_log": jnp.log(a_init),
        "gdn_dt_bias": dt + jnp.log(-jnp.expm1(-dt)),
        "gdn_o_norm": 1.0 + nrm((N_GDN, GDN_DV), 0.02),
        "gdn_w_out": nrm((N_GDN, GDN_HEADS * GDN_DV, d), DEEPNORM_BETA * (GDN_HEADS * GDN_DV) ** -0.5),
    }


def reference(x, c, ctx, c_ctx, mod_w, mod_b, ln_g, ln_b, ffn_w_in, ffn_w_out,
              hy_w_in, hy_b_in, hy_conv_w, hy_conv_b, hy_fw1, hy_fb1, hy_fw2, hy_fb2, hy_fw3, hy_fb3,
              hy_fw4, hy_freq, hy_skip, hy_w_out, hy_b_out,
              mla_w_in, mla_q_norm, mla_kv_norm, mla_w_uq, mla_w_ukv, mla_w_out,
              gdn_w_in, gdn_conv_w, gdn_a_log, gdn_dt_bias, gdn_o_norm, gdn_w_out):
    for i in range(DEPTH):
        kind, j = i % N_MIXERS, i // N_MIXERS
        ctx_out = any(l % N_MIXERS != MIXER_HYENA for l in range(i + 1, DEPTH))
        ml = [m[:, None, :] for m in adaln(c, mod_w[i], mod_b[i])]
        h_lat = x * (1.0 + ml[1]) + ml[0]
        if kind != MIXER_HYENA or ctx_out:
            mc = adaln(c_ctx, mod_w[i], mod_b[i])
            h_ctx = ctx * (1.0 + mc[1]) + mc[0]
        if kind == MIXER_HYENA:
            p = {"w_in": hy_w_in[j], "b_in": hy_b_in[j], "conv_w": hy_conv_w[j], "conv_b": hy_conv_b[j],
                 "fw1": hy_fw1[j], "fb1": hy_fb1[j], "fw2": hy_fw2[j], "fb2": hy_fb2[j],
                 "fw3": hy_fw3[j], "fb3": hy_fb3[j], "fw4": hy_fw4[j], "freq": hy_freq[j],
                 "skip": hy_skip[j], "w_out": hy_w_out[j], "b_out": hy_b_out[j]}
            y_lat = hyena_mixer(h_lat, p)
            y_ctx = hyena_mixer(h_ctx, p) if ctx_out else None
        elif kind == MIXER_MLA:
            p = {"w_in": mla_w_in[j], "q_norm": mla_q_norm[j], "kv_norm": mla_kv_norm[j],
                 "w_uq": mla_w_uq[j], "w_ukv": mla_w_ukv[j], "w_out": mla_w_out[j]}
            y_lat, y_ctx = mla_mixer(h_lat, h_ctx, p, ctx_out)
        else:
            p = {"w_in": gdn_w_in[j], "conv_w": gdn_conv_w[j], "a_log": gdn_a_log[j],
                 "dt_bias": gdn_dt_bias[j], "o_norm": gdn_o_norm[j], "w_out": gdn_w_out[j]}
            y_lat, y_ctx = gdn_mixer(h_lat, h_ctx, p, ctx_out)
        x = mixer_residual_and_ffn(x, y_lat, ml, ln_g[i], ln_b[i], ffn_w_in[i], ffn_w_out[i])
        if ctx_out:
            ctx = mixer_residual_and_ffn(ctx, y_ctx, mc, ln_g[i], ln_b[i], ffn_w_in[i], ffn_w_out[i])
    return x
```

```python
import contextlib
import math
import numpy as np
import ml_dtypes
import concourse.bass as bass
import concourse.mybir as mybir
from concourse.bass_utils import run_bass_kernel_spmd

F32 = mybir.dt.float32
BF16 = mybir.dt.bfloat16
I32 = mybir.dt.int32
AF = mybir.ActivationFunctionType
ALU = mybir.AluOpType

D = 1024
SEQ = 2048
CTX = 256
DEPTH = 4
DFF = 2816
ALPHA = (2 * DEPTH) ** 0.25
LN_EPS = 1e-5
RMS_EPS = 1e-6
NCORES = 8

ENGS = ("pe", "act", "dve", "pool", "sp")


class Tok:
    __slots__ = ("name", "lw", "rd")

    def __init__(self, name=""):
        self.name = name
        self.lw = None
        self.rd = []


class Buf:
    def __init__(self, t, name):
        self.t = t
        self.name = name
        self.tok = Tok(name)
        self.sub = {}

    def __getitem__(self, idx):
        return self.t[idx]

    def k(self, i):
        if i not in self.sub:
            self.sub[i] = Tok(f"{self.name}.{i}")
        return self.sub[i]


def _tok(x):
    return x.tok if isinstance(x, Buf) else x


class Sched:
    def __init__(self, nc, n_dma_sems=24):
        self.nc = nc
        self.ops = {e: [] for e in ENGS}
        self.cnt = {e: 0 for e in ENGS}
        self.waited = {e: {} for e in ENGS}
        self.n_dma_sems = n_dma_sems
        self.dma_cnt = [0] * n_dma_sems
        self.dma_rr = 0

    def _add_waits(self, stream, deps, is_pe=False):
        w = self.waited[stream]
        best = {}
        for d in deps:
            if d is None:
                continue
            key, val = d
            if is_pe and key == "pe":
                continue
            if w.get(key, 0) >= val:
                continue
            w[key] = val
            best[key] = max(best.get(key, 0), val)
        return list(best.items())

    def op(self, stream, fn, r=(), w=()):
        r = [_tok(x) for x in r]
        w = [_tok(x) for x in w]
        deps = []
        for t in r:
            deps.append(t.lw)
        for t in w:
            deps.append(t.lw)
            deps.extend(t.rd)
        waits = self._add_waits(stream, deps, is_pe=(stream == "pe"))
        self.cnt[stream] += 1
        me = (stream, self.cnt[stream])
        self.ops[stream].append((waits, fn, (stream, 1)))
        for t in r:
            t.rd = [x for x in t.rd if x[0] != stream] + [me]
        for t in w:
            t.lw = me
            t.rd = []

    def dma(self, stream, fn, r=(), w=()):
        r = [_tok(x) for x in r]
        w = [_tok(x) for x in w]
        si = self.dma_rr
        self.dma_rr = (self.dma_rr + 1) % self.n_dma_sems
        key = ("dma", si)
        deps = []
        for t in r:
            deps.append(t.lw)
        for t in w:
            deps.append(t.lw)
            deps.extend(t.rd)
        if self.dma_cnt[si] > 0:
            deps.append((key, 16 * self.dma_cnt[si]))
        waits = self._add_waits(stream, deps)
        self.dma_cnt[si] += 1
        me = (key, 16 * self.dma_cnt[si])
        self.ops[stream].append((waits, fn, (key, 16)))
        for t in r:
            t.rd = t.rd + [me]
        for t in w:
            t.lw = me
            t.rd = []

    def barrier(self):
        for s in ENGS:
            deps = []
            for e in ("pe", "act", "dve", "pool"):
                if self.cnt[e] > 0:
                    deps.append((e, self.cnt[e]))
            for i in range(self.n_dma_sems):
                if self.dma_cnt[i] > 0:
                    deps.append((("dma", i), 16 * self.dma_cnt[i]))
            waits = self._add_waits(s, deps)
            if waits:
                self.ops[s].append((waits, None, None))

    def emit(self):
        nc = self.nc
        with contextlib.ExitStack() as es:
            sems = {}
            for e in ("pe", "act", "dve", "pool"):
                sems[e] = es.enter_context(nc.semaphore(f"s_{e}"))
            for i in range(self.n_dma_sems):
                sems[("dma", i)] = es.enter_context(nc.semaphore(f"s_dma{i}"))
            block = es.enter_context(nc.Block())

            def run(stream):
                def body(eng):
                    for waits, fn, inc in self.ops[stream]:
                        for k, v in waits:
                            eng.wait_ge(sems[k], v)
                        if fn is not None:
                            fn(eng).then_inc(sems[inc[0]], inc[1])
                return body

            block.tensor(run("pe"))
            block.scalar(run("act"))
            block.vector(run("dve"))
            block.gpsimd(run("pool"))
            block.sync(run("sp"))


_DT_SIZE = {F32: 4, BF16: 2, I32: 4}


class Arena:
    def __init__(self, t, nwords):
        self.t = t
        self.n = nwords
        self.top = 0
        self.peak = 0

    def alloc(self, name, free_shape, dt=F32):
        n = int(np.prod(free_shape))
        words = (n * _DT_SIZE[dt] + 3) // 4
        words = (words + 7) // 8 * 8
        off = self.top
        self.top += words
        self.peak = max(self.peak, self.top)
        assert self.top <= self.n, f"arena overflow allocating {name}: {self.top} > {self.n}"
        ap = self.t[:, off:off + words]
        if dt != F32:
            ap = ap.bitcast(dt)
        ap = ap[:, 0:n]
        if len(free_shape) == 2:
            ap = ap.rearrange("p (a b) -> p a b", a=free_shape[0])
        elif len(free_shape) == 3:
            ap = ap.rearrange("p (a b c) -> p a b c", a=free_shape[0], b=free_shape[1])
        return Buf(ap, name)

    def mark(self):
        return self.top

    def release(self, m):
        self.top = m


def _bf16(a):
    return np.ascontiguousarray(a.astype(np.float32)).astype(ml_dtypes.bfloat16)


def _dft_consts(L):
    nt = L // 128
    TB = min(512, L)
    ntb = L // TB
    f = np.arange(L, dtype=np.float64)
    t = np.arange(L, dtype=np.float64)
    om = np.pi * (2 * f + 1) / (2 * L)
    ang = np.outer(t, om)
    C = np.cos(ang)
    Sn = np.sin(ang)
    fw = np.zeros((nt, 2, 128, nt, 128), np.float32)
    for part, M in enumerate((C, Sn)):
        M4 = M.reshape(nt, 128, nt, 128)
        fw[:, part] = M4.transpose(2, 1, 0, 3)
    inv = np.zeros((ntb, 2, 128, nt, TB), np.float32)
    for part, M in enumerate((C / L, Sn / L)):
        M4 = M.T.reshape(nt, 128, ntb, TB)
        inv[:, part] = M4.transpose(2, 1, 0, 3)
    return _bf16(fw), _bf16(inv)


def _hyena_pos(L):
    t01 = np.linspace(0.0, 1.0, L)[:, None]
    bands = 16
    w = (2.0 * math.pi / L) * np.arange(L, dtype=np.float64)
    f = np.linspace(1e-4, bands - 1, bands)
    ang = w[:, None] * f[None, :]
    z = np.concatenate([t01, np.cos(ang), -np.sin(ang)], axis=-1)
    max_decay = math.log(1e-2) / 0.3
    min_decay = math.log(1e-2) / 1.5
    deltas = np.abs(np.linspace(min_decay, max_decay, D))
    tt = np.linspace(0.0, 1.0, L)
    win = np.exp(-tt[:, None] * deltas[None, :])
    return np.ascontiguousarray(z.T.astype(np.float32)), np.ascontiguousarray(win.astype(np.float32))


_CONST_CACHE = {}


def host_consts():
    if _CONST_CACHE:
        return _CONST_CACHE
    c = {}
    for L in (SEQ, CTX):
        fw, inv = _dft_consts(L)
        c[f"fw{L}"] = fw
        c[f"inv{L}"] = inv
        z, win = _hyena_pos(L)
        c[f"z0T{L}"] = z
        c[f"win{L}"] = win
    TT = CTX + SEQ
    l = np.arange(SEQ)
    row = (l // 64).astype(np.float64); colp = (l % 64).astype(np.float64)
    inv_freq = 10000.0 ** (-np.arange(16, dtype=np.float64) / 16)
    rc = np.ones((64, TT), np.float64); rs = np.zeros((64, TT), np.float64)
    ar = inv_freq[:, None] * row[None, :]; ac = inv_freq[:, None] * colp[None, :]
    rc[0:16, CTX:] = np.cos(ar); rc[16:32, CTX:] = np.cos(ar); rc[32:48, CTX:] = np.cos(ac); rc[48:64, CTX:] = np.cos(ac)
    rs[0:16, CTX:] = np.sin(ar); rs[16:32, CTX:] = np.sin(ar); rs[32:48, CTX:] = np.sin(ac); rs[48:64, CTX:] = np.sin(ac)
    c["ropeC"] = rc.astype(np.float32); c["ropeS"] = rs.astype(np.float32)
    Pm = np.zeros((64, 64), np.float32)
    for base in (0, 32):
        for i in range(16):
            Pm[base + i, base + 16 + i] = -1.0
            Pm[base + 16 + i, base + i] = 1.0
    c["ropeP"] = np.ascontiguousarray(Pm.T)
    c["onesb"] = _bf16(np.ones((128, 128)))
    jj = np.arange(128)[:, None]; cc_ = np.arange(128)[None, :]
    same = (jj // 64) == (cc_ // 64)
    mk = np.zeros((128, 16, 128), np.float32)
    mk[:, 0] = same & (jj <= cc_); mk[:, 1] = same & (jj >= cc_)
    mk[:, 2] = same & (jj > cc_); mk[:, 3] = same & (jj < cc_)
    mk[:, 4] = same & (jj >= cc_); mk[:, 5] = same & (jj <= cc_)
    mk[:, 6] = same & (jj > cc_); mk[:, 7] = same & (jj < cc_)
    mk[:, 8] = -mk[:, 0]; mk[:, 9] = -mk[:, 1]
    for li_, b_ in enumerate((1, 2, 4, 8, 16, 32)):
        mk[:, 10 + li_] = ((jj // (2 * b_)) == (cc_ // (2 * b_))) & ((jj // b_) != (cc_ // b_))
    c["gmask"] = mk
    bk = np.zeros((128, 2), np.float32); bk[:64, 0] = 1; bk[64:, 1] = 1
    c["gblk"] = bk
    c["ident"] = np.eye(128, dtype=np.float32)
    c["identb"] = _bf16(np.eye(128))
    c["ones"] = np.ones((128, 128), np.float32)
    _CONST_CACHE.update(c)
    return c


def col_layout(v, nchunks):
    return np.ascontiguousarray(np.asarray(v, np.float32).reshape(nchunks, 128).T)


class Prog:
    def __init__(self, n_layers=DEPTH, debug_out=None, layers=None, gdn_stop=9, gdn_dirs=(0, 1), dbg_gdn=False):
        self.gdn_dirs = gdn_dirs
        self.dma_alt = True
        self.dbg_gdn = dbg_gdn
        self.n_layers = n_layers
        self.layers = layers if layers is not None else list(range(n_layers))
        self.gdn_stop = gdn_stop
        self.debug_out = debug_out
        self.nc = bass.Bass("TRN2", target_bir_lowering=False)
        self.din = {}
        self.es = contextlib.ExitStack()

    def inp(self, name, shape, dt=F32):
        self.din[name] = self.nc.dram_tensor(name, list(shape), dt, kind="ExternalInput").ap()
        return self.din[name]

    def scr(self, name, shape, dt=F32):
        return self.nc.dram_tensor(name, list(shape), dt, kind="Internal").ap()

    def mark(self, name):
        if not hasattr(self, 'marks'):
            self.marks = []
        self.marks.append((name, self.S.cnt['dve']))

    def mm(self, out, lhsT, rhs, start, stop, r, w):
        self.S.op("pe", lambda e: e.matmul(out, lhsT, rhs, start=start, stop=stop), r=r, w=w)

    def tr(self, out, in_, ident, r, w):
        self.S.op("pe", lambda e: e.transpose(out, in_, ident), r=r, w=w)

    def act(self, out, in_, func, r, w, bias=None, scale=None, eng="act"):
        kw = {}
        if bias is not None:
            kw["bias"] = bias
        if scale is not None:
            kw["scale"] = scale
        self.S.op("act", lambda e: e.activation(out=out, in_=in_, func=func, **kw), r=r, w=w)

    def ts(self, out, in0, s1, s2, op0, op1, r, w, eng="dve"):
        if op1 is None:
            self.S.op(eng, lambda e: e.tensor_scalar(out, in0, s1, None, op0), r=r, w=w)
        else:
            self.S.op(eng, lambda e: e.tensor_scalar(out, in0, s1, s2, op0, op1), r=r, w=w)

    def tt(self, out, in0, in1, op, r, w, eng="dve"):
        self.S.op(eng, lambda e: e.tensor_tensor(out, in0, in1, op), r=r, w=w)

    def stt(self, out, in0, scalar, in1, op0, op1, r, w, eng="dve"):
        self.S.op(eng, lambda e: e.scalar_tensor_tensor(out, in0, scalar, in1, op0, op1), r=r, w=w)

    def cp(self, out, in_, r, w, eng="dve"):
        self.S.op(eng, lambda e: e.tensor_copy(out, in_), r=r, w=w)

    def ld(self, out, in_, r=(), w=(), q="sp"):
        if q == "sp" and self.dma_alt:
            self._ld_rr = getattr(self, "_ld_rr", 0) + 1
            if self._ld_rr % 2:
                q = "pool"
        self.S.dma(q, lambda e: e.dma_start(out=out, in_=in_), r=r, w=w)

    def rsqrt(self, out, in_, scale, eps, tok, in_toks=()):
        self.ts(out, in_, scale, eps, ALU.mult, ALU.add, r=[tok] + list(in_toks), w=[tok])
        self.act(out, out, AF.Ln, r=[tok], w=[tok])
        self.act(out, out, AF.Exp, r=[tok], w=[tok], scale=-0.5)

    def psbank(self):
        self.ps_rr = (self.ps_rr + 1) % len(self.ps_pool)
        return self.ps_pool[self.ps_rr]

    def alloc_wst(self, KC, width, dt=BF16, nstage=2):
        d = {"bufs": [self.A.alloc(f"wst{i}", (KC, width), dt) for i in range(2)], "rr": 0, "dt": dt}
        if dt == BF16:
            d["stage"] = [self.A.alloc(f"wstf{i}", (KC, width), F32) for i in range(nstage)]
        return d

    def load_w(self, wst, src, KC, width):
        i = wst["rr"] % 2
        wst["rr"] += 1
        ws = wst["bufs"][i]
        if wst["dt"] == BF16:
            st = wst["stage"][i % len(wst["stage"])]
            self.ld(st[:, 0:KC, 0:width], src, w=[st], q="sp")
            if wst["rr"] % 2:
                self.act(ws[:, 0:KC, 0:width], st[:, 0:KC, 0:width], AF.Copy, r=[st], w=[ws])
            else:
                self.cp(ws[:, 0:KC, 0:width], st[:, 0:KC, 0:width], r=[st], w=[ws])
        else:
            self.ld(ws[:, 0:KC, 0:width], src, w=[ws], q="sp")
        return ws

    def linear(self, W, K, groups, rhs_fn, rhs_toks, T, evac, tb=512, wst=None, hook=None):
        A, S = self.A, self.S
        KC = K // 128
        own = wst is None
        if own:
            m0 = A.mark()
            wst = self.alloc_wst(KC, max(g[1] for g in groups))
        for gi, (c0, width, subs) in enumerate(groups):
            src = W[:, c0:c0 + width].rearrange("(kc p) n -> p kc n", p=128)
            ws = self.load_w(wst, src, KC, width)
            for (off, m, tag) in subs:
                for t0 in range(0, T, tb):
                    t1 = min(T, t0 + tb)
                    b = self.psbank()
                    ps = self.PS[0:m, b, 0:t1 - t0]
                    for kc in range(KC):
                        self.mm(ps, ws[:, kc, off:off + m], rhs_fn(kc, t0, t1), kc == 0, kc == KC - 1,
                                r=[ws] + list(rhs_toks), w=[self.PS.k(b)])
                    evac(tag, t0, t1, ps, self.PS.k(b))
            if hook is not None:
                hook()
        if own:
            S.barrier()
            A.release(m0)

    def modulation_gen(self, li, mods, wst, part):
        HW = 1536
        k = 0
        for kc in range(8):
            for q in range(4):
                ws = wst[k % 2]
                k += 1
                self.ld(ws[:, :], self.din["mod_w"][li, kc * 128:(kc + 1) * 128, q * HW:(q + 1) * HW], w=[ws])
                b = self.psbank()
                for n in range(12):
                    self.mm(self.PS[:, b, 2 * n:2 * n + 2], ws[:, n * 128:(n + 1) * 128], self.cs[:, kc, :],
                            True, True, r=[ws, self.cs], w=[self.PS.k(b)])
                src = self.PS[:, b, 0:24].rearrange("p (n c) -> p n c", c=2)
                dst = mods[:, q * 12:(q + 1) * 12, :]
                if kc == 0:
                    self.cp(dst, src, r=[self.PS.k(b)], w=[mods])
                else:
                    self.tt(dst, dst, src, ALU.add, r=[self.PS.k(b), mods], w=[mods])
                yield
        for c in range(2):
            self.tt(mods[:, :, c], mods[:, :, c], self.modb[:, li, :], ALU.add, r=[mods, self.modb], w=[mods])
        for j in (1, 4):
            self.ts(mods[:, j * 8:(j + 1) * 8, :], mods[:, j * 8:(j + 1) * 8, :], 1.0, None, ALU.add, None,
                    r=[mods], w=[mods])
        yield

    def modulation(self, li):
        A, S = self.A, self.S
        if self.mods_ready == li:
            self.mods, self.mods_alt = self.mods_alt, self.mods
            self.mods_ready = None
            return
        m0 = A.mark()
        wst = [A.alloc(f"mw{i}", (1536,), F32) for i in range(2)]
        for _ in self.modulation_gen(li, self.mods, wst, None):
            pass
        S.barrier()
        A.release(m0)

    def modulate(self, hT, hoff, xT, T, jshift, col):
        mods = self.mods
        for dc in range(8):
            if dc % 2 == 0:
                self.act(hT[:, dc, hoff:hoff + T], xT[:, dc, 0:T], AF.Identity, r=[xT, mods], w=[hT],
                         bias=mods[:, jshift * 8 + dc, col:col + 1], scale=mods[:, (jshift + 1) * 8 + dc, col:col + 1])
            else:
                self.ts(hT[:, dc, hoff:hoff + T], xT[:, dc, 0:T], mods[:, (jshift + 1) * 8 + dc, col:col + 1],
                        mods[:, jshift * 8 + dc, col:col + 1], ALU.mult, ALU.add, r=[xT, mods], w=[hT])

    def prescale(self, xT, T, bg=None):
        for dc in range(8):
            if dc % 2:
                if bg is None:
                    self.ts(xT[:, dc, 0:T], xT[:, dc, 0:T], ALPHA, None, ALU.mult, None, r=[xT], w=[xT])
                else:
                    self.ts(xT[:, dc, 0:T], xT[:, dc, 0:T], ALPHA, bg[:, dc:dc + 1], ALU.mult, ALU.add, r=[xT, bg], w=[xT])
            else:
                if bg is None:
                    self.act(xT[:, dc, 0:T], xT[:, dc, 0:T], AF.Copy, r=[xT], w=[xT], scale=ALPHA)
                else:
                    self.act(xT[:, dc, 0:T], xT[:, dc, 0:T], AF.Identity, r=[xT, bg], w=[xT], scale=ALPHA, bias=bg[:, dc:dc + 1])

    def layernorm(self, xT, T, li, which):
        A, S = self.A, self.S
        self.mark(f'LN{T}')
        m0 = A.mark()
        TB = min(512, T)
        sq = [A.alloc(f"lnsq{i}", (TB,), BF16) for i in range(3)]
        xb = [A.alloc(f"lnxb{i}", (TB,), BF16) for i in range(3)]
        mean = A.alloc("lnmean", (T,), F32)
        rstd = A.alloc("lnrstd", (T,), F32)
        msq = A.alloc("lnmsq", (TB,), F32)
        tmp = [A.alloc(f"lntmp{i}", (TB,), F32) for i in range(3)]
        gcol = lambda dc: self.lng[:, (li * 2 + which) * 8 + dc:(li * 2 + which) * 8 + dc + 1]
        bcol = lambda dc: self.lnb[:, (li * 2 + which) * 8 + dc:(li * 2 + which) * 8 + dc + 1]
        nsq = 0
        for t0 in range(0, T, TB):
            bs = self.psbank()
            bq = self.psbank()
            for dc in range(8):
                x_ = xb[nsq % 3]
                s_ = sq[nsq % 3]
                nsq += 1
                self.cp(x_[:, :], xT[:, dc, t0:t0 + TB], r=[xT], w=[x_])
                self.act(s_[:, :], xT[:, dc, t0:t0 + TB], AF.Square, r=[xT], w=[s_])
                self.mm(self.PS[:, bs, 0:TB], self.onesb[:, :], x_[:, :], dc == 0, dc == 7,
                        r=[self.onesb, x_], w=[self.PS.k(bs)])
                self.mm(self.PS[:, bq, 0:TB], self.onesb[:, :], s_[:, :], dc == 0, dc == 7,
                        r=[self.onesb, s_], w=[self.PS.k(bq)])
            mt, rt = mean.k(t0), rstd.k(t0)
            self.ts(mean[:, t0:t0 + TB], self.PS[:, bs, 0:TB], 1.0 / D, None, ALU.mult, None, r=[self.PS.k(bs)], w=[mt])
            self.tt(msq[:, :], mean[:, t0:t0 + TB], mean[:, t0:t0 + TB], ALU.mult, r=[mt], w=[msq])
            self.stt(rstd[:, t0:t0 + TB], self.PS[:, bq, 0:TB], 1.0 / D, msq[:, :], ALU.mult, ALU.subtract,
                     r=[self.PS.k(bq), msq], w=[rt])
            self.rsqrt(rstd[:, t0:t0 + TB], rstd[:, t0:t0 + TB], 1.0, LN_EPS, rt)
        S.barrier()
        nt_ = 0
        for t0 in range(0, T, TB):
            mt, rt = mean.k(t0), rstd.k(t0)
            for dc in range(8):
                t_ = tmp[nt_ % 3]
                nt_ += 1
                xtok = Tok("lnx")
                self.tt(t_[:, :], xT[:, dc, t0:t0 + TB], mean[:, t0:t0 + TB], ALU.subtract, r=[xtok, mt], w=[t_])
                self.tt(t_[:, :], t_[:, :], rstd[:, t0:t0 + TB], ALU.mult, r=[t_, rt], w=[t_])
                self.act(xT[:, dc, t0:t0 + TB], t_[:, :], AF.Identity, r=[t_, self.lng, self.lnb], w=[xtok],
                         bias=bcol(dc), scale=gcol(dc))
        S.barrier()
        A.release(m0)

    def ffn(self, li, with_ctx):
        A, S = self.A, self.S
        mods = self.mods
        xT, cT = self.xT, self.cT
        T = SEQ + (CTX if with_ctx else 0)
        m0 = A.mark()
        hT = A.alloc("ffn_h", (8, T), BF16)
        self.modulate(hT, 0, xT, SEQ, 3, 0)
        self.prescale(xT, SEQ)
        if with_ctx:
            self.modulate(hT, SEQ, cT, CTX, 3, 1)
            self.prescale(cT, CTX)
        Win = self.din["ffn_w_in"][li]
        Wout = self.din["ffn_w_out"][li]
        aT = A.alloc("ffn_a", (6, T), BF16)
        gs = A.alloc("ffn_gs", (2, T), BF16)
        wst = self.alloc_wst(8, 256, nstage=1)
        woutb = A.alloc("ffn_wob", (6, D), BF16)
        wostg = [A.alloc(f"ffn_wos{i}", (D,), F32) for i in range(2)]
        nwo = 0
        hook = None
        nxt = self.layers[self.layers.index(li) + 1] if self.layers.index(li) + 1 < len(self.layers) else None
        if nxt is not None:
            mwst = [A.alloc(f"mw{i}", (1536,), F32) for i in range(2)]
            mgen = self.modulation_gen(nxt, self.mods_alt, mwst, None)

            def hook():
                for _ in range(2):
                    try:
                        next(mgen)
                    except StopIteration:
                        pass
        for (j0, nj) in ((0, 6), (6, 6), (12, 5), (17, 5)):
            groups = []
            for jj0 in range(0, nj, 2):
                n2 = min(2, nj - jj0)
                groups.append(((j0 + jj0) * 128, n2 * 128, [(q * 128, 128, ("g", jj0 + q, q)) for q in range(n2)]))
                groups.append((DFF + (j0 + jj0) * 128, n2 * 128, [(q * 128, 128, ("u", jj0 + q, q)) for q in range(n2)]))

            def evac1(tag, t0, t1, ps, ptok):
                kind, jl, q = tag
                if kind == "g":
                    self.act(gs[:, q, t0:t1], ps, AF.Silu, r=[ptok], w=[gs.k(q)])
                else:
                    self.tt(aT[:, jl, t0:t1], ps, gs[:, q, t0:t1], ALU.mult, r=[ptok, gs.k(q)], w=[aT])

            self.linear(Win, D, groups, lambda kc, t0, t1: hT[:, kc, t0:t1], [hT], T, evac1, wst=wst, hook=hook)
            for jl in range(nj):
                st = wostg[nwo % 2]
                nwo += 1
                self.ld(st[:, :], Wout[(j0 + jl) * 128:(j0 + jl + 1) * 128, :], w=[st], q="sp")
                if nwo % 2:
                    self.act(woutb[:, jl, :], st[:, :], AF.Copy, r=[st], w=[woutb])
                else:
                    self.cp(woutb[:, jl, :], st[:, :], r=[st], w=[woutb])
            for dc in range(8):
                for t0 in range(0, T, 512):
                    t1 = min(T, t0 + 512)
                    b = self.psbank()
                    ps = self.PS[:, b, 0:t1 - t0]
                    for jl in range(nj):
                        self.mm(ps, woutb[:, jl, dc * 128:(dc + 1) * 128], aT[:, jl, t0:t1], jl == 0, jl == nj - 1,
                                r=[woutb, aT], w=[self.PS.k(b)])
                    if t0 < SEQ:
                        self.stt(xT[:, dc, t0:t1], ps, mods[:, 5 * 8 + dc, 0:1], xT[:, dc, t0:t1],
                                 ALU.mult, ALU.add, r=[self.PS.k(b), mods, xT], w=[xT])
                    else:
                        self.stt(cT[:, dc, t0 - SEQ:t1 - SEQ], ps, mods[:, 5 * 8 + dc, 1:2], cT[:, dc, t0 - SEQ:t1 - SEQ],
                                 ALU.mult, ALU.add, r=[self.PS.k(b), mods, cT], w=[cT])
        if nxt is not None:
            for _ in mgen:
                pass
            self.mods_ready = nxt
        S.barrier()
        A.release(m0)
        self.layernorm(xT, SEQ, li, 1)
        if with_ctx:
            self.layernorm(cT, CTX, li, 1)

    def hyena_filter(self, j, L, big):
        A, S = self.A, self.S
        nt = L // 128
        TBK = min(512, L)
        m0 = A.mark()
        z = [A.alloc(f"hz{i}", (L,), F32) for i in range(2)]
        arg = A.alloc("harg", (TBK,), F32)
        ki = A.alloc("hki", (TBK,), I32)
        kf = A.alloc("hkf", (TBK,), F32)
        w1 = A.alloc("hw1", (64,), F32)
        w23 = A.alloc("hw23", (2, 64), F32)
        fw4 = A.alloc("hfw4", (2048,), F32)
        mb = big.mark()
        kp = big.alloc("hkp", (nt, 1024), BF16)
        km = big.alloc("hkm", (nt, 1024), BF16)
        self.ld(z[0][0:33, :], self.din[f"z0T{L}"][:, :], w=[z[0]])
        self.ld(w1[0:33, :], self.din["hy_fw1"][j], w=[w1])
        self.ld(w23[0:64, 0, :], self.din["hy_fw2"][j], w=[w23])
        self.ld(w23[0:64, 1, :], self.din["hy_fw3"][j], w=[w23])
        self.ld(fw4[0:64, :], self.din["hy_fw4"][j], w=[fw4])
        hyp = self.hyp
        cur = 0
        for layer in range(3):
            K = 33 if layer == 0 else 64
            wl = w1[0:33, :] if layer == 0 else w23[0:64, layer - 1, :]
            wtok = w1 if layer == 0 else w23
            zin, zout = z[cur], z[1 - cur]
            for t0 in range(0, L, TBK):
                b = self.psbank()
                ps = self.PS[0:64, b, 0:TBK]
                self.mm(ps, wl, zin[0:K, t0:t0 + TBK], True, True, r=[wtok, zin], w=[self.PS.k(b)])
                self.ts(arg[0:64, :], ps, hyp[0:64, j, layer:layer + 1], hyp[0:64, j, 3:4], ALU.add, ALU.mult,
                        r=[self.PS.k(b), hyp], w=[arg])
                self.ts(ki[0:64, :], arg[0:64, :], 1.0 / (2 * math.pi), None, ALU.mult, None, r=[arg], w=[ki])
                self.cp(kf[0:64, :], ki[0:64, :], r=[ki], w=[kf])
                self.stt(arg[0:64, :], kf[0:64, :], -2.0 * math.pi, arg[0:64, :], ALU.mult, ALU.add, r=[kf, arg], w=[arg])
                self.ts(arg[0:64, :], arg[0:64, :], 3.141592, -3.141592, ALU.min, ALU.max, r=[arg], w=[arg])
                self.act(zout[0:64, t0:t0 + TBK], arg[0:64, :], AF.Sin, r=[arg], w=[zout])
            cur = 1 - cur
        z3 = z[cur]
        z3b = A.alloc("hz3b", (L,), BF16)
        fw4b = A.alloc("hfw4b", (2048,), BF16)
        self.cp(z3b[0:64, :], z3[0:64, :], r=[z3], w=[z3b])
        self.act(fw4b[0:64, :], fw4[0:64, :], AF.Copy, r=[fw4], w=[fw4b])
        win = [A.alloc(f"hwin{i}", (1024,), F32) for i in range(2)]
        kfw = A.alloc("hkfw", (1024,), F32)
        kbw = A.alloc("hkbw", (1024,), F32)
        for tc in range(nt):
            wn = win[tc % 2]
            self.ld(wn[:, :], self.din[f"win{L}"][tc * 128:(tc + 1) * 128, :], w=[wn])
            for cb in range(4):
                b = self.psbank()
                ps = self.PS[:, b, 0:512]
                self.mm(ps, z3b[0:64, tc * 128:(tc + 1) * 128], fw4b[0:64, cb * 512:(cb + 1) * 512], True, True,
                        r=[z3b, fw4b], w=[self.PS.k(b)])
                dst = kfw if cb < 2 else kbw
                c0 = (cb % 2) * 512
                self.tt(dst[:, c0:c0 + 512], ps, wn[:, c0:c0 + 512], ALU.mult, r=[self.PS.k(b), wn], w=[dst])
            if tc == 0:
                self.tt(kfw[0:1, :], kfw[0:1, :], self.hskip[0:1, j, :], ALU.add, r=[kfw, self.hskip], w=[kfw])
            self.tt(kp[:, tc, :], kfw[:, :], kbw[:, :], ALU.add, r=[kfw, kbw], w=[kp])
            self.tt(km[:, tc, :], kfw[:, :], kbw[:, :], ALU.subtract, r=[kfw, kbw], w=[km])
        Hs = self.hscr[L]
        fwp = [A.alloc(f"hfwp{i}", (2, nt, 128), BF16) for i in range(2)]
        ho = [A.alloc(f"hho{i}", (512,), F32) for i in range(4)]
        oi = 0
        for fc in range(nt):
            fp = fwp[fc % 2]
            self.ld(fp[:, :, :, :], self.din[f"fw{L}"][fc].rearrange("a p t f -> p a t f"), w=[fp])
            for part in range(2):
                src = kp if part == 0 else km
                for cb in range(2):
                    b = self.psbank()
                    ps = self.PS[:, b, 0:512]
                    for tc in range(nt):
                        self.mm(ps, fp[:, part, tc, :], src[:, tc, cb * 512:(cb + 1) * 512], tc == 0, tc == nt - 1,
                                r=[fp, src], w=[self.PS.k(b)])
                    o = ho[oi % 4]
                    oi += 1
                    if oi % 2:
                        self.act(o[:, :], ps, AF.Copy, r=[self.PS.k(b)], w=[o])
                    else:
                        self.cp(o[:, :], ps, r=[self.PS.k(b)], w=[o])
                    self.ld(Hs[part, fc, :, cb * 512:(cb + 1) * 512], o[:, :], r=[o], w=[self.hscr_tok[L]])
        S.barrier()
        A.release(m0)
        big.release(mb)

    def hyena(self, j, li, xT, L, col):
        A, S = self.A, self.S
        mods = self.mods
        nt = L // 128
        TB = min(512, L)
        ntb = L // TB
        m0 = A.mark()
        hT = A.alloc("hy_h", (8, L), BF16)
        self.modulate(hT, 0, xT, L, 0, col)
        bg = A.alloc("hy_bg", (8,), F32)
        self.tt(bg[:, :], self.hbout[:, j, :], mods[:, 16:24, col], ALU.mult, r=[self.hbout, mods], w=[bg])
        self.prescale(xT, L, bg)
        spill = (L == SEQ)
        if spill:
            self.ld(self.xscr.rearrange("p (a t) -> p a t", a=8), xT[:, :, :], r=[xT], w=[self.xscr_tok])
            big = self.A2
            big.top = 0
        else:
            big = A
        S.barrier()
        self.mark(f'hy{L} filter')
        self.hyena_filter(j, L, big)
        self.mark(f'hy{L} P1')
        m2 = A.mark()
        zp = [A.alloc(f"hy_zp{i}", (L + 2,), F32) for i in range(3)]
        acc = [A.alloc(f"hy_acc{i}", (L,), F32) for i in range(2)]
        x0b = A.alloc("hy_x0b", (L,), BF16)
        ub = A.alloc("hy_ub", (L,), BF16)
        ust = A.alloc("hy_ust", (nt, 128), BF16)
        wst1 = self.alloc_wst(8, 128)
        for i in range(3):
            self.S.op("pool", lambda e, i=i: e.memset(zp[i][:, 0:1], 0.0), w=[zp[i]])
            self.S.op("pool", lambda e, i=i: e.memset(zp[i][:, L + 1:L + 2], 0.0), w=[zp[i]])
        Win = self.din["hy_w_in"][j]
        for cc in range(8):
            groups = [(part * 1024 + cc * 128, 128, [(0, 128, part)]) for part in range(3)]

            def evac(part, t0, t1, ps, ptok, cc=cc):
                self.act(zp[part][:, 1 + t0:1 + t1], ps, AF.Identity, r=[ptok, self.hbin], w=[zp[part]],
                         bias=self.hbin[:, j, part * 8 + cc:part * 8 + cc + 1])

            self.linear(Win, D, groups, lambda kc, t0, t1: hT[:, kc, t0:t1], [hT], L, evac, wst=wst1)
            res = []
            for part in range(3):
                q = part * 8 + cc
                a = acc[part % 2] if part < 2 else acc[0]
                eng = "dve"
                if part == 2:
                    pass
                cw = lambda kk, q=q: self.hcw[:, j, q, kk:kk + 1]
                self.ts(a[:, :], zp[part][:, 0:L], cw(0), self.hcb[:, j, q:q + 1], ALU.mult, ALU.add,
                        r=[zp[part], self.hcw, self.hcb], w=[a], eng=eng)
                self.stt(a[:, :], zp[part][:, 1:L + 1], cw(1), a[:, :], ALU.mult, ALU.add, r=[zp[part], self.hcw, a], w=[a], eng=eng)
                self.stt(a[:, :], zp[part][:, 2:L + 2], cw(2), a[:, :], ALU.mult, ALU.add, r=[zp[part], self.hcw, a], w=[a], eng=eng)
                if part == 0:
                    self.act(x0b[:, :], a[:, :], AF.Copy, r=[a], w=[x0b])
                    self.ld(self.x0scr[cc, :, 0:L], x0b[:, :], r=[x0b], w=[self.x0scr_tok])
                if part == 2:
                    self.tt(ub[:, :], acc[1][:, :], acc[0][:, :], ALU.mult, r=[acc[0], acc[1]], w=[ub])
            for tc in range(nt):
                b = self.psbank()
                psb = self.PS[:, b, 0:64].bitcast(BF16)
                self.tr(psb, ub[:, tc * 128:(tc + 1) * 128], self.identb[:, :], r=[ub, self.identb], w=[self.PS.k(b)])
                if tc % 2:
                    self.cp(ust[:, tc, :], psb, r=[self.PS.k(b)], w=[ust])
                else:
                    self.act(ust[:, tc, :], psb, AF.Copy, r=[self.PS.k(b)], w=[ust])
            self.ld(self.uscr[:, 0:nt, cc * 128:(cc + 1) * 128], ust[:, :, :], r=[ust], w=[self.uscr_tok])
        S.barrier()
        A.release(m2)
        m2 = A.mark()
        self.mark(f'hy{L} P2P3')
        gT = hT
        mb = big.mark()
        utm = big.alloc("hy_utm", (nt, 512), BF16)
        Ya = big.alloc("hy_ya", (nt, 512), BF16)
        Yb = big.alloc("hy_yb", (nt, 512), BF16)
        x0g = big.alloc("hy_x0g", (4, L), BF16)
        fwp = [A.alloc(f"hy_fwp{i}", (2, nt, 128), BF16) for i in range(2)]
        hsl = [A.alloc(f"hy_hsl{i}", (2, 512), F32) for i in range(2)]
        ucs = [A.alloc(f"hy_ucs{i}", (2, 512), F32) for i in range(2)]
        tq = [A.alloc(f"hy_tq{i}", (512,), F32) for i in range(4)]
        invp = A.alloc("hy_invp", (2, nt, TB), BF16)
        Hs = self.hscr[L]
        for g in range(2):
            self.ld(utm[:, :, :], self.uscr[:, 0:nt, g * 512:(g + 1) * 512], r=[self.uscr_tok], w=[utm])
            self.ld(x0g[:, :, :], self.x0scr[g * 4:(g + 1) * 4, :, 0:L].rearrange("c p t -> p c t"), r=[self.x0scr_tok], w=[x0g])
            for fc in range(nt):
                fp = fwp[fc % 2]
                hs = hsl[fc % 2]
                uc = ucs[fc % 2]
                self.ld(fp[:, :, :, :], self.din[f"fw{L}"][fc].rearrange("a p t f -> p a t f"), w=[fp])
                self.ld(hs[:, :, :], Hs[:, fc, :, g * 512:(g + 1) * 512].rearrange("a p c -> p a c"), r=[self.hscr_tok[L]], w=[hs])
                for part in range(2):
                    b = self.psbank()
                    ps = self.PS[:, b, 0:512]
                    for tc in range(nt):
                        self.mm(ps, fp[:, part, tc, :], utm[:, tc, :], tc == 0, tc == nt - 1, r=[fp, utm], w=[self.PS.k(b)])
                    self.act(uc[:, part, :], ps, AF.Copy, r=[self.PS.k(b)], w=[uc.k(part)])
                self.tt(tq[0][:, :], uc[:, 0, :], hs[:, 0, :], ALU.mult, r=[uc.k(0), hs], w=[tq[0]])
                self.tt(tq[1][:, :], uc[:, 1, :], hs[:, 1, :], ALU.mult, r=[uc.k(1), hs], w=[tq[1]])
                self.tt(Ya[:, fc, :], tq[0][:, :], tq[1][:, :], ALU.subtract, r=[tq[0], tq[1]], w=[Ya])
                self.tt(tq[2][:, :], uc[:, 0, :], hs[:, 1, :], ALU.mult, r=[uc.k(0), hs], w=[tq[2]])
                self.tt(tq[3][:, :], uc[:, 1, :], hs[:, 0, :], ALU.mult, r=[uc.k(1), hs], w=[tq[3]])
                self.tt(Yb[:, fc, :], tq[2][:, :], tq[3][:, :], ALU.add, r=[tq[2], tq[3]], w=[Yb])
            for tb_ in range(ntb):
                self.ld(invp[:, :, :, :], self.din[f"inv{L}"][tb_].rearrange("a p f t -> p a f t"), w=[invp])
                for cc in range(4):
                    b = self.psbank()
                    ps = self.PS[:, b, 0:TB]
                    n = 0
                    for part, Y in enumerate((Ya, Yb)):
                        for fc in range(nt):
                            self.mm(ps, Y[:, fc, cc * 128:(cc + 1) * 128], invp[:, part, fc, :], n == 0, n == 2 * nt - 1,
                                    r=[Y, invp], w=[self.PS.k(b)])
                            n += 1
                    self.tt(gT[:, g * 4 + cc, tb_ * TB:(tb_ + 1) * TB], ps, x0g[:, cc, tb_ * TB:(tb_ + 1) * TB], ALU.mult,
                            r=[self.PS.k(b), x0g], w=[gT])
            S.barrier()
        self.mark(f'hy{L} P4')
        A.release(m2)
        big.release(mb)
        if spill:
            self.ld(xT[:, :, :], self.xscr.rearrange("p (a t) -> p a t", a=8), r=[self.xscr_tok], w=[xT])
        S.barrier()
        groups = [(dc * 128, 128, [(0, 128, dc)]) for dc in range(8)]

        def evac4(dc, t0, t1, ps, ptok):
            self.stt(xT[:, dc, t0:t1], ps, mods[:, 16 + dc, col:col + 1], xT[:, dc, t0:t1], ALU.mult, ALU.add,
                     r=[ptok, mods, xT], w=[xT])

        self.linear(self.din["hy_w_out"][j], D, groups, lambda kc, t0, t1: gT[:, kc, t0:t1], [gT], L, evac4)
        A.release(m0)
        self.layernorm(xT, L, li, 0)


    def mla(self, j, li, ctx_out):
        A, S, A2 = self.A, self.S, self.A2
        mods = self.mods
        d = self.din
        TT = CTX + SEQ
        xT, cT = self.xT, self.cT
        SCALE = 192.0 ** -0.5
        m0 = A.mark()
        hT = A.alloc("ml_h", (8, TT), BF16)
        self.modulate(hT, 0, cT, CTX, 0, 1)
        self.modulate(hT, CTX, xT, SEQ, 0, 0)
        self.prescale(xT, SEQ)
        self.prescale(cT, CTX)
        self.ld(self.xscr.rearrange("p (a t) -> p a t", a=8), xT[:, :, :], r=[xT], w=[self.xscr_tok])
        S.barrier()
        A2.top = 0
        cqn = A.alloc("ml_cqn", (3, TT), BF16)
        ckvn = A.alloc("ml_ckvn", (2, TT), BF16)
        krT = A.alloc("ml_krT", (TT,), BF16)
        ropeC = A.alloc("ml_ropeC", (TT,), F32)
        ropeS = A.alloc("ml_ropeS", (TT,), F32)
        ropeP = A.alloc("ml_ropeP", (64,), F32)
        onesb = A.alloc("ml_onesb", (128,), BF16)
        self.ld(ropeC[0:64, :], d["ropeC"][:, :], w=[ropeC]); self.ld(ropeS[0:64, :], d["ropeS"][:, :], w=[ropeS])
        self.ld(ropeP[0:64, :], d["ropeP"][:, :], w=[ropeP]); self.ld(onesb[:, :], d["onesb"][:, :], w=[onesb])
        qg = A.alloc("ml_qg", (3,), F32); kvg = A.alloc("ml_kvg", (2,), F32)
        self.ld(qg[:, :], d["mla_qg"][:, j, :], w=[qg]); self.ld(kvg[:, :], d["mla_kvg"][:, j, :], w=[kvg])
        ma = A2.mark()
        craw = A2.alloc("ml_craw", (5, TT), F32)
        kraw = A.alloc("ml_kraw", (TT,), F32)
        groups = [(0, 256, [(0, 128, 0), (128, 128, 1)]), (256, 256, [(0, 128, 2), (128, 128, 3)]), (512, 192, [(0, 128, 4), (128, 64, 5)])]

        def evac_in(ch, t0, t1, ps, ptok):
            if ch < 5:
                if ch % 2:
                    self.act(craw[:, ch, t0:t1], ps, AF.Copy, r=[ptok], w=[craw.k(ch)])
                else:
                    self.cp(craw[:, ch, t0:t1], ps, r=[ptok], w=[craw.k(ch)])
            else:
                self.cp(kraw[0:64, t0:t1], ps, r=[ptok], w=[kraw])

        self.linear(d["mla_w_in"][j], D, groups, lambda kc, t0, t1: hT[:, kc, t0:t1], [hT], TT, evac_in)
        self.mark('mla norms')
        m1 = A.mark()
        sq = [A.alloc(f"ml_sq{i}", (512,), F32) for i in range(2)]
        rstd = A.alloc("ml_rstd", (512,), F32)
        for (c0, nch, dst, g) in ((0, 3, cqn, qg), (3, 2, ckvn, kvg)):
            for t0 in range(0, TT, 512):
                t1 = min(TT, t0 + 512)
                n = t1 - t0
                b = self.psbank()
                for ch in range(nch):
                    s_ = sq[ch % 2]
                    self.act(s_[:, 0:n], craw[:, c0 + ch, t0:t1], AF.Square, r=[craw.k(c0 + ch)], w=[s_])
                    self.mm(self.PS[:, b, 0:n], self.ones[:, :], s_[:, 0:n], ch == 0, ch == nch - 1, r=[self.ones, s_], w=[self.PS.k(b)])
                self.rsqrt(rstd[:, 0:n], self.PS[:, b, 0:n], 1.0 / (128 * nch), RMS_EPS, rstd, in_toks=[self.PS.k(b)])
                for ch in range(nch):
                    self.stt(dst[:, ch, t0:t1], craw[:, c0 + ch, t0:t1], g[:, ch:ch + 1], rstd[:, 0:n], ALU.mult, ALU.mult,
                             r=[craw.k(c0 + ch), g, rstd], w=[dst])
        rt = [A.alloc(f"ml_rt{i}", (512,), F32) for i in range(3)]

        def rope(dst, src, src_tok):
            for t0 in range(0, TT, 512):
                t1 = min(TT, t0 + 512)
                n = t1 - t0
                b = self.psbank()
                self.mm(self.PS[0:64, b, 0:n], ropeP[0:64, :], src[0:64, t0:t1], True, True, r=[ropeP, src_tok], w=[self.PS.k(b)])
                self.tt(rt[0][0:64, 0:n], src[0:64, t0:t1], ropeC[0:64, t0:t1], ALU.mult, r=[src_tok, ropeC], w=[rt[0]])
                self.tt(rt[1][0:64, 0:n], self.PS[0:64, b, 0:n], ropeS[0:64, t0:t1], ALU.mult, r=[self.PS.k(b), ropeS], w=[rt[1]])
                self.tt(dst[0:64, t0:t1], rt[0][0:64, 0:n], rt[1][0:64, 0:n], ALU.add, r=[rt[0], rt[1]], w=[dst])

        rope(krT, kraw, kraw)
        S.barrier()
        A2.release(ma)
        self.mark('mla heads')
        oT = hT
        qn = A2.alloc("ml_qn", (TT,), BF16)
        qr = A2.alloc("ml_qr", (TT,), BF16)
        qraw = A2.alloc("ml_qraw", (TT,), F32)
        kn = A2.alloc("ml_kn", (TT,), BF16)
        vtm = A2.alloc("ml_vtm", (18, 128), BF16)
        wq = [A2.alloc(f"ml_wq{i}", (3, 192), BF16) for i in range(2)]
        wkv = [A2.alloc(f"ml_wkv{i}", (2, 256), BF16) for i in range(2)]
        pT = [A2.alloc(f"ml_pT{i}", (512,), BF16) for i in range(3)]
        rl = A2.alloc("ml_rl", (512,), F32)
        for h in range(8):
            wq_, wkv_ = wq[h % 2], wkv[h % 2]
            self.ld(wq_[:, :, :], d["mla_w_uq"][j][:, h * 192:(h + 1) * 192].rearrange("(kc p) n -> p kc n", p=128), w=[wq_], q="pool")
            self.ld(wkv_[:, :, :], d["mla_w_ukv"][j][:, h * 256:(h + 1) * 256].rearrange("(kc p) n -> p kc n", p=128), w=[wkv_], q="pool")
            for t0 in range(0, TT, 512):
                t1 = min(TT, t0 + 512)
                n = t1 - t0
                b = self.psbank()
                for kc in range(3):
                    self.mm(self.PS[:, b, 0:n], wq_[:, kc, 0:128], cqn[:, kc, t0:t1], kc == 0, kc == 2, r=[wq_, cqn], w=[self.PS.k(b)])
                self.act(qn[:, t0:t1], self.PS[:, b, 0:n], AF.Copy, r=[self.PS.k(b)], w=[qn])
                b = self.psbank()
                for kc in range(3):
                    self.mm(self.PS[0:64, b, 0:n], wq_[:, kc, 128:192], cqn[:, kc, t0:t1], kc == 0, kc == 2, r=[wq_, cqn], w=[self.PS.k(b)])
                self.cp(qraw[0:64, t0:t1], self.PS[0:64, b, 0:n], r=[self.PS.k(b)], w=[qraw])
                b = self.psbank()
                for kc in range(2):
                    self.mm(self.PS[:, b, 0:n], wkv_[:, kc, 0:128], ckvn[:, kc, t0:t1], kc == 0, kc == 1, r=[wkv_, ckvn], w=[self.PS.k(b)])
                self.act(kn[:, t0:t1], self.PS[:, b, 0:n], AF.Copy, r=[self.PS.k(b)], w=[kn])
            rope(qr, qraw, qraw)
            for tc in range(18):
                b = self.psbank()
                for kc in range(2):
                    self.mm(self.PS[:, b, 0:128], ckvn[:, kc, tc * 128:(tc + 1) * 128], wkv_[:, kc, 128:256], kc == 0, kc == 1,
                            r=[wkv_, ckvn], w=[self.PS.k(b)])
                self.cp(vtm[:, tc, :], self.PS[:, b, 0:128], r=[self.PS.k(b)], w=[vtm])
            blocks = [(CTX + qb * 512, 512, 18) for qb in range(4)]
            if ctx_out:
                blocks.append((0, CTX, 2))
            self.ps_pool = [0, 1, 2, 3]
            for bi, (q0, nq, nk) in enumerate(blocks):
                bo = 4 + (bi % 2)
                bl = 6 + (bi % 2)
                def st_exp(kc_):
                    b = self.psbank()
                    ps = self.PS[:, b, 0:nq]
                    self.mm(ps, kn[:, kc_ * 128:(kc_ + 1) * 128], qn[:, q0:q0 + nq], True, False, r=[kn, qn], w=[self.PS.k(b)])
                    self.mm(ps, krT[0:64, kc_ * 128:(kc_ + 1) * 128], qr[0:64, q0:q0 + nq], False, True, r=[krT, qr], w=[self.PS.k(b)])
                    p_ = pT[kc_ % 3]
                    self.act(p_[:, 0:nq], ps, AF.Exp, r=[self.PS.k(b)], w=[p_], scale=SCALE)

                st_exp(0)
                for kc_ in range(nk):
                    if kc_ + 1 < nk:
                        st_exp(kc_ + 1)
                    p_ = pT[kc_ % 3]
                    self.mm(self.PS[:, bo, 0:nq], vtm[:, kc_, :], p_[:, 0:nq], kc_ == 0, kc_ == nk - 1, r=[vtm, p_], w=[self.PS.k(bo)])
                    self.mm(self.PS[:, bl, 0:nq], onesb[:, :], p_[:, 0:nq], kc_ == 0, kc_ == nk - 1, r=[onesb, p_], w=[self.PS.k(bl)])
                self.act(rl[:, 0:nq], self.PS[:, bl, 0:nq], AF.Ln, r=[self.PS.k(bl)], w=[rl])
                self.act(rl[:, 0:nq], rl[:, 0:nq], AF.Exp, r=[rl], w=[rl], scale=-1.0)
                self.tt(oT[:, h, q0:q0 + nq], self.PS[:, bo, 0:nq], rl[:, 0:nq], ALU.mult, r=[self.PS.k(bo), rl], w=[oT])
            self.ps_pool = list(range(8))
        S.barrier()
        A.release(m1)
        self.mark('mla outproj')
        self.ld(xT[:, :, :], self.xscr.rearrange("p (a t) -> p a t", a=8), r=[self.xscr_tok], w=[xT])
        S.barrier()
        groups = [(dc * 128, 128, [(0, 128, dc)]) for dc in range(8)]

        def evac_o(dc, t0, t1, ps, ptok):
            if t0 < CTX:
                if ctx_out:
                    self.stt(cT[:, dc, t0:t1], ps, mods[:, 16 + dc, 1:2], cT[:, dc, t0:t1], ALU.mult, ALU.add, r=[ptok, mods, cT], w=[cT])
            else:
                self.stt(xT[:, dc, t0 - CTX:t1 - CTX], ps, mods[:, 16 + dc, 0:1], xT[:, dc, t0 - CTX:t1 - CTX], ALU.mult, ALU.add,
                         r=[ptok, mods, xT], w=[xT])

        self.linear(d["mla_w_out"][j], D, groups, lambda kc, t0, t1: oT[:, kc, t0:t1], [oT], TT, evac_o, tb=256)
        A.release(m0)
        self.layernorm(xT, SEQ, li, 0)
        if ctx_out:
            self.layernorm(cT, CTX, li, 0)


    def gdn(self, j, li, ctx_out):
        assert not ctx_out
        A, S, A2 = self.A, self.S, self.A2
        mods = self.mods
        d = self.din
        TT = CTX + SEQ
        NP = TT // 128
        xT, cT = self.xT, self.cT
        m0 = A.mark()
        hT = A.alloc("gd_h", (8, TT), BF16)
        self.modulate(hT, 0, cT, CTX, 0, 1)
        self.modulate(hT, CTX, xT, SEQ, 0, 0)
        self.prescale(xT, SEQ)
        self.ld(self.xscr.rearrange("p (a t) -> p a t", a=8), xT[:, :, :], r=[xT], w=[self.xscr_tok])
        S.barrier()
        A2.top = 0
        gtile = [A.alloc(f"gd_gtile{i}", (512,), BF16) for i in range(2)]
        abtm = A.alloc("gd_abtm", (NP, 32), F32)
        msk = A.alloc("gd_msk", (16, 128), F32)
        blk = A.alloc("gd_blk", (2,), F32)
        gcw = A.alloc("gd_cw", (24, 3), F32)
        gab = A.alloc("gd_ab", (2,), F32)
        gon = A.alloc("gd_on", (1,), F32)
        self.ld(msk[:, :, :], d["gmask"][:, :, :], w=[msk]); self.ld(blk[:, :], d["gblk"][:, :], w=[blk])
        self.ld(gcw[:, :, :], d["gdn_cw"][:, :, :], w=[gcw]); self.ld(gab[0:16, :], d["gdn_ab"][:, :], w=[gab])
        self.ld(gon[:, :], d["gdn_on"][:, :], w=[gon])
        TRI = lambda dr: msk[:, 0 + dr, :]
        REM = lambda dr: msk[:, 2 + dr, :]
        INCL = lambda dr: msk[:, 4 + dr, :]
        STRICT = lambda dr: msk[:, 6 + dr, :]
        NTRI = lambda dr: msk[:, 8 + dr, :]
        Win = d["gdn_w_in"][j]
        tcol = lambda t: (1 + t) if t < CTX else (3 + t)
        m1 = A.mark()
        abr = A.alloc("gd_abr", (2, TT), F32)
        nal = A.alloc("gd_nal", (1,), F32)
        self.act(nal[0:16, :], gab[0:16, 0:1], AF.Exp, r=[gab], w=[nal])
        self.ts(nal[0:16, :], nal[0:16, :], -1.0, None, ALU.mult, None, r=[nal], w=[nal])
        groups = [(4096, 32, [(0, 16, 0), (16, 16, 1)])]

        def evac_ab(which, t0, t1, ps, ptok):
            if which == 0:
                self.act(abr[0:16, 0, t0:t1], ps, AF.Exp, r=[ptok, gab], w=[abr.k(0)], bias=gab[0:16, 1:2])
                self.ts(abr[0:16, 0, t0:t1], abr[0:16, 0, t0:t1], 1.0, None, ALU.add, None, r=[abr.k(0)], w=[abr.k(0)])
                self.act(abr[0:16, 0, t0:t1], abr[0:16, 0, t0:t1], AF.Ln, r=[abr.k(0)], w=[abr.k(0)])
                self.ts(abr[0:16, 0, t0:t1], abr[0:16, 0, t0:t1], nal[0:16, 0:1], None, ALU.mult, None, r=[abr.k(0), nal], w=[abr.k(0)])
            else:
                self.act(abr[0:16, 1, t0:t1], ps, AF.Sigmoid, r=[ptok], w=[abr.k(1)])

        wst_ab = self.alloc_wst(8, 32)
        self.linear(Win, D, groups, lambda kc, t0, t1: hT[:, kc, t0:t1], [hT], TT, evac_ab, wst=wst_ab)
        for p in range(NP):
            b = self.psbank()
            for which in range(2):
                self.tr(self.PS[:, b, which * 16:(which + 1) * 16], abr[0:16, which, p * 128:(p + 1) * 128], self.ident[0:16, 0:16],
                        r=[abr.k(which), self.ident], w=[self.PS.k(b)])
            self.cp(abtm[:, p, :], self.PS[:, b, 0:32], r=[self.PS.k(b)], w=[abtm])
        S.barrier()
        A.release(m1)
        self.mark('gdn heads')
        m_heads = A.mark()
        zp = A2.alloc("gd_zp", (TT + 4,), F32)
        qT = A2.alloc("gd_qT", (TT,), F32)
        kT = A2.alloc("gd_kT", (TT,), F32)
        vT = A2.alloc("gd_vT", (TT,), F32)
        ktm = A2.alloc("gd_ktm", (NP, 128), F32)
        vtm = A2.alloc("gd_vtm", (NP, 128), F32)
        oTh = A2.alloc("gd_oT", (SEQ,), F32)
        gate = A.alloc("gd_gate", (SEQ,), BF16)
        wst = self.alloc_wst(8, 128, nstage=1)
        _sq = A.alloc("gd_sq", (512,), F32)
        sqt = [_sq, _sq]
        rstd = A.alloc("gd_rstd", (512,), F32)
        St = [A.alloc(f"gd_S{i}", (128,), F32) for i in range(2)]
        vnew = [A.alloc(f"gd_vn{i}", (128,), F32) for i in range(2)]
        names = ("u", "wT", "qdT", "qkT", "kdec", "sc")
        PB = [[{n: A.alloc(f"gd_{n}{sl}{k}", (128,) if n != "sc" else (8,), F32) for n in names} for k in range(4)] for sl in range(2)]
        TMP = [{n: A.alloc(f"gd_t_{n}{k}", (128,), F32) for n in ("gmat", "gam", "ecr", "t", "A0", "B0", "T0", "T1", "P0", "P1", "Lb", "Ub", "Y1", "Y2", "vb", "kbe", "qk")}
               for k in range(4)]
        bec = [A.alloc(f"gd_bec{k}", (1,), F32) for k in range(4)]
        self.gdn_top = A.top
        for i in range(4):
            self.S.op("pool", lambda e, i=i: e.memset(zp[:, (0, CTX + 1, CTX + 2, TT + 3)[i]:(0, CTX + 1, CTX + 2, TT + 3)[i] + 1], 0.0), w=[zp])

        pf = list(range(NP))
        pb = [1, 0] + list(range(NP - 1, 1, -1))

        held = set()

        def acq():
            for _ in range(8):
                self.ps_rr = (self.ps_rr + 1) % 8
                if self.ps_rr not in held:
                    held.add(self.ps_rr)
                    return self.ps_rr
            raise RuntimeError("no free PSUM bank")

        def rel(b_):
            held.discard(b_)

        def prep(sl, k, dr, p, h):
            T_, O_ = TMP[k], PB[sl][k]
            bec_ = bec[k]
            c0, c1 = p * 128, (p + 1) * 128
            gcol = abtm[:, p, dr * 8 + h:dr * 8 + h + 1]
            bcol = abtm[:, p, 16 + dr * 8 + h:16 + dr * 8 + h + 1]
            SYM = lambda lv: msk[:, 10 + lv, :]
            self.ts(T_["gmat"][:, :], self.ones[:, :], gcol, None, ALU.mult, None, r=[self.ones, abtm], w=[T_["gmat"]])
            self.ts(T_["vb"][:, :], vtm[:, p, :], bcol, None, ALU.mult, None, r=[vtm, abtm], w=[T_["vb"]], eng="pool")
            yield
            bx = acq()
            X = self.PS[:, bx, :]
            xt_ = self.PS.k(bx)
            self.mm(X[:, 0:128], TRI(dr), T_["gmat"][:, :], True, False, r=[msk, T_["gmat"]], w=[xt_])
            self.mm(X[:, 0:128], T_["gmat"][:, :], NTRI(dr), False, True, r=[msk, T_["gmat"]], w=[xt_])
            self.mm(X[:, 128:256], T_["gmat"][:, :], TRI(dr), True, True, r=[msk, T_["gmat"]], w=[xt_])
            self.mm(X[:, 256:257], TRI(dr), gcol, True, True, r=[msk, abtm], w=[xt_])
            self.mm(X[:, 257:258], REM(dr), gcol, True, True, r=[msk, abtm], w=[xt_])
            self.mm(X[:, 258:260], T_["gmat"][:, :], blk[:, :], True, True, r=[blk, T_["gmat"]], w=[xt_])
            yield
            self.act(T_["gam"][:, :], X[:, 0:128], AF.Relu, r=[xt_], w=[T_["gam"]], scale=-1.0)
            self.act(T_["ecr"][:, :], X[:, 128:256], AF.Exp, r=[xt_], w=[T_["ecr"]])
            self.act(O_["sc"][:, 0:4], X[:, 256:260], AF.Exp, r=[xt_], w=[O_["sc"]])
            rel(bx)
            yield
            self.act(T_["gam"][:, :], T_["gam"][:, :], AF.Exp, r=[T_["gam"]], w=[T_["gam"]], scale=-1.0)
            self.tt(bec_[:, :], bcol, O_["sc"][:, 0:1], ALU.mult, r=[abtm, O_["sc"]], w=[bec_])
            self.ts(O_["kdec"][:, :], ktm[:, p, :], O_["sc"][:, 1:2], None, ALU.mult, None, r=[ktm, O_["sc"]], w=[O_["kdec"]])
            self.tt(O_["qdT"][:, :], qT[:, c0:c1], T_["ecr"][:, :], ALU.mult, r=[qT, T_["ecr"]], w=[O_["qdT"]], eng="pool")
            yield
            self.tt(T_["gam"][:, :], T_["gam"][:, :], INCL(dr), ALU.mult, r=[T_["gam"], msk], w=[T_["gam"]])
            self.ts(T_["kbe"][:, :], ktm[:, p, :], bec_[:, 0:1], None, ALU.mult, None, r=[ktm, bec_], w=[T_["kbe"]], eng="pool")
            by = acq()
            Y = self.PS[:, by, :]
            yt_ = self.PS.k(by)
            self.mm(Y[:, 0:128], kT[:, c0:c1], kT[:, c0:c1], True, True, r=[kT], w=[yt_])
            self.mm(Y[:, 128:256], qT[:, c0:c1], kT[:, c0:c1], True, True, r=[qT, kT], w=[yt_])
            yield
            self.tt(T_["t"][:, :], Y[:, 0:128], T_["gam"][:, :], ALU.mult, r=[yt_, T_["gam"]], w=[T_["t"]])
            self.tt(T_["qk"][:, :], Y[:, 128:256], T_["gam"][:, :], ALU.mult, r=[yt_, T_["gam"]], w=[T_["qk"]])
            rel(by)
            yield
            self.stt(T_["A0"][:, :], T_["t"][:, :], bcol, STRICT(dr), ALU.mult, ALU.mult, r=[T_["t"], abtm, msk], w=[T_["A0"]])
            yield
            bz = acq()
            Z = self.PS[:, bz, :]
            zt_ = self.PS.k(bz)
            self.mm(Z[:, 0:128], T_["A0"][:, :], self.ident[:, :], True, True, r=[T_["A0"], self.ident], w=[zt_])
            self.mm(Z[:, 128:256], T_["qk"][:, :], self.ident[:, :], True, True, r=[T_["qk"], self.ident], w=[zt_])
            self.tt(T_["Lb"][:, :], T_["A0"][:, :], SYM(0), ALU.mult, r=[T_["A0"], msk], w=[T_["Lb"]])
            yield
            self.act(T_["B0"][:, :], Z[:, 0:128], AF.Copy, r=[zt_], w=[T_["B0"]])
            self.act(O_["qkT"][:, :], Z[:, 128:256], AF.Copy, r=[zt_], w=[O_["qkT"]])
            rel(bz)
            self.tt(T_["T0"][:, :], self.ident[:, :], T_["Lb"][:, :], ALU.subtract, r=[self.ident, T_["Lb"]], w=[T_["T0"]])
            yield
            self.tt(T_["Ub"][:, :], T_["B0"][:, :], SYM(0), ALU.mult, r=[T_["B0"], msk], w=[T_["Ub"]])
            yield
            self.tt(T_["P0"][:, :], self.ident[:, :], T_["Ub"][:, :], ALU.subtract, r=[self.ident, T_["Ub"]], w=[T_["P0"]])
            Tcur, Tnxt, Pcur, Pnxt = "T0", "T1", "P0", "P1"
            for lv in range(1, 6):
                last = (lv == 5)
                self.tt(T_["Lb"][:, :], T_["A0"][:, :], SYM(lv), ALU.mult, r=[T_["A0"], msk], w=[T_["Lb"]])
                if not last:
                    self.tt(T_["Ub"][:, :], T_["B0"][:, :], SYM(lv), ALU.mult, r=[T_["B0"], msk], w=[T_["Ub"]])
                yield
                bw = acq()
                Wp = self.PS[:, bw, :]
                wt_ = self.PS.k(bw)
                self.mm(Wp[:, 0:128], T_["Lb"][:, :], T_[Pcur][:, :], True, True, r=[T_["Lb"], T_[Pcur]], w=[wt_])
                if not last:
                    self.mm(Wp[:, 128:256], T_["Ub"][:, :], T_[Tcur][:, :], True, True, r=[T_["Ub"], T_[Tcur]], w=[wt_])
                yield
                self.act(T_["Y1"][:, :], Wp[:, 0:128], AF.Copy, r=[wt_], w=[T_["Y1"]])
                if not last:
                    self.act(T_["Y2"][:, :], Wp[:, 128:256], AF.Copy, r=[wt_], w=[T_["Y2"]])
                rel(bw)
                yield
                bv = acq()
                Vp = self.PS[:, bv, :]
                vt_ = self.PS.k(bv)
                self.mm(Vp[:, 0:128], T_[Tcur][:, :], T_["Y1"][:, :], True, True, r=[T_[Tcur], T_["Y1"]], w=[vt_])
                if not last:
                    self.mm(Vp[:, 128:256], T_[Pcur][:, :], T_["Y2"][:, :], True, True, r=[T_[Pcur], T_["Y2"]], w=[vt_])
                yield
                self.tt(T_[Pnxt][:, :], T_[Pcur][:, :], Vp[:, 0:128], ALU.subtract, r=[T_[Pcur], vt_], w=[T_[Pnxt]])
                if not last:
                    self.tt(T_[Tnxt][:, :], T_[Tcur][:, :], Vp[:, 128:256], ALU.subtract, r=[T_[Tcur], vt_], w=[T_[Tnxt]])
                rel(bv)
                Tcur, Tnxt = Tnxt, Tcur
                Pcur, Pnxt = Pnxt, Pcur
                yield
            P = T_[Pcur]
            bu = acq()
            U = self.PS[:, bu, :]
            ut_ = self.PS.k(bu)
            self.mm(U[:, 0:128], P[:, :], T_["vb"][:, :], True, True, r=[P, T_["vb"]], w=[ut_])
            self.mm(U[:, 128:256], T_["kbe"][:, :], P[:, :], True, True, r=[P, T_["kbe"]], w=[ut_])
            yield
            self.cp(O_["u"][:, :], U[:, 0:128], r=[ut_], w=[O_["u"]])
            self.cp(O_["wT"][:, :], U[:, 128:256], r=[ut_], w=[O_["wT"]])
            rel(bu)
            yield

        def chain(sl, dr, steps):
            Sd = St[dr]
            vn = vnew[dr]
            halves = (0, 1) if dr == 0 else (1, 0)
            for (k, p, first_dir_write) in steps:
                O_ = PB[sl][k]
                for hf in halves:
                    hr = slice(hf * 64, (hf + 1) * 64)
                    ba = acq()
                    self.mm(self.PS[:, ba, 0:128], O_["wT"][:, :], Sd[:, :], True, True, r=[O_["wT"], Sd], w=[self.PS.k(ba)])
                    yield
                    self.tt(vn[hr, :], O_["u"][hr, :], self.PS[hr, ba, 0:128], ALU.subtract, r=[O_["u"], self.PS.k(ba)], w=[vn])
                    rel(ba)
                    yield
                    bs_ = acq()
                    bt_ = self.PS.k(bs_)
                    self.mm(self.PS[:, bs_, 0:128], O_["kdec"][hr, :], vn[hr, :], True, True, r=[O_["kdec"], vn], w=[bt_])
                    if p >= 2:
                        self.mm(self.PS[:, bs_, 128:256], Sd[:, :], O_["qdT"][:, :], True, False, r=[Sd, O_["qdT"]], w=[bt_])
                        self.mm(self.PS[:, bs_, 128:256], vn[hr, :], O_["qkT"][hr, :], False, True, r=[vn, O_["qkT"]], w=[bt_])
                    yield
                    self.stt(Sd[:, :], Sd[:, :], O_["sc"][:, 2 + hf:3 + hf], self.PS[:, bs_, 0:128], ALU.mult, ALU.add,
                             r=[Sd, O_["sc"], bt_], w=[Sd])
                    if p >= 2 and dr in self.gdn_dirs:
                        tk0 = (p - 2) * 128 + hf * 64
                        src_o = self.PS[:, bs_, 128 + hf * 64:128 + (hf + 1) * 64]
                        if first_dir_write and not self.dbg_gdn:
                            self.cp(oTh[:, tk0:tk0 + 64], src_o, r=[bt_], w=[oTh.k(p)])
                        else:
                            self.tt(oTh[:, tk0:tk0 + 64], oTh[:, tk0:tk0 + 64], src_o, ALU.add, r=[bt_, oTh.k(p)], w=[oTh.k(p)])
                    rel(bs_)
                    yield

        def run_rr(gens):
            gens = list(gens)
            while gens:
                for g_ in list(gens):
                    try:
                        next(g_)
                    except StopIteration:
                        gens.remove(g_)

        for h in range(8 if self.gdn_stop >= 2 else 0):
            for part in range(4):
                groups = [(part * 1024 + h * 128, 128, [(0, 128, part)])]
                if part < 3:
                    def evac_p(part_, t0, t1, ps, ptok):
                        self.act(zp[:, tcol(t0):tcol(t0) + (t1 - t0)], ps, AF.Copy, r=[ptok], w=[zp])
                    self.linear(Win, D, groups, lambda kc, t0, t1: hT[:, kc, t0:t1], [hT], TT, evac_p, tb=256, wst=wst)
                    dst = (qT, kT, vT)[part]
                    q_ = part * 8 + h
                    for (z0, n, o0) in ((0, CTX, 0), (CTX + 2, SEQ, CTX)):
                        self.ts(dst[:, o0:o0 + n], zp[:, z0:z0 + n], gcw[:, q_, 0:1], None, ALU.mult, None, r=[zp, gcw], w=[dst])
                        self.stt(dst[:, o0:o0 + n], zp[:, z0 + 1:z0 + 1 + n], gcw[:, q_, 1:2], dst[:, o0:o0 + n], ALU.mult, ALU.add, r=[zp, gcw, dst], w=[dst])
                        self.stt(dst[:, o0:o0 + n], zp[:, z0 + 2:z0 + 2 + n], gcw[:, q_, 2:3], dst[:, o0:o0 + n], ALU.mult, ALU.add, r=[zp, gcw, dst], w=[dst])
                    self.act(dst[:, :], dst[:, :], AF.Silu, r=[dst], w=[dst])
                else:
                    def evac_g(part_, t0, t1, ps, ptok):
                        if t0 >= CTX:
                            self.act(gate[:, t0 - CTX:t1 - CTX], ps, AF.Silu, r=[ptok], w=[gate])
                    self.linear(Win, D, groups, lambda kc, t0, t1: hT[:, kc, t0:t1], [hT], TT, evac_g, tb=256, wst=wst)
            if self.gdn_stop < 2.3:
                continue
            if h == 0: self.mark('gdn h0 l2norm')
            for (src, sc_) in ((qT, 128.0 ** -0.5), (kT, 1.0)):
                for t0 in range(0, TT, 512):
                    t1 = min(TT, t0 + 512)
                    n = t1 - t0
                    b = self.psbank()
                    s_ = sqt[0]
                    self.act(s_[:, 0:n], src[:, t0:t1], AF.Square, r=[src], w=[s_])
                    self.mm(self.PS[:, b, 0:n], self.ones[:, :], s_[:, 0:n], True, True, r=[self.ones, s_], w=[self.PS.k(b)])
                    self.rsqrt(rstd[:, 0:n], self.PS[:, b, 0:n], 1.0, RMS_EPS, rstd, in_toks=[self.PS.k(b)])
                    self.stt(src[:, t0:t1], src[:, t0:t1], sc_, rstd[:, 0:n], ALU.mult, ALU.mult, r=[src, rstd], w=[src])
            if self.gdn_stop < 2.6:
                continue
            for p in range(NP):
                b = self.psbank()
                self.mm(self.PS[:, b, 0:128], kT[:, p * 128:(p + 1) * 128], self.ident[:, :], True, True, r=[kT, self.ident], w=[self.PS.k(b)])
                self.mm(self.PS[:, b, 128:256], vT[:, p * 128:(p + 1) * 128], self.ident[:, :], True, True, r=[vT, self.ident], w=[self.PS.k(b)])
                if p % 2:
                    self.cp(ktm[:, p, :], self.PS[:, b, 0:128], r=[self.PS.k(b)], w=[ktm])
                    self.cp(vtm[:, p, :], self.PS[:, b, 128:256], r=[self.PS.k(b)], w=[vtm])
                else:
                    self.act(ktm[:, p, :], self.PS[:, b, 0:128], AF.Copy, r=[self.PS.k(b)], w=[ktm])
                    self.act(vtm[:, p, :], self.PS[:, b, 128:256], AF.Copy, r=[self.PS.k(b)], w=[vtm])
            if self.gdn_stop < 3:
                continue
            for dr in range(2):
                self.S.op("pool", lambda e, dr=dr: e.memset(St[dr][:, :], 0.0), w=[St[dr]])
            if self.dbg_gdn:
                for p_ in range(2, NP):
                    self.S.op("pool", lambda e, p_=p_: e.memset(oTh[:, (p_ - 2) * 128:(p_ - 1) * 128], 0.0), w=[oTh.k(p_)])
            if h == 0: self.mark('gdn h0 recur')
            NBAT = NP // 2
            def batch_preps(bi):
                gl = []
                for st in range(2):
                    i_ = 2 * bi + st
                    gl.append(prep(bi % 2, st * 2 + 0, 0, pf[i_], h))
                    gl.append(prep(bi % 2, st * 2 + 1, 1, pb[i_], h))
                return gl
            def batch_chains(bi):
                fs, bs2 = [], []
                for st in range(2):
                    i_ = 2 * bi + st
                    pfi, pbi = pf[i_], pb[i_]
                    fs.append((st * 2 + 0, pfi, (pfi >= 2 and pfi < (NP + 1 - pfi) + 0.5)))
                    bs2.append((st * 2 + 1, pbi, (pbi >= 2 and (NP + 1 - pbi) < pbi)))
                return [chain(bi % 2, 0, fs), chain(bi % 2, 1, bs2)]
            run_rr(batch_preps(0))
            for bi in range(NBAT):
                gl = []
                if self.gdn_stop >= 4:
                    gl += batch_chains(bi)
                if bi + 1 < NBAT:
                    gl += batch_preps(bi + 1)
                run_rr(gl)
            if self.dbg_gdn and h == 0:
                self.ld(self.dbg_o[:, :], oTh[:, :], r=[oTh.k(p_) for p_ in range(2, NP)], w=[])
            if h == 0: self.mark('gdn h0 onorm')
            for t0 in range(0, SEQ, 512):
                b = self.psbank()
                s_ = sqt[1]
                ptoks = [oTh.k(2 + (t0 // 128) + q) for q in range(4)]
                self.act(s_[:, :], oTh[:, t0:t0 + 512], AF.Square, r=ptoks, w=[s_])
                self.mm(self.PS[:, b, 0:512], self.ones[:, :], s_[:, :], True, True, r=[self.ones, s_], w=[self.PS.k(b)])
                self.rsqrt(rstd[:, :], self.PS[:, b, 0:512], 1.0 / 128, RMS_EPS, rstd, in_toks=[self.PS.k(b)])
                self.stt(s_[:, :], oTh[:, t0:t0 + 512], gon[:, 0:1], rstd[:, :], ALU.mult, ALU.mult, r=ptoks + [gon, rstd], w=[s_])
                gt_ = gtile[(t0 // 512) % 2]
                self.tt(gt_[:, :], s_[:, :], gate[:, t0:t0 + 512], ALU.mult, r=[s_, gate], w=[gt_])
                self.ld(self.gscr[h, :, t0:t0 + 512], gt_[:, :], r=[gt_], w=[self.gscr_tok])
            S.barrier()
        self.mark('gdn outproj')
        S.barrier()
        A.release(m0)
        gT = A.alloc("gd_gT", (8, SEQ), BF16)
        self.ld(gT[:, :, :], self.gscr.rearrange("h p t -> p h t"), r=[self.gscr_tok], w=[gT])
        self.ld(xT[:, :, :], self.xscr.rearrange("p (a t) -> p a t", a=8), r=[self.xscr_tok], w=[xT])
        S.barrier()
        groups = [(dc * 128, 128, [(0, 128, dc)]) for dc in range(8)]

        def evac_o(dc, t0, t1, ps, ptok):
            self.stt(xT[:, dc, t0:t1], ps, mods[:, 16 + dc, 0:1], xT[:, dc, t0:t1], ALU.mult, ALU.add, r=[ptok, mods, xT], w=[xT])

        self.linear(d["gdn_w_out"][j], D, groups, lambda kc, t0, t1: gT[:, kc, t0:t1], [gT], SEQ, evac_o)
        A.release(m0)
        self.layernorm(xT, SEQ, li, 0)

    def build(self):
        nc = self.nc
        es = self.es
        hc = host_consts()
        inp = self.inp
        inp("x", [SEQ, D]); inp("ctx", [CTX, D]); inp("ccol", [128, 8, 2])
        inp("mod_w", [DEPTH, D, 6 * D]); inp("ffn_w_in", [DEPTH, D, 2 * DFF]); inp("ffn_w_out", [DEPTH, DFF, D])
        inp("hy_w_in", [2, D, 3 * D]); inp("hy_w_out", [2, D, D])
        inp("hy_fw1", [2, 33, 64]); inp("hy_fw2", [2, 64, 64]); inp("hy_fw3", [2, 64, 64]); inp("hy_fw4", [2, 64, 2048])
        inp("modb", [128, DEPTH, 48]); inp("lng", [128, DEPTH * 16]); inp("lnb", [128, DEPTH * 16])
        inp("hbin", [128, 2, 24]); inp("hcw", [128, 2, 24, 3]); inp("hcb", [128, 2, 24]); inp("hbout", [128, 2, 8])
        inp("hyp", [64, 2, 4]); inp("hskip", [1, 2, 1024])
        for L in (SEQ, CTX):
            nt = L // 128
            TB = min(512, L)
            inp(f"fw{L}", [nt, 2, 128, nt, 128], BF16); inp(f"inv{L}", [L // TB, 2, 128, nt, TB], BF16)
            inp(f"z0T{L}", [33, L]); inp(f"win{L}", [L, D])
        inp("ident", [128, 128]); inp("identb", [128, 128], BF16); inp("ones", [128, 128])
        self.extra_inputs()
        self.out = nc.dram_tensor("out", [SEQ, D], F32, kind="ExternalOutput").ap()
        if self.debug_out:
            self.out_ctx = nc.dram_tensor("out_ctx", [CTX, D], F32, kind="ExternalOutput").ap()
        if self.dbg_gdn:
            self.dbg_o = nc.dram_tensor("dbg_o", [128, SEQ], F32, kind="ExternalOutput").ap()
            self.dbg_s = nc.dram_tensor("dbg_s", [2, 128, 128], F32, kind="ExternalOutput").ap()
            self.dbg_s0 = nc.dram_tensor("dbg_s0", [2, 128, 128], F32, kind="ExternalOutput").ap()
            self.dbg_ab = nc.dram_tensor("dbg_ab", [128, 18, 32], F32, kind="ExternalOutput").ap()
            self.dbg_pb = nc.dram_tensor("dbg_pb", [5, 128, 128], F32, kind="ExternalOutput").ap()
            self.dbg_sc = nc.dram_tensor("dbg_sc", [128, 8], F32, kind="ExternalOutput").ap()
        self.xscr = self.scr("xscr", [128, 8 * SEQ]); self.xscr_tok = Tok("xscr")
        self.hscr = {L: self.scr(f"hscr{L}", [2, L // 128, 128, D]) for L in (SEQ, CTX)}
        self.hscr_tok = {L: Tok(f"hscr{L}") for L in (SEQ, CTX)}
        self.x0scr = self.scr("x0scr", [8, 128, SEQ], BF16); self.x0scr_tok = Tok("x0scr")
        self.uscr = self.scr("uscr", [128, SEQ // 128, D], BF16); self.uscr_tok = Tok("uscr")
        self.extra_scratch()
        NA = 36600
        xt = es.enter_context(nc.sbuf_tensor("XT", [128, 8 * SEQ], F32))
        at = es.enter_context(nc.sbuf_tensor("ARENA", [128, NA], F32))
        pst = es.enter_context(nc.psum_tensor("PS", [128, 8, 512], F32))
        self.PS = Buf(pst, "PS")
        self.ps_rr = 0
        self.ps_pool = list(range(8))
        self.xT = Buf(xt[:, :].rearrange("p (a t) -> p a t", a=8), "xT")
        self.A2 = Arena(xt, 8 * SEQ)
        self.A = A = Arena(at, NA)
        self.S = S = Sched(nc)
        self.cT = A.alloc("cT", (8, CTX), F32)
        self.ident = A.alloc("ident", (128,), F32); self.identb = A.alloc("identb", (128,), BF16)
        self.ones = A.alloc("ones", (128,), F32)
        self.onesb = A.alloc("onesb", (128,), BF16)
        self.cs = A.alloc("cs", (8, 2), F32)
        self.mods = A.alloc("mods", (48, 2), F32)
        self.mods_alt = A.alloc("mods_alt", (48, 2), F32)
        self.mods_ready = None
        self.modb = A.alloc("modb", (DEPTH, 48), F32)
        self.lng = A.alloc("lng", (DEPTH * 16,), F32); self.lnb = A.alloc("lnb", (DEPTH * 16,), F32)
        self.hbin = A.alloc("hbin", (2, 24), F32); self.hcw = A.alloc("hcw", (2, 24, 3), F32)
        self.hcb = A.alloc("hcb", (2, 24), F32); self.hbout = A.alloc("hbout", (2, 8), F32)
        self.hyp = A.alloc("hyp", (2, 4), F32); self.hskip = A.alloc("hskip", (2, 1024), F32)
        d = self.din
        self.ld(self.ident[:, :], d["ident"][:, :], w=[self.ident]); self.ld(self.identb[:, :], d["identb"][:, :], w=[self.identb])
        self.ld(self.ones[:, :], d["ones"][:, :], w=[self.ones])
        self.ld(self.onesb[:, :], d["onesb"][:, :], w=[self.onesb])
        self.ld(self.cs[:, :, :], d["ccol"][:, :, :], w=[self.cs])
        self.ld(self.modb[:, :, :], d["modb"][:, :, :], w=[self.modb])
        self.ld(self.lng[:, :], d["lng"][:, :], w=[self.lng]); self.ld(self.lnb[:, :], d["lnb"][:, :], w=[self.lnb])
        self.ld(self.hbin[:, :, :], d["hbin"][:, :, :], w=[self.hbin]); self.ld(self.hcw[:, :, :, :], d["hcw"][:, :, :, :], w=[self.hcw])
        self.ld(self.hcb[:, :, :], d["hcb"][:, :, :], w=[self.hcb]); self.ld(self.hbout[:, :, :], d["hbout"][:, :, :], w=[self.hbout])
        self.ld(self.hyp[0:64, :, :], d["hyp"][:, :, :], w=[self.hyp]); self.ld(self.hskip[0:1, :, :], d["hskip"][:, :, :], w=[self.hskip])
        self.extra_persistent()
        self.act(self.cs[:, :, :], self.cs[:, :, :], AF.Silu, r=[self.cs], w=[self.cs])
        self.load_fm(d["x"], self.xT, SEQ)
        self.load_fm(d["ctx"], self.cT, CTX)
        S.barrier()
        for li in self.layers:
            self.layer(li)
        self.store_tm(self.xT, self.out, SEQ)
        if self.debug_out:
            self.store_tm(self.cT, self.out_ctx, CTX)
        S.barrier()
        S.emit()
        return nc

    def extra_inputs(self):
        inp = self.inp
        TT = CTX + SEQ
        inp("mla_w_in", [1, D, 704]); inp("mla_w_uq", [1, 384, 1536]); inp("mla_w_ukv", [1, 256, 2048]); inp("mla_w_out", [1, D, D])
        inp("mla_qg", [128, 1, 3]); inp("mla_kvg", [128, 1, 2])
        inp("gdn_w_in", [1, D, 4128]); inp("gdn_w_out", [1, D, D]); inp("gmask", [128, 16, 128]); inp("gblk", [128, 2])
        inp("gdn_cw", [128, 24, 3]); inp("gdn_ab", [16, 2]); inp("gdn_on", [128, 1])
        inp("ropeC", [64, TT]); inp("ropeS", [64, TT]); inp("ropeP", [64, 64]); inp("onesb", [128, 128], BF16)

    def extra_scratch(self):
        self.gscr = self.scr("gscr", [8, 128, SEQ], BF16)
        self.gscr_tok = Tok("gscr")

    def extra_persistent(self):
        pass

    def load_fm(self, src, dstT, T):
        A = self.A
        m0 = A.mark()
        st = [A.alloc(f"ldst{i}", (D,), F32) for i in range(2)]
        for tc in range(T // 128):
            s_ = st[tc % 2]
            self.ld(s_[:, :], src[tc * 128:(tc + 1) * 128, :], w=[s_])
            for half in range(2):
                b = self.psbank()
                for q in range(4):
                    dc = half * 4 + q
                    self.tr(self.PS[:, b, q * 128:(q + 1) * 128], s_[:, dc * 128:(dc + 1) * 128], self.ident[:, :],
                            r=[s_, self.ident], w=[self.PS.k(b)])
                src_ps = self.PS[:, b, :].rearrange("p (q t) -> p q t", q=4)
                if half:
                    self.cp(dstT[:, 4:8, tc * 128:(tc + 1) * 128], src_ps, r=[self.PS.k(b)], w=[dstT])
                else:
                    self.act(dstT[:, 0:4, tc * 128:(tc + 1) * 128], src_ps, AF.Copy, r=[self.PS.k(b)], w=[dstT])
        self.S.barrier()
        A.release(m0)

    def store_tm(self, srcT, dst, T):
        A = self.A
        m0 = A.mark()
        st = [A.alloc(f"stst{i}", (D,), F32) for i in range(2)]
        for tc in range(T // 128):
            s_ = st[tc % 2]
            for half in range(2):
                b = self.psbank()
                for q in range(4):
                    dc = half * 4 + q
                    self.tr(self.PS[:, b, q * 128:(q + 1) * 128], srcT[:, dc, tc * 128:(tc + 1) * 128], self.ident[:, :],
                            r=[srcT, self.ident], w=[self.PS.k(b)])
                if half:
                    self.cp(s_[:, 512:1024], self.PS[:, b, :], r=[self.PS.k(b)], w=[s_])
                else:
                    self.act(s_[:, 0:512], self.PS[:, b, :], AF.Copy, r=[self.PS.k(b)], w=[s_])
            self.ld(dst[tc * 128:(tc + 1) * 128, :], s_[:, :], r=[s_])
        self.S.barrier()
        A.release(m0)

    def layer(self, li):
        kind, j = li % 3, li // 3
        ctx_out = any(l % 3 != 0 for l in range(li + 1, DEPTH))
        self.mark(f"L{li} modulation")
        self.modulation(li)
        self.mark(f"L{li} mixer")
        if kind == 0:
            self.hyena(j, li, self.xT, SEQ, 0)
            if ctx_out:
                self.hyena(j, li, self.cT, CTX, 1)
        elif kind == 1:
            self.mla(j, li, ctx_out)
        else:
            self.gdn(j, li, ctx_out)
        self.mark(f"L{li} ffn")
        self.ffn(li, ctx_out)
        self.mark(f"L{li} end")


def prep_shared(inputs):
    f = lambda a: np.ascontiguousarray(np.asarray(a, np.float32))
    hc = host_consts()
    m = {}
    for k in ("mod_w", "ffn_w_in", "ffn_w_out", "hy_w_in", "hy_w_out", "hy_fw1", "hy_fw2", "hy_fw3", "hy_fw4"):
        m[k] = f(inputs[k])
    m["modb"] = f(np.stack([col_layout(inputs["mod_b"][i], 48) for i in range(DEPTH)], axis=1))
    m["lng"] = f(np.concatenate([col_layout(inputs["ln_g"][i, w], 8) for i in range(DEPTH) for w in range(2)], axis=1))
    m["lnb"] = f(np.concatenate([col_layout(inputs["ln_b"][i, w], 8) for i in range(DEPTH) for w in range(2)], axis=1))
    m["hbin"] = f(np.stack([col_layout(inputs["hy_b_in"][j], 24) for j in range(2)], axis=1))
    m["hcb"] = f(np.stack([col_layout(inputs["hy_conv_b"][j], 24) for j in range(2)], axis=1))
    m["hcw"] = f(np.stack([np.stack([col_layout(inputs["hy_conv_w"][j, k], 24) for k in range(3)], axis=-1) for j in range(2)], axis=1))
    m["hbout"] = f(np.stack([col_layout(inputs["hy_b_out"][j], 8) for j in range(2)], axis=1))
    m["hyp"] = f(np.stack([np.stack([inputs["hy_fb1"][j], inputs["hy_fb2"][j], inputs["hy_fb3"][j], inputs["hy_freq"][j]], axis=-1)
                           for j in range(2)], axis=1))
    m["hskip"] = f(np.asarray(inputs["hy_skip"])[None, :, :])
    m["gdn_cw"] = f(np.stack([col_layout(inputs["gdn_conv_w"][0, k], 24) for k in range(3)], axis=-1))
    m["gdn_ab"] = f(np.stack([np.asarray(inputs["gdn_a_log"][0]).reshape(16), np.asarray(inputs["gdn_dt_bias"][0]).reshape(16)], axis=-1))
    m["gdn_on"] = f(np.asarray(inputs["gdn_o_norm"][0]).reshape(128, 1))
    for k in ("mla_w_in", "mla_w_uq", "mla_w_ukv", "mla_w_out", "gdn_w_in", "gdn_w_out"):
        m[k] = f(inputs[k])
    m["mla_qg"] = f(np.stack([col_layout(inputs["mla_q_norm"][j], 3) for j in range(1)], axis=1))
    m["mla_kvg"] = f(np.stack([col_layout(inputs["mla_kv_norm"][j], 2) for j in range(1)], axis=1))
    for k, v in hc.items():
        m[k] = v
    return m


def prep_core(inputs, b):
    f = lambda a: np.ascontiguousarray(np.asarray(a, np.float32))
    ccol = np.stack([col_layout(inputs["c"][b], 8), col_layout(inputs["c_ctx"], 8)], axis=-1)
    return {"x": f(inputs["x"][b]), "ctx": f(inputs["ctx"][b]), "ccol": f(ccol)}


_PROG_CACHE = {}


def kernel(**inputs):
    if "prog" not in _PROG_CACHE:
        _PROG_CACHE["prog"] = Prog().build()
    nc = _PROG_CACHE["prog"]
    shared = prep_shared(inputs)
    in_maps = []
    for b in range(NCORES):
        m = dict(shared)
        m.update(prep_core(inputs, b))
        in_maps.append(m)
    res = run_bass_kernel_spmd(nc, in_maps, core_ids=list(range(NCORES)))
    return np.stack([np.asarray(r["out"], np.float32) for r in res.results], axis=0)
```

```python
import contextlib
import math
import numpy as np
import ml_dtypes
import concourse.bass as bass
import concourse.mybir as mybir
from concourse.bass_utils import run_bass_kernel_spmd

F32 = mybir.dt.float32
BF16 = mybir.dt.bfloat16
I32 = mybir.dt.int32
AF = mybir.ActivationFunctionType
ALU = mybir.AluOpType

D = 1024
SEQ = 2048
CTX = 256
DEPTH = 4
DFF = 2816
ALPHA = (2 * DEPTH) ** 0.25
LN_EPS = 1e-5
RMS_EPS = 1e-6
NCORES = 8

ENGS = ("pe", "act", "dve", "pool", "sp")


class Tok:
    __slots__ = ("name", "lw", "rd")

    def __init__(self, name=""):
        self.name = name
        self.lw = None
        self.rd = []


class Buf:
    def __init__(self, t, name):
        self.t = t
        self.name = name
        self.tok = Tok(name)
        self.sub = {}

    def __getitem__(self, idx):
        return self.t[idx]

    def k(self, i):
        if i not in self.sub:
            self.sub[i] = Tok(f"{self.name}.{i}")
        return self.sub[i]


def _tok(x):
    return x.tok if isinstance(x, Buf) else x


class Sched:
    def __init__(self, nc, n_dma_sems=24):
        self.nc = nc
        self.ops = {e: [] for e in ENGS}
        self.cnt = {e: 0 for e in ENGS}
        self.waited = {e: {} for e in ENGS}
        self.n_dma_sems = n_dma_sems
        self.dma_cnt = [0] * n_dma_sems
        self.dma_rr = 0

    def _add_waits(self, stream, deps, is_pe=False):
        w = self.waited[stream]
        best = {}
        for d in deps:
            if d is None:
                continue
            key, val = d
            if is_pe and key == "pe":
                continue
            if w.get(key, 0) >= val:
                continue
            w[key] = val
            best[key] = max(best.get(key, 0), val)
        return list(best.items())

    def op(self, stream, fn, r=(), w=()):
        r = [_tok(x) for x in r]
        w = [_tok(x) for x in w]
        deps = []
        for t in r:
            deps.append(t.lw)
        for t in w:
            deps.append(t.lw)
            deps.extend(t.rd)
        waits = self._add_waits(stream, deps, is_pe=(stream == "pe"))
        self.cnt[stream] += 1
        me = (stream, self.cnt[stream])
        self.ops[stream].append((waits, fn, (stream, 1)))
        for t in r:
            t.rd = [x for x in t.rd if x[0] != stream] + [me]
        for t in w:
            t.lw = me
            t.rd = []

    def dma(self, stream, fn, r=(), w=()):
        r = [_tok(x) for x in r]
        w = [_tok(x) for x in w]
        si = self.dma_rr
        self.dma_rr = (self.dma_rr + 1) % self.n_dma_sems
        key = ("dma", si)
        deps = []
        for t in r:
            deps.append(t.lw)
        for t in w:
            deps.append(t.lw)
            deps.extend(t.rd)
        if self.dma_cnt[si] > 0:
            deps.append((key, 16 * self.dma_cnt[si]))
        waits = self._add_waits(stream, deps)
        self.dma_cnt[si] += 1
        me = (key, 16 * self.dma_cnt[si])
        self.ops[stream].append((waits, fn, (key, 16)))
        for t in r:
            t.rd = t.rd + [me]
        for t in w:
            t.lw = me
            t.rd = []

    def barrier(self):
        for s in ENGS:
            deps = []
            for e in ("pe", "act", "dve", "pool"):
                if self.cnt[e] > 0:
                    deps.append((e, self.cnt[e]))
            for i in range(self.n_dma_sems):
                if self.dma_cnt[i] > 0:
                    deps.append((("dma", i), 16 * self.dma_cnt[i]))
            waits = self._add_waits(s, deps)
            if waits:
                self.ops[s].append((waits, None, None))

    def emit(self):
        nc = self.nc
        with contextlib.ExitStack() as es:
            sems = {}
            for e in ("pe", "act", "dve", "pool"):
                sems[e] = es.enter_context(nc.semaphore(f"s_{e}"))
            for i in range(self.n_dma_sems):
                sems[("dma", i)] = es.enter_context(nc.semaphore(f"s_dma{i}"))
            block = es.enter_context(nc.Block())

            def run(stream):
                def body(eng):
                    for waits, fn, inc in self.ops[stream]:
                        for k, v in waits:
                            eng.wait_ge(sems[k], v)
                        if fn is not None:
                            fn(eng).then_inc(sems[inc[0]], inc[1])
                return body

            block.tensor(run("pe"))
            block.scalar(run("act"))
            block.vector(run("dve"))
            block.gpsimd(run("pool"))
            block.sync(run("sp"))


_DT_SIZE = {F32: 4, BF16: 2, I32: 4}


class Arena:
    def __init__(self, t, nwords):
        self.t = t
        self.n = nwords
        self.top = 0
        self.peak = 0

    def alloc(self, name, free_shape, dt=F32):
        n = int(np.prod(free_shape))
        words = (n * _DT_SIZE[dt] + 3) // 4
        words = (words + 7) // 8 * 8
        off = self.top
        self.top += words
        self.peak = max(self.peak, self.top)
        assert self.top <= self.n, f"arena overflow allocating {name}: {self.top} > {self.n}"
        ap = self.t[:, off:off + words]
        if dt != F32:
            ap = ap.bitcast(dt)
        ap = ap[:, 0:n]
        if len(free_shape) == 2:
            ap = ap.rearrange("p (a b) -> p a b", a=free_shape[0])
        elif len(free_shape) == 3:
            ap = ap.rearrange("p (a b c) -> p a b c", a=free_shape[0], b=free_shape[1])
        return Buf(ap, name)

    def mark(self):
        return self.top

    def release(self, m):
        self.top = m


def _bf16(a):
    return np.ascontiguousarray(a.astype(np.float32)).astype(ml_dtypes.bfloat16)


def _dft_consts(L):
    nt = L // 128
    TB = min(512, L)
    ntb = L // TB
    f = np.arange(L, dtype=np.float64)
    t = np.arange(L, dtype=np.float64)
    om = np.pi * (2 * f + 1) / (2 * L)
    ang = np.outer(t, om)
    C = np.cos(ang)
    Sn = np.sin(ang)
    fw = np.zeros((nt, 2, 128, nt, 128), np.float32)
    for part, M in enumerate((C, Sn)):
        M4 = M.reshape(nt, 128, nt, 128)
        fw[:, part] = M4.transpose(2, 1, 0, 3)
    inv = np.zeros((ntb, 2, 128, nt, TB), np.float32)
    for part, M in enumerate((C / L, Sn / L)):
        M4 = M.T.reshape(nt, 128, ntb, TB)
        inv[:, part] = M4.transpose(2, 1, 0, 3)
    return _bf16(fw), _bf16(inv)


def _hyena_pos(L):
    t01 = np.linspace(0.0, 1.0, L)[:, None]
    bands = 16
    w = (2.0 * math.pi / L) * np.arange(L, dtype=np.float64)
    f = np.linspace(1e-4, bands - 1, bands)
    ang = w[:, None] * f[None, :]
    z = np.concatenate([t01, np.cos(ang), -np.sin(ang)], axis=-1)
    max_decay = math.log(1e-2) / 0.3
    min_decay = math.log(1e-2) / 1.5
    deltas = np.abs(np.linspace(min_decay, max_decay, D))
    tt = np.linspace(0.0, 1.0, L)
    win = np.exp(-tt[:, None] * deltas[None, :])
    return np.ascontiguousarray(z.T.astype(np.float32)), np.ascontiguousarray(win.astype(np.float32))


_CONST_CACHE = {}


def host_consts():
    if _CONST_CACHE:
        return _CONST_CACHE
    c = {}
    for L in (SEQ, CTX):
        fw, inv = _dft_consts(L)
        c[f"fw{L}"] = fw
        c[f"inv{L}"] = inv
        z, win = _hyena_pos(L)
        c[f"z0T{L}"] = z
        c[f"win{L}"] = win
    TT = CTX + SEQ
    l = np.arange(SEQ)
    row = (l // 64).astype(np.float64); colp = (l % 64).astype(np.float64)
    inv_freq = 10000.0 ** (-np.arange(16, dtype=np.float64) / 16)
    rc = np.ones((64, TT), np.float64); rs = np.zeros((64, TT), np.float64)
    ar = inv_freq[:, None] * row[None, :]; ac = inv_freq[:, None] * colp[None, :]
    rc[0:16, CTX:] = np.cos(ar); rc[16:32, CTX:] = np.cos(ar); rc[32:48, CTX:] = np.cos(ac); rc[48:64, CTX:] = np.cos(ac)
    rs[0:16, CTX:] = np.sin(ar); rs[16:32, CTX:] = np.sin(ar); rs[32:48, CTX:] = np.sin(ac); rs[48:64, CTX:] = np.sin(ac)
    c["ropeC"] = rc.astype(np.float32); c["ropeS"] = rs.astype(np.float32)
    Pm = np.zeros((64, 64), np.float32)
    for base in (0, 32):
        for i in range(16):
            Pm[base + i, base + 16 + i] = -1.0
            Pm[base + 16 + i, base + i] = 1.0
    c["ropeP"] = np.ascontiguousarray(Pm.T)
    c["onesb"] = _bf16(np.ones((128, 128)))
    jj = np.arange(128)[:, None]; cc_ = np.arange(128)[None, :]
    same = (jj // 64) == (cc_ // 64)
    mk = np.zeros((128, 16, 128), np.float32)
    mk[:, 0] = same & (jj <= cc_); mk[:, 1] = same & (jj >= cc_)
    mk[:, 2] = same & (jj > cc_); mk[:, 3] = same & (jj < cc_)
    mk[:, 4] = same & (jj >= cc_); mk[:, 5] = same & (jj <= cc_)
    mk[:, 6] = same & (jj > cc_); mk[:, 7] = same & (jj < cc_)
    mk[:, 8] = -mk[:, 0]; mk[:, 9] = -mk[:, 1]
    for li_, b_ in enumerate((1, 2, 4, 8, 16, 32)):
        mk[:, 10 + li_] = ((jj // (2 * b_)) == (cc_ // (2 * b_))) & ((jj // b_) != (cc_ // b_))
    c["gmask"] = mk
    bk = np.zeros((128, 2), np.float32); bk[:64, 0] = 1; bk[64:, 1] = 1
    c["gblk"] = bk
    c["ident"] = np.eye(128, dtype=np.float32)
    c["identb"] = _bf16(np.eye(128))
    c["ones"] = np.ones((128, 128), np.float32)
    _CONST_CACHE.update(c)
    return c


def col_layout(v, nchunks):
    return np.ascontiguousarray(np.asarray(v, np.float32).reshape(nchunks, 128).T)


class Prog:
    def __init__(self, n_layers=DEPTH, debug_out=None, layers=None, gdn_stop=9, gdn_dirs=(0, 1), dbg_gdn=False):
        self.gdn_dirs = gdn_dirs
        self.dma_alt = True
        self.dbg_gdn = dbg_gdn
        self.n_layers = n_layers
        self.layers = layers if layers is not None else list(range(n_layers))
        self.gdn_stop = gdn_stop
        self.debug_out = debug_out
        self.nc = bass.Bass("TRN2", target_bir_lowering=False)
        self.din = {}
        self.es = contextlib.ExitStack()

    def inp(self, name, shape, dt=F32):
        self.din[name] = self.nc.dram_tensor(name, list(shape), dt, kind="ExternalInput").ap()
        return self.din[name]

    def scr(self, name, shape, dt=F32):
        return self.nc.dram_tensor(name, list(shape), dt, kind="Internal").ap()

    def mark(self, name):
        if not hasattr(self, 'marks'):
            self.marks = []
        self.marks.append((name, self.S.cnt['dve']))

    def mm(self, out, lhsT, rhs, start, stop, r, w):
        self.S.op("pe", lambda e: e.matmul(out, lhsT, rhs, start=start, stop=stop), r=r, w=w)

    def tr(self, out, in_, ident, r, w):
        self.S.op("pe", lambda e: e.transpose(out, in_, ident), r=r, w=w)

    def act(self, out, in_, func, r, w, bias=None, scale=None, eng="act"):
        kw = {}
        if bias is not None:
            kw["bias"] = bias
        if scale is not None:
            kw["scale"] = scale
        self.S.op("act", lambda e: e.activation(out=out, in_=in_, func=func, **kw), r=r, w=w)

    def ts(self, out, in0, s1, s2, op0, op1, r, w, eng="dve"):
        if op1 is None:
            self.S.op(eng, lambda e: e.tensor_scalar(out, in0, s1, None, op0), r=r, w=w)
        else:
            self.S.op(eng, lambda e: e.tensor_scalar(out, in0, s1, s2, op0, op1), r=r, w=w)

    def tt(self, out, in0, in1, op, r, w, eng="dve"):
        self.S.op(eng, lambda e: e.tensor_tensor(out, in0, in1, op), r=r, w=w)

    def stt(self, out, in0, scalar, in1, op0, op1, r, w, eng="dve"):
        self.S.op(eng, lambda e: e.scalar_tensor_tensor(out, in0, scalar, in1, op0, op1), r=r, w=w)

    def cp(self, out, in_, r, w, eng="dve"):
        self.S.op(eng, lambda e: e.tensor_copy(out, in_), r=r, w=w)

    def ld(self, out, in_, r=(), w=(), q="sp"):
        if q == "sp" and self.dma_alt:
            self._ld_rr = getattr(self, "_ld_rr", 0) + 1
            if self._ld_rr % 2:
                q = "pool"
        self.S.dma(q, lambda e: e.dma_start(out=out, in_=in_), r=r, w=w)

    def rsqrt(self, out, in_, scale, eps, tok, in_toks=()):
        self.ts(out, in_, scale, eps, ALU.mult, ALU.add, r=[tok] + list(in_toks), w=[tok])
        self.act(out, out, AF.Ln, r=[tok], w=[tok])
        self.act(out, out, AF.Exp, r=[tok], w=[tok], scale=-0.5)

    def psbank(self):
        self.ps_rr = (self.ps_rr + 1) % len(self.ps_pool)
        return self.ps_pool[self.ps_rr]

    def alloc_wst(self, KC, width, dt=BF16, nstage=2):
        d = {"bufs": [self.A.alloc(f"wst{i}", (KC, width), dt) for i in range(2)], "rr": 0, "dt": dt}
        if dt == BF16:
            d["stage"] = [self.A.alloc(f"wstf{i}", (KC, width), F32) for i in range(nstage)]
        return d

    def load_w(self, wst, src, KC, width):
        i = wst["rr"] % 2
        wst["rr"] += 1
        ws = wst["bufs"][i]
        if wst["dt"] == BF16:
            st = wst["stage"][i % len(wst["stage"])]
            self.ld(st[:, 0:KC, 0:width], src, w=[st], q="sp")
            if wst["rr"] % 2:
                self.act(ws[:, 0:KC, 0:width], st[:, 0:KC, 0:width], AF.Copy, r=[st], w=[ws])
            else:
                self.cp(ws[:, 0:KC, 0:width], st[:, 0:KC, 0:width], r=[st], w=[ws])
        else:
            self.ld(ws[:, 0:KC, 0:width], src, w=[ws], q="sp")
        return ws

    def linear(self, W, K, groups, rhs_fn, rhs_toks, T, evac, tb=512, wst=None, hook=None):
        A, S = self.A, self.S
        KC = K // 128
        own = wst is None
        if own:
            m0 = A.mark()
            wst = self.alloc_wst(KC, max(g[1] for g in groups))
        for gi, (c0, width, subs) in enumerate(groups):
            src = W[:, c0:c0 + width].rearrange("(kc p) n -> p kc n", p=128)
            ws = self.load_w(wst, src, KC, width)
            for (off, m, tag) in subs:
                for t0 in range(0, T, tb):
                    t1 = min(T, t0 + tb)
                    b = self.psbank()
                    ps = self.PS[0:m, b, 0:t1 - t0]
                    for kc in range(KC):
                        self.mm(ps, ws[:, kc, off:off + m], rhs_fn(kc, t0, t1), kc == 0, kc == KC - 1,
                                r=[ws] + list(rhs_toks), w=[self.PS.k(b)])
                    evac(tag, t0, t1, ps, self.PS.k(b))
            if hook is not None:
                hook()
        if own:
            S.barrier()
            A.release(m0)

    def modulation_gen(self, li, mods, wst, part):
        HW = 1536
        k = 0
        for kc in range(8):
            for q in range(4):
                ws = wst[k % 2]
                k += 1
                self.ld(ws[:, :], self.din["mod_w"][li, kc * 128:(kc + 1) * 128, q * HW:(q + 1) * HW], w=[ws])
                b = self.psbank()
                for n in range(12):
                    self.mm(self.PS[:, b, 2 * n:2 * n + 2], ws[:, n * 128:(n + 1) * 128], self.cs[:, kc, :],
                            True, True, r=[ws, self.cs], w=[self.PS.k(b)])
                src = self.PS[:, b, 0:24].rearrange("p (n c) -> p n c", c=2)
                dst = mods[:, q * 12:(q + 1) * 12, :]
                if kc == 0:
                    self.cp(dst, src, r=[self.PS.k(b)], w=[mods])
                else:
                    self.tt(dst, dst, src, ALU.add, r=[self.PS.k(b), mods], w=[mods])
                yield
        for c in range(2):
            self.tt(mods[:, :, c], mods[:, :, c], self.modb[:, li, :], ALU.add, r=[mods, self.modb], w=[mods])
        for j in (1, 4):
            self.ts(mods[:, j * 8:(j + 1) * 8, :], mods[:, j * 8:(j + 1) * 8, :], 1.0, None, ALU.add, None,
                    r=[mods], w=[mods])
        yield

    def modulation(self, li):
        A, S = self.A, self.S
        if self.mods_ready == li:
            self.mods, self.mods_alt = self.mods_alt, self.mods
            self.mods_ready = None
            return
        m0 = A.mark()
        wst = [A.alloc(f"mw{i}", (1536,), F32) for i in range(2)]
        for _ in self.modulation_gen(li, self.mods, wst, None):
            pass
        S.barrier()
        A.release(m0)

    def modulate(self, hT, hoff, xT, T, jshift, col):
        mods = self.mods
        for dc in range(8):
            if dc % 2 == 0:
                self.act(hT[:, dc, hoff:hoff + T], xT[:, dc, 0:T], AF.Identity, r=[xT, mods], w=[hT],
                         bias=mods[:, jshift * 8 + dc, col:col + 1], scale=mods[:, (jshift + 1) * 8 + dc, col:col + 1])
            else:
                self.ts(hT[:, dc, hoff:hoff + T], xT[:, dc, 0:T], mods[:, (jshift + 1) * 8 + dc, col:col + 1],
                        mods[:, jshift * 8 + dc, col:col + 1], ALU.mult, ALU.add, r=[xT, mods], w=[hT])

    def prescale(self, xT, T, bg=None):
        for dc in range(8):
            if dc % 2:
                if bg is None:
                    self.ts(xT[:, dc, 0:T], xT[:, dc, 0:T], ALPHA, None, ALU.mult, None, r=[xT], w=[xT])
                else:
                    self.ts(xT[:, dc, 0:T], xT[:, dc, 0:T], ALPHA, bg[:, dc:dc + 1], ALU.mult, ALU.add, r=[xT, bg], w=[xT])
            else:
                if bg is None:
                    self.act(xT[:, dc, 0:T], xT[:, dc, 0:T], AF.Copy, r=[xT], w=[xT], scale=ALPHA)
                else:
                    self.act(xT[:, dc, 0:T], xT[:, dc, 0:T], AF.Identity, r=[xT, bg], w=[xT], scale=ALPHA, bias=bg[:, dc:dc + 1])

    def layernorm(self, xT, T, li, which):
        A, S = self.A, self.S
        self.mark(f'LN{T}')
        m0 = A.mark()
        TB = min(512, T)
        sq = [A.alloc(f"lnsq{i}", (TB,), BF16) for i in range(3)]
        xb = [A.alloc(f"lnxb{i}", (TB,), BF16) for i in range(3)]
        mean = A.alloc("lnmean", (T,), F32)
        rstd = A.alloc("lnrstd", (T,), F32)
        msq = A.alloc("lnmsq", (TB,), F32)
        tmp = [A.alloc(f"lntmp{i}", (TB,), F32) for i in range(3)]
        gcol = lambda dc: self.lng[:, (li * 2 + which) * 8 + dc:(li * 2 + which) * 8 + dc + 1]
        bcol = lambda dc: self.lnb[:, (li * 2 + which) * 8 + dc:(li * 2 + which) * 8 + dc + 1]
        nsq = 0
        for t0 in range(0, T, TB):
            bs = self.psbank()
            bq = self.psbank()
            for dc in range(8):
                x_ = xb[nsq % 3]
                s_ = sq[nsq % 3]
                nsq += 1
                self.cp(x_[:, :], xT[:, dc, t0:t0 + TB], r=[xT], w=[x_])
                self.act(s_[:, :], xT[:, dc, t0:t0 + TB], AF.Square, r=[xT], w=[s_])
                self.mm(self.PS[:, bs, 0:TB], self.onesb[:, :], x_[:, :], dc == 0, dc == 7,
                        r=[self.onesb, x_], w=[self.PS.k(bs)])
                self.mm(self.PS[:, bq, 0:TB], self.onesb[:, :], s_[:, :], dc == 0, dc == 7,
                        r=[self.onesb, s_], w=[self.PS.k(bq)])
            mt, rt = mean.k(t0), rstd.k(t0)
            self.ts(mean[:, t0:t0 + TB], self.PS[:, bs, 0:TB], 1.0 / D, None, ALU.mult, None, r=[self.PS.k(bs)], w=[mt])
            self.tt(msq[:, :], mean[:, t0:t0 + TB], mean[:, t0:t0 + TB], ALU.mult, r=[mt], w=[msq])
            self.stt(rstd[:, t0:t0 + TB], self.PS[:, bq, 0:TB], 1.0 / D, msq[:, :], ALU.mult, ALU.subtract,
                     r=[self.PS.k(bq), msq], w=[rt])
            self.rsqrt(rstd[:, t0:t0 + TB], rstd[:, t0:t0 + TB], 1.0, LN_EPS, rt)
        S.barrier()
        nt_ = 0
        for t0 in range(0, T, TB):
            mt, rt = mean.k(t0), rstd.k(t0)
            for dc in range(8):
                t_ = tmp[nt_ % 3]
                nt_ += 1
                xtok = Tok("lnx")
                self.tt(t_[:, :], xT[:, dc, t0:t0 + TB], mean[:, t0:t0 + TB], ALU.subtract, r=[xtok, mt], w=[t_])
                self.tt(t_[:, :], t_[:, :], rstd[:, t0:t0 + TB], ALU.mult, r=[t_, rt], w=[t_])
                self.act(xT[:, dc, t0:t0 + TB], t_[:, :], AF.Identity, r=[t_, self.lng, self.lnb], w=[xtok],
                         bias=bcol(dc), scale=gcol(dc))
        S.barrier()
        A.release(m0)

    def ffn(self, li, with_ctx):
        A, S = self.A, self.S
        mods = self.mods
        xT, cT = self.xT, self.cT
        T = SEQ + (CTX if with_ctx else 0)
        m0 = A.mark()
        hT = A.alloc("ffn_h", (8, T), BF16)
        self.modulate(hT, 0, xT, SEQ, 3, 0)
        self.prescale(xT, SEQ)
        if with_ctx:
            self.modulate(hT, SEQ, cT, CTX, 3, 1)
            self.prescale(cT, CTX)
        Win = self.din["ffn_w_in"][li]
        Wout = self.din["ffn_w_out"][li]
        aT = A.alloc("ffn_a", (6, T), BF16)
        gs = A.alloc("ffn_gs", (2, T), BF16)
        wst = self.alloc_wst(8, 256, nstage=1)
        woutb = A.alloc("ffn_wob", (6, D), BF16)
        wostg = [A.alloc(f"ffn_wos{i}", (D,), F32) for i in range(2)]
        nwo = 0
        hook = None
        nxt = self.layers[self.layers.index(li) + 1] if self.layers.index(li) + 1 < len(self.layers) else None
        if nxt is not None:
            mwst = [A.alloc(f"mw{i}", (1536,), F32) for i in range(2)]
            mgen = self.modulation_gen(nxt, self.mods_alt, mwst, None)

            def hook():
                for _ in range(2):
                    try:
                        next(mgen)
                    except StopIteration:
                        pass
        for (j0, nj) in ((0, 6), (6, 6), (12, 5), (17, 5)):
            groups = []
            for jj0 in range(0, nj, 2):
                n2 = min(2, nj - jj0)
                groups.append(((j0 + jj0) * 128, n2 * 128, [(q * 128, 128, ("g", jj0 + q, q)) for q in range(n2)]))
                groups.append((DFF + (j0 + jj0) * 128, n2 * 128, [(q * 128, 128, ("u", jj0 + q, q)) for q in range(n2)]))

            def evac1(tag, t0, t1, ps, ptok):
                kind, jl, q = tag
                if kind == "g":
                    self.act(gs[:, q, t0:t1], ps, AF.Silu, r=[ptok], w=[gs.k(q)])
                else:
                    self.tt(aT[:, jl, t0:t1], ps, gs[:, q, t0:t1], ALU.mult, r=[ptok, gs.k(q)], w=[aT])

            self.linear(Win, D, groups, lambda kc, t0, t1: hT[:, kc, t0:t1], [hT], T, evac1, wst=wst, hook=hook)
            for jl in range(nj):
                st = wostg[nwo % 2]
                nwo += 1
                self.ld(st[:, :], Wout[(j0 + jl) * 128:(j0 + jl + 1) * 128, :], w=[st], q="sp")
                if nwo % 2:
                    self.act(woutb[:, jl, :], st[:, :], AF.Copy, r=[st], w=[woutb])
                else:
                    self.cp(woutb[:, jl, :], st[:, :], r=[st], w=[woutb])
            for dc in range(8):
                for t0 in range(0, T, 512):
                    t1 = min(T, t0 + 512)
                    b = self.psbank()
                    ps = self.PS[:, b, 0:t1 - t0]
                    for jl in range(nj):
                        self.mm(ps, woutb[:, jl, dc * 128:(dc + 1) * 128], aT[:, jl, t0:t1], jl == 0, jl == nj - 1,
                                r=[woutb, aT], w=[self.PS.k(b)])
                    if t0 < SEQ:
                        self.stt(xT[:, dc, t0:t1], ps, mods[:, 5 * 8 + dc, 0:1], xT[:, dc, t0:t1],
                                 ALU.mult, ALU.add, r=[self.PS.k(b), mods, xT], w=[xT])
                    else:
                        self.stt(cT[:, dc, t0 - SEQ:t1 - SEQ], ps, mods[:, 5 * 8 + dc, 1:2], cT[:, dc, t0 - SEQ:t1 - SEQ],
                                 ALU.mult, ALU.add, r=[self.PS.k(b), mods, cT], w=[cT])
        if nxt is not None:
            for _ in mgen:
                pass
            self.mods_ready = nxt
        S.barrier()
        A.release(m0)
        self.layernorm(xT, SEQ, li, 1)
        if with_ctx:
            self.layernorm(cT, CTX, li, 1)

    def hyena_filter(self, j, L, big):
        A, S = self.A, self.S
        nt = L // 128
        TBK = min(512, L)
        m0 = A.mark()
        z = [A.alloc(f"hz{i}", (L,), F32) for i in range(2)]
        arg = A.alloc("harg", (TBK,), F32)
        ki = A.alloc("hki", (TBK,), I32)
        kf = A.alloc("hkf", (TBK,), F32)
        w1 = A.alloc("hw1", (64,), F32)
        w23 = A.alloc("hw23", (2, 64), F32)
        fw4 = A.alloc("hfw4", (2048,), F32)
        mb = big.mark()
        kp = big.alloc("hkp", (nt, 1024), BF16)
        km = big.alloc("hkm", (nt, 1024), BF16)
        self.ld(z[0][0:33, :], self.din[f"z0T{L}"][:, :], w=[z[0]])
        self.ld(w1[0:33, :], self.din["hy_fw1"][j], w=[w1])
        self.ld(w23[0:64, 0, :], self.din["hy_fw2"][j], w=[w23])
        self.ld(w23[0:64, 1, :], self.din["hy_fw3"][j], w=[w23])
        self.ld(fw4[0:64, :], self.din["hy_fw4"][j], w=[fw4])
        hyp = self.hyp
        cur = 0
        for layer in range(3):
            K = 33 if layer == 0 else 64
            wl = w1[0:33, :] if layer == 0 else w23[0:64, layer - 1, :]
            wtok = w1 if layer == 0 else w23
            zin, zout = z[cur], z[1 - cur]
            for t0 in range(0, L, TBK):
                b = self.psbank()
                ps = self.PS[0:64, b, 0:TBK]
                self.mm(ps, wl, zin[0:K, t0:t0 + TBK], True, True, r=[wtok, zin], w=[self.PS.k(b)])
                self.ts(arg[0:64, :], ps, hyp[0:64, j, layer:layer + 1], hyp[0:64, j, 3:4], ALU.add, ALU.mult,
                        r=[self.PS.k(b), hyp], w=[arg])
                self.ts(ki[0:64, :], arg[0:64, :], 1.0 / (2 * math.pi), None, ALU.mult, None, r=[arg], w=[ki])
                self.cp(kf[0:64, :], ki[0:64, :], r=[ki], w=[kf])
                self.stt(arg[0:64, :], kf[0:64, :], -2.0 * math.pi, arg[0:64, :], ALU.mult, ALU.add, r=[kf, arg], w=[arg])
                self.ts(arg[0:64, :], arg[0:64, :], 3.141592, -3.141592, ALU.min, ALU.max, r=[arg], w=[arg])
                self.act(zout[0:64, t0:t0 + TBK], arg[0:64, :], AF.Sin, r=[arg], w=[zout])
            cur = 1 - cur
        z3 = z[cur]
        z3b = A.alloc("hz3b", (L,), BF16)
        fw4b = A.alloc("hfw4b", (2048,), BF16)
        self.cp(z3b[0:64, :], z3[0:64, :], r=[z3], w=[z3b])
        self.act(fw4b[0:64, :], fw4[0:64, :], AF.Copy, r=[fw4], w=[fw4b])
        win = [A.alloc(f"hwin{i}", (1024,), F32) for i in range(2)]
        kfw = A.alloc("hkfw", (1024,), F32)
        kbw = A.alloc("hkbw", (1024,), F32)
        for tc in range(nt):
            wn = win[tc % 2]
            self.ld(wn[:, :], self.din[f"win{L}"][tc * 128:(tc + 1) * 128, :], w=[wn])
            for cb in range(4):
                b = self.psbank()
                ps = self.PS[:, b, 0:512]
                self.mm(ps, z3b[0:64, tc * 128:(tc + 1) * 128], fw4b[0:64, cb * 512:(cb + 1) * 512], True, True,
                        r=[z3b, fw4b], w=[self.PS.k(b)])
                dst = kfw if cb < 2 else kbw
                c0 = (cb % 2) * 512
                self.tt(dst[:, c0:c0 + 512], ps, wn[:, c0:c0 + 512], ALU.mult, r=[self.PS.k(b), wn], w=[dst])
            if tc == 0:
                self.tt(kfw[0:1, :], kfw[0:1, :], self.hskip[0:1, j, :], ALU.add, r=[kfw, self.hskip], w=[kfw])
            self.tt(kp[:, tc, :], kfw[:, :], kbw[:, :], ALU.add, r=[kfw, kbw], w=[kp])
            self.tt(km[:, tc, :], kfw[:, :], kbw[:, :], ALU.subtract, r=[kfw, kbw], w=[km])
        Hs = self.hscr[L]
        fwp = [A.alloc(f"hfwp{i}", (2, nt, 128), BF16) for i in range(2)]
        ho = [A.alloc(f"hho{i}", (512,), F32) for i in range(4)]
        oi = 0
        for fc in range(nt):
            fp = fwp[fc % 2]
            self.ld(fp[:, :, :, :], self.din[f"fw{L}"][fc].rearrange("a p t f -> p a t f"), w=[fp])
            for part in range(2):
                src = kp if part == 0 else km
                for cb in range(2):
                    b = self.psbank()
                    ps = self.PS[:, b, 0:512]
                    for tc in range(nt):
                        self.mm(ps, fp[:, part, tc, :], src[:, tc, cb * 512:(cb + 1) * 512], tc == 0, tc == nt - 1,
                                r=[fp, src], w=[self.PS.k(b)])
                    o = ho[oi % 4]
                    oi += 1
                    if oi % 2:
                        self.act(o[:, :], ps, AF.Copy, r=[self.PS.k(b)], w=[o])
                    else:
                        self.cp(o[:, :], ps, r=[self.PS.k(b)], w=[o])
                    self.ld(Hs[part, fc, :, cb * 512:(cb + 1) * 512], o[:, :], r=[o], w=[self.hscr_tok[L]])
        S.barrier()
        A.release(m0)
        big.release(mb)

    def hyena(self, j, li, xT, L, col):
        A, S = self.A, self.S
        mods = self.mods
        nt = L // 128
        TB = min(512, L)
        ntb = L // TB
        m0 = A.mark()
        hT = A.alloc("hy_h", (8, L), BF16)
        self.modulate(hT, 0, xT, L, 0, col)
        bg = A.alloc("hy_bg", (8,), F32)
        self.tt(bg[:, :], self.hbout[:, j, :], mods[:, 16:24, col], ALU.mult, r=[self.hbout, mods], w=[bg])
        self.prescale(xT, L, bg)
        spill = (L == SEQ)
        if spill:
            self.ld(self.xscr.rearrange("p (a t) -> p a t", a=8), xT[:, :, :], r=[xT], w=[self.xscr_tok])
            big = self.A2
            big.top = 0
        else:
            big = A
        S.barrier()
        self.mark(f'hy{L} filter')
        self.hyena_filter(j, L, big)
        self.mark(f'hy{L} P1')
        m2 = A.mark()
        zp = [A.alloc(f"hy_zp{i}", (L + 2,), F32) for i in range(3)]
        acc = [A.alloc(f"hy_acc{i}", (L,), F32) for i in range(2)]
        x0b = A.alloc("hy_x0b", (L,), BF16)
        ub = A.alloc("hy_ub", (L,), BF16)
        ust = A.alloc("hy_ust", (nt, 128), BF16)
        wst1 = self.alloc_wst(8, 128)
        for i in range(3):
            self.S.op("pool", lambda e, i=i: e.memset(zp[i][:, 0:1], 0.0), w=[zp[i]])
            self.S.op("pool", lambda e, i=i: e.memset(zp[i][:, L + 1:L + 2], 0.0), w=[zp[i]])
        Win = self.din["hy_w_in"][j]
        for cc in range(8):
            groups = [(part * 1024 + cc * 128, 128, [(0, 128, part)]) for part in range(3)]

            def evac(part, t0, t1, ps, ptok, cc=cc):
                self.act(zp[part][:, 1 + t0:1 + t1], ps, AF.Identity, r=[ptok, self.hbin], w=[zp[part]],
                         bias=self.hbin[:, j, part * 8 + cc:part * 8 + cc + 1])

            self.linear(Win, D, groups, lambda kc, t0, t1: hT[:, kc, t0:t1], [hT], L, evac, wst=wst1)
            res = []
            for part in range(3):
                q = part * 8 + cc
                a = acc[part % 2] if part < 2 else acc[0]
                eng = "dve"
                if part == 2:
                    pass
                cw = lambda kk, q=q: self.hcw[:, j, q, kk:kk + 1]
                self.ts(a[:, :], zp[part][:, 0:L], cw(0), self.hcb[:, j, q:q + 1], ALU.mult, ALU.add,
                        r=[zp[part], self.hcw, self.hcb], w=[a], eng=eng)
                self.stt(a[:, :], zp[part][:, 1:L + 1], cw(1), a[:, :], ALU.mult, ALU.add, r=[zp[part], self.hcw, a], w=[a], eng=eng)
                self.stt(a[:, :], zp[part][:, 2:L + 2], cw(2), a[:, :], ALU.mult, ALU.add, r=[zp[part], self.hcw, a], w=[a], eng=eng)
                if part == 0:
                    self.act(x0b[:, :], a[:, :], AF.Copy, r=[a], w=[x0b])
                    self.ld(self.x0scr[cc, :, 0:L], x0b[:, :], r=[x0b], w=[self.x0scr_tok])
                if part == 2:
                    self.tt(ub[:, :], acc[1][:, :], acc[0][:, :], ALU.mult, r=[acc[0], acc[1]], w=[ub])
            for tc in range(nt):
                b = self.psbank()
                psb = self.PS[:, b, 0:64].bitcast(BF16)
                self.tr(psb, ub[:, tc * 128:(tc + 1) * 128], self.identb[:, :], r=[ub, self.identb], w=[self.PS.k(b)])
                if tc % 2:
                    self.cp(ust[:, tc, :], psb, r=[self.PS.k(b)], w=[ust])
                else:
                    self.act(ust[:, tc, :], psb, AF.Copy, r=[self.PS.k(b)], w=[ust])
            self.ld(self.uscr[:, 0:nt, cc * 128:(cc + 1) * 128], ust[:, :, :], r=[ust], w=[self.uscr_tok])
        S.barrier()
        A.release(m2)
        m2 = A.mark()
        self.mark(f'hy{L} P2P3')
        gT = hT
        mb = big.mark()
        utm = big.alloc("hy_utm", (nt, 512), BF16)
        Ya = big.alloc("hy_ya", (nt, 512), BF16)
        Yb = big.alloc("hy_yb", (nt, 512), BF16)
        x0g = big.alloc("hy_x0g", (4, L), BF16)
        fwp = [A.alloc(f"hy_fwp{i}", (2, nt, 128), BF16) for i in range(2)]
        hsl = [A.alloc(f"hy_hsl{i}", (2, 512), F32) for i in range(2)]
        ucs = [A.alloc(f"hy_ucs{i}", (2, 512), F32) for i in range(2)]
        tq = [A.alloc(f"hy_tq{i}", (512,), F32) for i in range(4)]
        TBI = min(256, L)
        invp2 = [A.alloc(f"hy_invp{i}", (2, nt, TBI), BF16) for i in range(2)]
        ninv = 0
        Hs = self.hscr[L]
        for g in range(2):
            self.ld(utm[:, :, :], self.uscr[:, 0:nt, g * 512:(g + 1) * 512], r=[self.uscr_tok], w=[utm])
            self.ld(x0g[:, :, :], self.x0scr[g * 4:(g + 1) * 4, :, 0:L].rearrange("c p t -> p c t"), r=[self.x0scr_tok], w=[x0g])
            for fc in range(nt):
                fp = fwp[fc % 2]
                hs = hsl[fc % 2]
                uc = ucs[fc % 2]
                self.ld(fp[:, :, :, :], self.din[f"fw{L}"][fc].rearrange("a p t f -> p a t f"), w=[fp])
                self.ld(hs[:, :, :], Hs[:, fc, :, g * 512:(g + 1) * 512].rearrange("a p c -> p a c"), r=[self.hscr_tok[L]], w=[hs])
                for part in range(2):
                    b = self.psbank()
                    ps = self.PS[:, b, 0:512]
                    for tc in range(nt):
                        self.mm(ps, fp[:, part, tc, :], utm[:, tc, :], tc == 0, tc == nt - 1, r=[fp, utm], w=[self.PS.k(b)])
                    self.act(uc[:, part, :], ps, AF.Copy, r=[self.PS.k(b)], w=[uc.k(part)])
                self.tt(tq[0][:, :], uc[:, 0, :], hs[:, 0, :], ALU.mult, r=[uc.k(0), hs], w=[tq[0]])
                self.tt(tq[1][:, :], uc[:, 1, :], hs[:, 1, :], ALU.mult, r=[uc.k(1), hs], w=[tq[1]])
                self.tt(Ya[:, fc, :], tq[0][:, :], tq[1][:, :], ALU.subtract, r=[tq[0], tq[1]], w=[Ya])
                self.tt(tq[2][:, :], uc[:, 0, :], hs[:, 1, :], ALU.mult, r=[uc.k(0), hs], w=[tq[2]])
                self.tt(tq[3][:, :], uc[:, 1, :], hs[:, 0, :], ALU.mult, r=[uc.k(1), hs], w=[tq[3]])
                self.tt(Yb[:, fc, :], tq[2][:, :], tq[3][:, :], ALU.add, r=[tq[2], tq[3]], w=[Yb])
            for tb_ in range(ntb):
                for hh in range(TB // TBI):
                    ip = invp2[ninv % 2]
                    ninv += 1
                    t0 = tb_ * TB + hh * TBI
                    for a_ in range(2):
                        self.ld(ip[:, a_, :, :], self.din[f"inv{L}"][tb_][a_][:, :, hh * TBI:(hh + 1) * TBI], w=[ip])
                    for cc in range(4):
                        b = self.psbank()
                        ps = self.PS[:, b, 0:TBI]
                        n = 0
                        for part, Y in enumerate((Ya, Yb)):
                            for fc in range(nt):
                                self.mm(ps, Y[:, fc, cc * 128:(cc + 1) * 128], ip[:, part, fc, :], n == 0, n == 2 * nt - 1,
                                        r=[Y, ip], w=[self.PS.k(b)])
                                n += 1
                        self.tt(gT[:, g * 4 + cc, t0:t0 + TBI], ps, x0g[:, cc, t0:t0 + TBI], ALU.mult,
                                r=[self.PS.k(b), x0g], w=[gT])
            S.barrier()
        self.mark(f'hy{L} P4')
        A.release(m2)
        big.release(mb)
        if spill:
            self.ld(xT[:, :, :], self.xscr.rearrange("p (a t) -> p a t", a=8), r=[self.xscr_tok], w=[xT])
        S.barrier()
        groups = [(dc * 128, 128, [(0, 128, dc)]) for dc in range(8)]

        def evac4(dc, t0, t1, ps, ptok):
            self.stt(xT[:, dc, t0:t1], ps, mods[:, 16 + dc, col:col + 1], xT[:, dc, t0:t1], ALU.mult, ALU.add,
                     r=[ptok, mods, xT], w=[xT])

        self.linear(self.din["hy_w_out"][j], D, groups, lambda kc, t0, t1: gT[:, kc, t0:t1], [gT], L, evac4)
        A.release(m0)
        self.layernorm(xT, L, li, 0)


    def mla(self, j, li, ctx_out):
        A, S, A2 = self.A, self.S, self.A2
        mods = self.mods
        d = self.din
        TT = CTX + SEQ
        xT, cT = self.xT, self.cT
        SCALE = 192.0 ** -0.5
        m0 = A.mark()
        hT = A.alloc("ml_h", (8, TT), BF16)
        self.modulate(hT, 0, cT, CTX, 0, 1)
        self.modulate(hT, CTX, xT, SEQ, 0, 0)
        self.prescale(xT, SEQ)
        self.prescale(cT, CTX)
        self.ld(self.xscr.rearrange("p (a t) -> p a t", a=8), xT[:, :, :], r=[xT], w=[self.xscr_tok])
        S.barrier()
        A2.top = 0
        cqn = A.alloc("ml_cqn", (3, TT), BF16)
        ckvn = A.alloc("ml_ckvn", (2, TT), BF16)
        krT = A.alloc("ml_krT", (TT,), BF16)
        ropeC = A.alloc("ml_ropeC", (TT,), F32)
        ropeS = A.alloc("ml_ropeS", (TT,), F32)
        ropeP = A.alloc("ml_ropeP", (64,), F32)
        onesb = A.alloc("ml_onesb", (128,), BF16)
        self.ld(ropeC[0:64, :], d["ropeC"][:, :], w=[ropeC]); self.ld(ropeS[0:64, :], d["ropeS"][:, :], w=[ropeS])
        self.ld(ropeP[0:64, :], d["ropeP"][:, :], w=[ropeP]); self.ld(onesb[:, :], d["onesb"][:, :], w=[onesb])
        qg = A.alloc("ml_qg", (3,), F32); kvg = A.alloc("ml_kvg", (2,), F32)
        self.ld(qg[:, :], d["mla_qg"][:, j, :], w=[qg]); self.ld(kvg[:, :], d["mla_kvg"][:, j, :], w=[kvg])
        ma = A2.mark()
        craw = A2.alloc("ml_craw", (5, TT), F32)
        kraw = A.alloc("ml_kraw", (TT,), F32)
        groups = [(0, 256, [(0, 128, 0), (128, 128, 1)]), (256, 256, [(0, 128, 2), (128, 128, 3)]), (512, 192, [(0, 128, 4), (128, 64, 5)])]

        def evac_in(ch, t0, t1, ps, ptok):
            if ch < 5:
                if ch % 2:
                    self.act(craw[:, ch, t0:t1], ps, AF.Copy, r=[ptok], w=[craw.k(ch)])
                else:
                    self.cp(craw[:, ch, t0:t1], ps, r=[ptok], w=[craw.k(ch)])
            else:
                self.cp(kraw[0:64, t0:t1], ps, r=[ptok], w=[kraw])

        self.linear(d["mla_w_in"][j], D, groups, lambda kc, t0, t1: hT[:, kc, t0:t1], [hT], TT, evac_in)
        self.mark('mla norms')
        m1 = A.mark()
        sq = [A.alloc(f"ml_sq{i}", (512,), F32) for i in range(2)]
        rstd = A.alloc("ml_rstd", (512,), F32)
        for (c0, nch, dst, g) in ((0, 3, cqn, qg), (3, 2, ckvn, kvg)):
            for t0 in range(0, TT, 512):
                t1 = min(TT, t0 + 512)
                n = t1 - t0
                b = self.psbank()
                for ch in range(nch):
                    s_ = sq[ch % 2]
                    self.act(s_[:, 0:n], craw[:, c0 + ch, t0:t1], AF.Square, r=[craw.k(c0 + ch)], w=[s_])
                    self.mm(self.PS[:, b, 0:n], self.ones[:, :], s_[:, 0:n], ch == 0, ch == nch - 1, r=[self.ones, s_], w=[self.PS.k(b)])
                self.rsqrt(rstd[:, 0:n], self.PS[:, b, 0:n], 1.0 / (128 * nch), RMS_EPS, rstd, in_toks=[self.PS.k(b)])
                for ch in range(nch):
                    self.stt(dst[:, ch, t0:t1], craw[:, c0 + ch, t0:t1], g[:, ch:ch + 1], rstd[:, 0:n], ALU.mult, ALU.mult,
                             r=[craw.k(c0 + ch), g, rstd], w=[dst])
        rt = [A.alloc(f"ml_rt{i}", (512,), F32) for i in range(3)]

        def rope(dst, src, src_tok):
            for t0 in range(0, TT, 512):
                t1 = min(TT, t0 + 512)
                n = t1 - t0
                b = self.psbank()
                self.mm(self.PS[0:64, b, 0:n], ropeP[0:64, :], src[0:64, t0:t1], True, True, r=[ropeP, src_tok], w=[self.PS.k(b)])
                self.tt(rt[0][0:64, 0:n], src[0:64, t0:t1], ropeC[0:64, t0:t1], ALU.mult, r=[src_tok, ropeC], w=[rt[0]])
                self.tt(rt[1][0:64, 0:n], self.PS[0:64, b, 0:n], ropeS[0:64, t0:t1], ALU.mult, r=[self.PS.k(b), ropeS], w=[rt[1]])
                self.tt(dst[0:64, t0:t1], rt[0][0:64, 0:n], rt[1][0:64, 0:n], ALU.add, r=[rt[0], rt[1]], w=[dst])

        rope(krT, kraw, kraw)
        S.barrier()
        A2.release(ma)
        self.mark('mla heads')
        oT = hT
        qn = A2.alloc("ml_qn", (TT,), BF16)
        qr = A2.alloc("ml_qr", (TT,), BF16)
        qraw = A2.alloc("ml_qraw", (TT,), F32)
        kn = A2.alloc("ml_kn", (TT,), BF16)
        vtm = A2.alloc("ml_vtm", (18, 128), BF16)
        wq = [A2.alloc(f"ml_wq{i}", (3, 192), BF16) for i in range(2)]
        wkv = [A2.alloc(f"ml_wkv{i}", (2, 256), BF16) for i in range(2)]
        pT = [A2.alloc(f"ml_pT{i}", (512,), BF16) for i in range(3)]
        rl = A2.alloc("ml_rl", (512,), F32)
        for h in range(8):
            wq_, wkv_ = wq[h % 2], wkv[h % 2]
            self.ld(wq_[:, :, :], d["mla_w_uq"][j][:, h * 192:(h + 1) * 192].rearrange("(kc p) n -> p kc n", p=128), w=[wq_], q="pool")
            self.ld(wkv_[:, :, :], d["mla_w_ukv"][j][:, h * 256:(h + 1) * 256].rearrange("(kc p) n -> p kc n", p=128), w=[wkv_], q="pool")
            for t0 in range(0, TT, 512):
                t1 = min(TT, t0 + 512)
                n = t1 - t0
                b = self.psbank()
                for kc in range(3):
                    self.mm(self.PS[:, b, 0:n], wq_[:, kc, 0:128], cqn[:, kc, t0:t1], kc == 0, kc == 2, r=[wq_, cqn], w=[self.PS.k(b)])
                self.act(qn[:, t0:t1], self.PS[:, b, 0:n], AF.Copy, r=[self.PS.k(b)], w=[qn])
                b = self.psbank()
                for kc in range(3):
                    self.mm(self.PS[0:64, b, 0:n], wq_[:, kc, 128:192], cqn[:, kc, t0:t1], kc == 0, kc == 2, r=[wq_, cqn], w=[self.PS.k(b)])
                self.cp(qraw[0:64, t0:t1], self.PS[0:64, b, 0:n], r=[self.PS.k(b)], w=[qraw])
                b = self.psbank()
                for kc in range(2):
                    self.mm(self.PS[:, b, 0:n], wkv_[:, kc, 0:128], ckvn[:, kc, t0:t1], kc == 0, kc == 1, r=[wkv_, ckvn], w=[self.PS.k(b)])
                self.act(kn[:, t0:t1], self.PS[:, b, 0:n], AF.Copy, r=[self.PS.k(b)], w=[kn])
            rope(qr, qraw, qraw)
            for tc in range(18):
                b = self.psbank()
                for kc in range(2):
                    self.mm(self.PS[:, b, 0:128], ckvn[:, kc, tc * 128:(tc + 1) * 128], wkv_[:, kc, 128:256], kc == 0, kc == 1,
                            r=[wkv_, ckvn], w=[self.PS.k(b)])
                self.cp(vtm[:, tc, :], self.PS[:, b, 0:128], r=[self.PS.k(b)], w=[vtm])
            blocks = [(CTX + qb * 512, 512, 18) for qb in range(4)]
            if ctx_out:
                blocks.append((0, CTX, 2))
            self.ps_pool = [0, 1, 2, 3]
            for bi, (q0, nq, nk) in enumerate(blocks):
                bo = 4 + (bi % 2)
                bl = 6 + (bi % 2)
                def st_exp(kc_):
                    b = self.psbank()
                    ps = self.PS[:, b, 0:nq]
                    self.mm(ps, kn[:, kc_ * 128:(kc_ + 1) * 128], qn[:, q0:q0 + nq], True, False, r=[kn, qn], w=[self.PS.k(b)])
                    self.mm(ps, krT[0:64, kc_ * 128:(kc_ + 1) * 128], qr[0:64, q0:q0 + nq], False, True, r=[krT, qr], w=[self.PS.k(b)])
                    p_ = pT[kc_ % 3]
                    self.act(p_[:, 0:nq], ps, AF.Exp, r=[self.PS.k(b)], w=[p_], scale=SCALE)

                st_exp(0)
                for kc_ in range(nk):
                    if kc_ + 1 < nk:
                        st_exp(kc_ + 1)
                    p_ = pT[kc_ % 3]
                    self.mm(self.PS[:, bo, 0:nq], vtm[:, kc_, :], p_[:, 0:nq], kc_ == 0, kc_ == nk - 1, r=[vtm, p_], w=[self.PS.k(bo)])
                    self.mm(self.PS[:, bl, 0:nq], onesb[:, :], p_[:, 0:nq], kc_ == 0, kc_ == nk - 1, r=[onesb, p_], w=[self.PS.k(bl)])
                self.act(rl[:, 0:nq], self.PS[:, bl, 0:nq], AF.Ln, r=[self.PS.k(bl)], w=[rl])
                self.act(rl[:, 0:nq], rl[:, 0:nq], AF.Exp, r=[rl], w=[rl], scale=-1.0)
                self.tt(oT[:, h, q0:q0 + nq], self.PS[:, bo, 0:nq], rl[:, 0:nq], ALU.mult, r=[self.PS.k(bo), rl], w=[oT])
            self.ps_pool = list(range(8))
        S.barrier()
        A.release(m1)
        self.mark('mla outproj')
        self.ld(xT[:, :, :], self.xscr.rearrange("p (a t) -> p a t", a=8), r=[self.xscr_tok], w=[xT])
        S.barrier()
        groups = [(dc * 128, 128, [(0, 128, dc)]) for dc in range(8)]

        def evac_o(dc, t0, t1, ps, ptok):
            if t0 < CTX:
                if ctx_out:
                    self.stt(cT[:, dc, t0:t1], ps, mods[:, 16 + dc, 1:2], cT[:, dc, t0:t1], ALU.mult, ALU.add, r=[ptok, mods, cT], w=[cT])
            else:
                self.stt(xT[:, dc, t0 - CTX:t1 - CTX], ps, mods[:, 16 + dc, 0:1], xT[:, dc, t0 - CTX:t1 - CTX], ALU.mult, ALU.add,
                         r=[ptok, mods, xT], w=[xT])

        self.linear(d["mla_w_out"][j], D, groups, lambda kc, t0, t1: oT[:, kc, t0:t1], [oT], TT, evac_o, tb=256)
        A.release(m0)
        self.layernorm(xT, SEQ, li, 0)
        if ctx_out:
            self.layernorm(cT, CTX, li, 0)


    def gdn(self, j, li, ctx_out):
        assert not ctx_out
        A, S, A2 = self.A, self.S, self.A2
        mods = self.mods
        d = self.din
        TT = CTX + SEQ
        NP = TT // 128
        xT, cT = self.xT, self.cT
        m0 = A.mark()
        hT = A.alloc("gd_h", (8, TT), BF16)
        self.modulate(hT, 0, cT, CTX, 0, 1)
        self.modulate(hT, CTX, xT, SEQ, 0, 0)
        self.prescale(xT, SEQ)
        self.ld(self.xscr.rearrange("p (a t) -> p a t", a=8), xT[:, :, :], r=[xT], w=[self.xscr_tok])
        S.barrier()
        A2.top = 0
        gtile = [A.alloc(f"gd_gtile{i}", (512,), BF16) for i in range(2)]
        abtm = A.alloc("gd_abtm", (NP, 32), F32)
        msk = A.alloc("gd_msk", (16, 128), F32)
        blk = A.alloc("gd_blk", (2,), F32)
        gcw = A.alloc("gd_cw", (24, 3), F32)
        gab = A.alloc("gd_ab", (2,), F32)
        gon = A.alloc("gd_on", (1,), F32)
        self.ld(msk[:, :, :], d["gmask"][:, :, :], w=[msk]); self.ld(blk[:, :], d["gblk"][:, :], w=[blk])
        self.ld(gcw[:, :, :], d["gdn_cw"][:, :, :], w=[gcw]); self.ld(gab[0:16, :], d["gdn_ab"][:, :], w=[gab])
        self.ld(gon[:, :], d["gdn_on"][:, :], w=[gon])
        TRI = lambda dr: msk[:, 0 + dr, :]
        REM = lambda dr: msk[:, 2 + dr, :]
        INCL = lambda dr: msk[:, 4 + dr, :]
        STRICT = lambda dr: msk[:, 6 + dr, :]
        NTRI = lambda dr: msk[:, 8 + dr, :]
        Win = d["gdn_w_in"][j]
        tcol = lambda t: (1 + t) if t < CTX else (3 + t)
        m1 = A.mark()
        abr = A.alloc("gd_abr", (2, TT), F32)
        nal = A.alloc("gd_nal", (1,), F32)
        self.act(nal[0:16, :], gab[0:16, 0:1], AF.Exp, r=[gab], w=[nal])
        self.ts(nal[0:16, :], nal[0:16, :], -1.0, None, ALU.mult, None, r=[nal], w=[nal])
        groups = [(4096, 32, [(0, 16, 0), (16, 16, 1)])]

        def evac_ab(which, t0, t1, ps, ptok):
            if which == 0:
                self.act(abr[0:16, 0, t0:t1], ps, AF.Exp, r=[ptok, gab], w=[abr.k(0)], bias=gab[0:16, 1:2])
                self.ts(abr[0:16, 0, t0:t1], abr[0:16, 0, t0:t1], 1.0, None, ALU.add, None, r=[abr.k(0)], w=[abr.k(0)])
                self.act(abr[0:16, 0, t0:t1], abr[0:16, 0, t0:t1], AF.Ln, r=[abr.k(0)], w=[abr.k(0)])
                self.ts(abr[0:16, 0, t0:t1], abr[0:16, 0, t0:t1], nal[0:16, 0:1], None, ALU.mult, None, r=[abr.k(0), nal], w=[abr.k(0)])
            else:
                self.act(abr[0:16, 1, t0:t1], ps, AF.Sigmoid, r=[ptok], w=[abr.k(1)])

        wst_ab = self.alloc_wst(8, 32)
        self.linear(Win, D, groups, lambda kc, t0, t1: hT[:, kc, t0:t1], [hT], TT, evac_ab, wst=wst_ab)
        for p in range(NP):
            b = self.psbank()
            for which in range(2):
                self.tr(self.PS[:, b, which * 16:(which + 1) * 16], abr[0:16, which, p * 128:(p + 1) * 128], self.ident[0:16, 0:16],
                        r=[abr.k(which), self.ident], w=[self.PS.k(b)])
            self.cp(abtm[:, p, :], self.PS[:, b, 0:32], r=[self.PS.k(b)], w=[abtm])
        S.barrier()
        A.release(m1)
        self.mark('gdn heads')
        m_heads = A.mark()
        zp = A2.alloc("gd_zp", (TT + 4,), F32)
        qT = A2.alloc("gd_qT", (TT,), F32)
        kT = A2.alloc("gd_kT", (TT,), F32)
        vT = A2.alloc("gd_vT", (TT,), F32)
        ktm = A2.alloc("gd_ktm", (NP, 128), F32)
        vtm = A2.alloc("gd_vtm", (NP, 128), F32)
        oTh = A2.alloc("gd_oT", (SEQ,), F32)
        gate = A.alloc("gd_gate", (SEQ,), BF16)
        wst = self.alloc_wst(8, 128, nstage=1)
        _sq = A.alloc("gd_sq", (512,), F32)
        sqt = [_sq, _sq]
        rstd = A.alloc("gd_rstd", (512,), F32)
        St = [A.alloc(f"gd_S{i}", (128,), F32) for i in range(2)]
        vnew = [A.alloc(f"gd_vn{i}", (128,), F32) for i in range(2)]
        names = ("u", "wT", "qdT", "qkT", "kdec", "sc")
        PB = [[{n: A.alloc(f"gd_{n}{sl}{k}", (128,) if n != "sc" else (8,), F32) for n in names} for k in range(4)] for sl in range(2)]
        TMP = [{n: A.alloc(f"gd_t_{n}{k}", (128,), F32) for n in ("gmat", "gam", "ecr", "t", "A0", "B0", "T0", "T1", "P0", "P1", "Lb", "Ub", "Y1", "Y2", "vb", "kbe", "qk")}
               for k in range(4)]
        bec = [A.alloc(f"gd_bec{k}", (1,), F32) for k in range(4)]
        self.gdn_top = A.top
        for i in range(4):
            self.S.op("pool", lambda e, i=i: e.memset(zp[:, (0, CTX + 1, CTX + 2, TT + 3)[i]:(0, CTX + 1, CTX + 2, TT + 3)[i] + 1], 0.0), w=[zp])

        pf = list(range(NP))
        pb = [1, 0] + list(range(NP - 1, 1, -1))

        held = set()

        def acq():
            for _ in range(8):
                self.ps_rr = (self.ps_rr + 1) % 8
                if self.ps_rr not in held:
                    held.add(self.ps_rr)
                    return self.ps_rr
            raise RuntimeError("no free PSUM bank")

        def rel(b_):
            held.discard(b_)

        def prep(sl, k, dr, p, h):
            T_, O_ = TMP[k], PB[sl][k]
            bec_ = bec[k]
            c0, c1 = p * 128, (p + 1) * 128
            gcol = abtm[:, p, dr * 8 + h:dr * 8 + h + 1]
            bcol = abtm[:, p, 16 + dr * 8 + h:16 + dr * 8 + h + 1]
            SYM = lambda lv: msk[:, 10 + lv, :]
            self.ts(T_["gmat"][:, :], self.ones[:, :], gcol, None, ALU.mult, None, r=[self.ones, abtm], w=[T_["gmat"]])
            self.ts(T_["vb"][:, :], vtm[:, p, :], bcol, None, ALU.mult, None, r=[vtm, abtm], w=[T_["vb"]], eng="pool")
            yield
            bx = acq()
            X = self.PS[:, bx, :]
            xt_ = self.PS.k(bx)
            self.mm(X[:, 0:128], TRI(dr), T_["gmat"][:, :], True, False, r=[msk, T_["gmat"]], w=[xt_])
            self.mm(X[:, 0:128], T_["gmat"][:, :], NTRI(dr), False, True, r=[msk, T_["gmat"]], w=[xt_])
            self.mm(X[:, 128:256], T_["gmat"][:, :], TRI(dr), True, True, r=[msk, T_["gmat"]], w=[xt_])
            self.mm(X[:, 256:257], TRI(dr), gcol, True, True, r=[msk, abtm], w=[xt_])
            self.mm(X[:, 257:258], REM(dr), gcol, True, True, r=[msk, abtm], w=[xt_])
            self.mm(X[:, 258:260], T_["gmat"][:, :], blk[:, :], True, True, r=[blk, T_["gmat"]], w=[xt_])
            yield
            self.act(T_["gam"][:, :], X[:, 0:128], AF.Relu, r=[xt_], w=[T_["gam"]], scale=-1.0)
            self.act(T_["ecr"][:, :], X[:, 128:256], AF.Exp, r=[xt_], w=[T_["ecr"]])
            self.act(O_["sc"][:, 0:4], X[:, 256:260], AF.Exp, r=[xt_], w=[O_["sc"]])
            rel(bx)
            yield
            self.act(T_["gam"][:, :], T_["gam"][:, :], AF.Exp, r=[T_["gam"]], w=[T_["gam"]], scale=-1.0)
            self.tt(bec_[:, :], bcol, O_["sc"][:, 0:1], ALU.mult, r=[abtm, O_["sc"]], w=[bec_])
            self.ts(O_["kdec"][:, :], ktm[:, p, :], O_["sc"][:, 1:2], None, ALU.mult, None, r=[ktm, O_["sc"]], w=[O_["kdec"]])
            self.tt(O_["qdT"][:, :], qT[:, c0:c1], T_["ecr"][:, :], ALU.mult, r=[qT, T_["ecr"]], w=[O_["qdT"]], eng="pool")
            yield
            self.tt(T_["gam"][:, :], T_["gam"][:, :], INCL(dr), ALU.mult, r=[T_["gam"], msk], w=[T_["gam"]])
            self.ts(T_["kbe"][:, :], ktm[:, p, :], bec_[:, 0:1], None, ALU.mult, None, r=[ktm, bec_], w=[T_["kbe"]], eng="pool")
            by = acq()
            Y = self.PS[:, by, :]
            yt_ = self.PS.k(by)
            self.mm(Y[:, 0:128], kT[:, c0:c1], kT[:, c0:c1], True, True, r=[kT], w=[yt_])
            self.mm(Y[:, 128:256], qT[:, c0:c1], kT[:, c0:c1], True, True, r=[qT, kT], w=[yt_])
            yield
            self.tt(T_["t"][:, :], Y[:, 0:128], T_["gam"][:, :], ALU.mult, r=[yt_, T_["gam"]], w=[T_["t"]])
            self.tt(T_["qk"][:, :], Y[:, 128:256], T_["gam"][:, :], ALU.mult, r=[yt_, T_["gam"]], w=[T_["qk"]])
            rel(by)
            yield
            self.stt(T_["A0"][:, :], T_["t"][:, :], bcol, STRICT(dr), ALU.mult, ALU.mult, r=[T_["t"], abtm, msk], w=[T_["A0"]])
            yield
            bz = acq()
            Z = self.PS[:, bz, :]
            zt_ = self.PS.k(bz)
            self.mm(Z[:, 0:128], T_["A0"][:, :], self.ident[:, :], True, True, r=[T_["A0"], self.ident], w=[zt_])
            self.mm(Z[:, 128:256], T_["qk"][:, :], self.ident[:, :], True, True, r=[T_["qk"], self.ident], w=[zt_])
            self.tt(T_["Lb"][:, :], T_["A0"][:, :], SYM(0), ALU.mult, r=[T_["A0"], msk], w=[T_["Lb"]])
            yield
            self.act(T_["B0"][:, :], Z[:, 0:128], AF.Copy, r=[zt_], w=[T_["B0"]])
            self.act(O_["qkT"][:, :], Z[:, 128:256], AF.Copy, r=[zt_], w=[O_["qkT"]])
            rel(bz)
            self.tt(T_["T0"][:, :], self.ident[:, :], T_["Lb"][:, :], ALU.subtract, r=[self.ident, T_["Lb"]], w=[T_["T0"]])
            yield
            self.tt(T_["Ub"][:, :], T_["B0"][:, :], SYM(0), ALU.mult, r=[T_["B0"], msk], w=[T_["Ub"]])
            yield
            self.tt(T_["P0"][:, :], self.ident[:, :], T_["Ub"][:, :], ALU.subtract, r=[self.ident, T_["Ub"]], w=[T_["P0"]])
            Tcur, Tnxt, Pcur, Pnxt = "T0", "T1", "P0", "P1"
            for lv in range(1, 6):
                last = (lv == 5)
                self.tt(T_["Lb"][:, :], T_["A0"][:, :], SYM(lv), ALU.mult, r=[T_["A0"], msk], w=[T_["Lb"]])
                if not last:
                    self.tt(T_["Ub"][:, :], T_["B0"][:, :], SYM(lv), ALU.mult, r=[T_["B0"], msk], w=[T_["Ub"]])
                yield
                bw = acq()
                Wp = self.PS[:, bw, :]
                wt_ = self.PS.k(bw)
                self.mm(Wp[:, 0:128], T_["Lb"][:, :], T_[Pcur][:, :], True, True, r=[T_["Lb"], T_[Pcur]], w=[wt_])
                if not last:
                    self.mm(Wp[:, 128:256], T_["Ub"][:, :], T_[Tcur][:, :], True, True, r=[T_["Ub"], T_[Tcur]], w=[wt_])
                yield
                self.act(T_["Y1"][:, :], Wp[:, 0:128], AF.Copy, r=[wt_], w=[T_["Y1"]])
                if not last:
                    self.act(T_["Y2"][:, :], Wp[:, 128:256], AF.Copy, r=[wt_], w=[T_["Y2"]])
                rel(bw)
                yield
                bv = acq()
                Vp = self.PS[:, bv, :]
                vt_ = self.PS.k(bv)
                self.mm(Vp[:, 0:128], T_[Tcur][:, :], T_["Y1"][:, :], True, True, r=[T_[Tcur], T_["Y1"]], w=[vt_])
                if not last:
                    self.mm(Vp[:, 128:256], T_[Pcur][:, :], T_["Y2"][:, :], True, True, r=[T_[Pcur], T_["Y2"]], w=[vt_])
                yield
                self.tt(T_[Pnxt][:, :], T_[Pcur][:, :], Vp[:, 0:128], ALU.subtract, r=[T_[Pcur], vt_], w=[T_[Pnxt]])
                if not last:
                    self.tt(T_[Tnxt][:, :], T_[Tcur][:, :], Vp[:, 128:256], ALU.subtract, r=[T_[Tcur], vt_], w=[T_[Tnxt]])
                rel(bv)
                Tcur, Tnxt = Tnxt, Tcur
                Pcur, Pnxt = Pnxt, Pcur
                yield
            P = T_[Pcur]
            bu = acq()
            U = self.PS[:, bu, :]
            ut_ = self.PS.k(bu)
            self.mm(U[:, 0:128], P[:, :], T_["vb"][:, :], True, True, r=[P, T_["vb"]], w=[ut_])
            self.mm(U[:, 128:256], T_["kbe"][:, :], P[:, :], True, True, r=[P, T_["kbe"]], w=[ut_])
            yield
            self.cp(O_["u"][:, :], U[:, 0:128], r=[ut_], w=[O_["u"]])
            self.cp(O_["wT"][:, :], U[:, 128:256], r=[ut_], w=[O_["wT"]])
            rel(bu)
            yield

        def chain(sl, dr, steps):
            Sd = St[dr]
            vn = vnew[dr]
            halves = (0, 1) if dr == 0 else (1, 0)
            for (k, p, first_dir_write) in steps:
                O_ = PB[sl][k]
                for hf in halves:
                    hr = slice(hf * 64, (hf + 1) * 64)
                    ba = acq()
                    self.mm(self.PS[:, ba, 0:128], O_["wT"][:, :], Sd[:, :], True, True, r=[O_["wT"], Sd], w=[self.PS.k(ba)])
                    yield
                    self.tt(vn[hr, :], O_["u"][hr, :], self.PS[hr, ba, 0:128], ALU.subtract, r=[O_["u"], self.PS.k(ba)], w=[vn])
                    rel(ba)
                    yield
                    bs_ = acq()
                    bt_ = self.PS.k(bs_)
                    self.mm(self.PS[:, bs_, 0:128], O_["kdec"][hr, :], vn[hr, :], True, True, r=[O_["kdec"], vn], w=[bt_])
                    if p >= 2:
                        self.mm(self.PS[:, bs_, 128:256], Sd[:, :], O_["qdT"][:, :], True, False, r=[Sd, O_["qdT"]], w=[bt_])
                        self.mm(self.PS[:, bs_, 128:256], vn[hr, :], O_["qkT"][hr, :], False, True, r=[vn, O_["qkT"]], w=[bt_])
                    yield
                    self.stt(Sd[:, :], Sd[:, :], O_["sc"][:, 2 + hf:3 + hf], self.PS[:, bs_, 0:128], ALU.mult, ALU.add,
                             r=[Sd, O_["sc"], bt_], w=[Sd])
                    if p >= 2 and dr in self.gdn_dirs:
                        tk0 = (p - 2) * 128 + hf * 64
                        src_o = self.PS[:, bs_, 128 + hf * 64:128 + (hf + 1) * 64]
                        if first_dir_write and not self.dbg_gdn:
                            self.cp(oTh[:, tk0:tk0 + 64], src_o, r=[bt_], w=[oTh.k(p)])
                        else:
                            self.tt(oTh[:, tk0:tk0 + 64], oTh[:, tk0:tk0 + 64], src_o, ALU.add, r=[bt_, oTh.k(p)], w=[oTh.k(p)])
                    rel(bs_)
                    yield

        def run_rr(gens):
            gens = list(gens)
            while gens:
                for g_ in list(gens):
                    try:
                        next(g_)
                    except StopIteration:
                        gens.remove(g_)

        for h in range(8 if self.gdn_stop >= 2 else 0):
            for part in range(4):
                groups = [(part * 1024 + h * 128, 128, [(0, 128, part)])]
                if part < 3:
                    def evac_p(part_, t0, t1, ps, ptok):
                        self.act(zp[:, tcol(t0):tcol(t0) + (t1 - t0)], ps, AF.Copy, r=[ptok], w=[zp])
                    self.linear(Win, D, groups, lambda kc, t0, t1: hT[:, kc, t0:t1], [hT], TT, evac_p, tb=256, wst=wst)
                    dst = (qT, kT, vT)[part]
                    q_ = part * 8 + h
                    for (z0, n, o0) in ((0, CTX, 0), (CTX + 2, SEQ, CTX)):
                        self.ts(dst[:, o0:o0 + n], zp[:, z0:z0 + n], gcw[:, q_, 0:1], None, ALU.mult, None, r=[zp, gcw], w=[dst])
                        self.stt(dst[:, o0:o0 + n], zp[:, z0 + 1:z0 + 1 + n], gcw[:, q_, 1:2], dst[:, o0:o0 + n], ALU.mult, ALU.add, r=[zp, gcw, dst], w=[dst])
                        self.stt(dst[:, o0:o0 + n], zp[:, z0 + 2:z0 + 2 + n], gcw[:, q_, 2:3], dst[:, o0:o0 + n], ALU.mult, ALU.add, r=[zp, gcw, dst], w=[dst])
                    self.act(dst[:, :], dst[:, :], AF.Silu, r=[dst], w=[dst])
                else:
                    def evac_g(part_, t0, t1, ps, ptok):
                        if t0 >= CTX:
                            self.act(gate[:, t0 - CTX:t1 - CTX], ps, AF.Silu, r=[ptok], w=[gate])
                    self.linear(Win, D, groups, lambda kc, t0, t1: hT[:, kc, t0:t1], [hT], TT, evac_g, tb=256, wst=wst)
            if self.gdn_stop < 2.3:
                continue
            if h == 0: self.mark('gdn h0 l2norm')
            for (src, sc_) in ((qT, 128.0 ** -0.5), (kT, 1.0)):
                for t0 in range(0, TT, 512):
                    t1 = min(TT, t0 + 512)
                    n = t1 - t0
                    b = self.psbank()
                    s_ = sqt[0]
                    self.act(s_[:, 0:n], src[:, t0:t1], AF.Square, r=[src], w=[s_])
                    self.mm(self.PS[:, b, 0:n], self.ones[:, :], s_[:, 0:n], True, True, r=[self.ones, s_], w=[self.PS.k(b)])
                    self.rsqrt(rstd[:, 0:n], self.PS[:, b, 0:n], 1.0, RMS_EPS, rstd, in_toks=[self.PS.k(b)])
                    self.stt(src[:, t0:t1], src[:, t0:t1], sc_, rstd[:, 0:n], ALU.mult, ALU.mult, r=[src, rstd], w=[src])
            if self.gdn_stop < 2.6:
                continue
            for p in range(NP):
                b = self.psbank()
                self.mm(self.PS[:, b, 0:128], kT[:, p * 128:(p + 1) * 128], self.ident[:, :], True, True, r=[kT, self.ident], w=[self.PS.k(b)])
                self.mm(self.PS[:, b, 128:256], vT[:, p * 128:(p + 1) * 128], self.ident[:, :], True, True, r=[vT, self.ident], w=[self.PS.k(b)])
                if p % 2:
                    self.cp(ktm[:, p, :], self.PS[:, b, 0:128], r=[self.PS.k(b)], w=[ktm])
                    self.cp(vtm[:, p, :], self.PS[:, b, 128:256], r=[self.PS.k(b)], w=[vtm])
                else:
                    self.act(ktm[:, p, :], self.PS[:, b, 0:128], AF.Copy, r=[self.PS.k(b)], w=[ktm])
                    self.act(vtm[:, p, :], self.PS[:, b, 128:256], AF.Copy, r=[self.PS.k(b)], w=[vtm])
            if self.gdn_stop < 3:
                continue
            for dr in range(2):
                self.S.op("pool", lambda e, dr=dr: e.memset(St[dr][:, :], 0.0), w=[St[dr]])
            if self.dbg_gdn:
                for p_ in range(2, NP):
                    self.S.op("pool", lambda e, p_=p_: e.memset(oTh[:, (p_ - 2) * 128:(p_ - 1) * 128], 0.0), w=[oTh.k(p_)])
            if h == 0: self.mark('gdn h0 recur')
            NBAT = NP // 2
            def batch_preps(bi):
                gl = []
                for st in range(2):
                    i_ = 2 * bi + st
                    gl.append(prep(bi % 2, st * 2 + 0, 0, pf[i_], h))
                    gl.append(prep(bi % 2, st * 2 + 1, 1, pb[i_], h))
                return gl
            def batch_chains(bi):
                fs, bs2 = [], []
                for st in range(2):
                    i_ = 2 * bi + st
                    pfi, pbi = pf[i_], pb[i_]
                    fs.append((st * 2 + 0, pfi, (pfi >= 2 and pfi < (NP + 1 - pfi) + 0.5)))
                    bs2.append((st * 2 + 1, pbi, (pbi >= 2 and (NP + 1 - pbi) < pbi)))
                return [chain(bi % 2, 0, fs), chain(bi % 2, 1, bs2)]
            run_rr(batch_preps(0))
            for bi in range(NBAT):
                gl = []
                if bi + 1 < NBAT:
                    gl += batch_preps(bi + 1)
                if self.gdn_stop >= 4:
                    gl += batch_chains(bi)
                run_rr(gl)
            if self.dbg_gdn and h == 0:
                self.ld(self.dbg_o[:, :], oTh[:, :], r=[oTh.k(p_) for p_ in range(2, NP)], w=[])
            if h == 0: self.mark('gdn h0 onorm')
            for t0 in range(0, SEQ, 512):
                b = self.psbank()
                s_ = sqt[1]
                ptoks = [oTh.k(2 + (t0 // 128) + q) for q in range(4)]
                self.act(s_[:, :], oTh[:, t0:t0 + 512], AF.Square, r=ptoks, w=[s_])
                self.mm(self.PS[:, b, 0:512], self.ones[:, :], s_[:, :], True, True, r=[self.ones, s_], w=[self.PS.k(b)])
                self.rsqrt(rstd[:, :], self.PS[:, b, 0:512], 1.0 / 128, RMS_EPS, rstd, in_toks=[self.PS.k(b)])
                self.stt(s_[:, :], oTh[:, t0:t0 + 512], gon[:, 0:1], rstd[:, :], ALU.mult, ALU.mult, r=ptoks + [gon, rstd], w=[s_])
                gt_ = gtile[(t0 // 512) % 2]
                self.tt(gt_[:, :], s_[:, :], gate[:, t0:t0 + 512], ALU.mult, r=[s_, gate], w=[gt_])
                self.ld(self.gscr[h, :, t0:t0 + 512], gt_[:, :], r=[gt_], w=[self.gscr_tok])
            S.barrier()
        self.mark('gdn outproj')
        S.barrier()
        A.release(m0)
        gT = A.alloc("gd_gT", (8, SEQ), BF16)
        self.ld(gT[:, :, :], self.gscr.rearrange("h p t -> p h t"), r=[self.gscr_tok], w=[gT])
        self.ld(xT[:, :, :], self.xscr.rearrange("p (a t) -> p a t", a=8), r=[self.xscr_tok], w=[xT])
        S.barrier()
        groups = [(dc * 128, 128, [(0, 128, dc)]) for dc in range(8)]

        def evac_o(dc, t0, t1, ps, ptok):
            self.stt(xT[:, dc, t0:t1], ps, mods[:, 16 + dc, 0:1], xT[:, dc, t0:t1], ALU.mult, ALU.add, r=[ptok, mods, xT], w=[xT])

        self.linear(d["gdn_w_out"][j], D, groups, lambda kc, t0, t1: gT[:, kc, t0:t1], [gT], SEQ, evac_o)
        A.release(m0)
        self.layernorm(xT, SEQ, li, 0)

    def build(self):
        nc = self.nc
        es = self.es
        hc = host_consts()
        inp = self.inp
        inp("x", [SEQ, D]); inp("ctx", [CTX, D]); inp("ccol", [128, 8, 2])
        inp("mod_w", [DEPTH, D, 6 * D]); inp("ffn_w_in", [DEPTH, D, 2 * DFF]); inp("ffn_w_out", [DEPTH, DFF, D])
        inp("hy_w_in", [2, D, 3 * D]); inp("hy_w_out", [2, D, D])
        inp("hy_fw1", [2, 33, 64]); inp("hy_fw2", [2, 64, 64]); inp("hy_fw3", [2, 64, 64]); inp("hy_fw4", [2, 64, 2048])
        inp("modb", [128, DEPTH, 48]); inp("lng", [128, DEPTH * 16]); inp("lnb", [128, DEPTH * 16])
        inp("hbin", [128, 2, 24]); inp("hcw", [128, 2, 24, 3]); inp("hcb", [128, 2, 24]); inp("hbout", [128, 2, 8])
        inp("hyp", [64, 2, 4]); inp("hskip", [1, 2, 1024])
        for L in (SEQ, CTX):
            nt = L // 128
            TB = min(512, L)
            inp(f"fw{L}", [nt, 2, 128, nt, 128], BF16); inp(f"inv{L}", [L // TB, 2, 128, nt, TB], BF16)
            inp(f"z0T{L}", [33, L]); inp(f"win{L}", [L, D])
        inp("ident", [128, 128]); inp("identb", [128, 128], BF16); inp("ones", [128, 128])
        self.extra_inputs()
        self.out = nc.dram_tensor("out", [SEQ, D], F32, kind="ExternalOutput").ap()
        if self.debug_out:
            self.out_ctx = nc.dram_tensor("out_ctx", [CTX, D], F32, kind="ExternalOutput").ap()
        if self.dbg_gdn:
            self.dbg_o = nc.dram_tensor("dbg_o", [128, SEQ], F32, kind="ExternalOutput").ap()
            self.dbg_s = nc.dram_tensor("dbg_s", [2, 128, 128], F32, kind="ExternalOutput").ap()
            self.dbg_s0 = nc.dram_tensor("dbg_s0", [2, 128, 128], F32, kind="ExternalOutput").ap()
            self.dbg_ab = nc.dram_tensor("dbg_ab", [128, 18, 32], F32, kind="ExternalOutput").ap()
            self.dbg_pb = nc.dram_tensor("dbg_pb", [5, 128, 128], F32, kind="ExternalOutput").ap()
            self.dbg_sc = nc.dram_tensor("dbg_sc", [128, 8], F32, kind="ExternalOutput").ap()
        self.xscr = self.scr("xscr", [128, 8 * SEQ]); self.xscr_tok = Tok("xscr")
        self.hscr = {L: self.scr(f"hscr{L}", [2, L // 128, 128, D]) for L in (SEQ, CTX)}
        self.hscr_tok = {L: Tok(f"hscr{L}") for L in (SEQ, CTX)}
        self.x0scr = self.scr("x0scr", [8, 128, SEQ], BF16); self.x0scr_tok = Tok("x0scr")
        self.uscr = self.scr("uscr", [128, SEQ // 128, D], BF16); self.uscr_tok = Tok("uscr")
        self.extra_scratch()
        NA = 36600
        xt = es.enter_context(nc.sbuf_tensor("XT", [128, 8 * SEQ], F32))
        at = es.enter_context(nc.sbuf_tensor("ARENA", [128, NA], F32))
        pst = es.enter_context(nc.psum_tensor("PS", [128, 8, 512], F32))
        self.PS = Buf(pst, "PS")
        self.ps_rr = 0
        self.ps_pool = list(range(8))
        self.xT = Buf(xt[:, :].rearrange("p (a t) -> p a t", a=8), "xT")
        self.A2 = Arena(xt, 8 * SEQ)
        self.A = A = Arena(at, NA)
        self.S = S = Sched(nc)
        self.cT = A.alloc("cT", (8, CTX), F32)
        self.ident = A.alloc("ident", (128,), F32); self.identb = A.alloc("identb", (128,), BF16)
        self.ones = A.alloc("ones", (128,), F32)
        self.onesb = A.alloc("onesb", (128,), BF16)
        self.cs = A.alloc("cs", (8, 2), F32)
        self.mods = A.alloc("mods", (48, 2), F32)
        self.mods_alt = A.alloc("mods_alt", (48, 2), F32)
        self.mods_ready = None
        self.modb = A.alloc("modb", (DEPTH, 48), F32)
        self.lng = A.alloc("lng", (DEPTH * 16,), F32); self.lnb = A.alloc("lnb", (DEPTH * 16,), F32)
        self.hbin = A.alloc("hbin", (2, 24), F32); self.hcw = A.alloc("hcw", (2, 24, 3), F32)
        self.hcb = A.alloc("hcb", (2, 24), F32); self.hbout = A.alloc("hbout", (2, 8), F32)
        self.hyp = A.alloc("hyp", (2, 4), F32); self.hskip = A.alloc("hskip", (2, 1024), F32)
        d = self.din
        self.ld(self.ident[:, :], d["ident"][:, :], w=[self.ident]); self.ld(self.identb[:, :], d["identb"][:, :], w=[self.identb])
        self.ld(self.ones[:, :], d["ones"][:, :], w=[self.ones])
        self.ld(self.onesb[:, :], d["onesb"][:, :], w=[self.onesb])
        self.ld(self.cs[:, :, :], d["ccol"][:, :, :], w=[self.cs])
        self.ld(self.modb[:, :, :], d["modb"][:, :, :], w=[self.modb])
        self.ld(self.lng[:, :], d["lng"][:, :], w=[self.lng]); self.ld(self.lnb[:, :], d["lnb"][:, :], w=[self.lnb])
        self.ld(self.hbin[:, :, :], d["hbin"][:, :, :], w=[self.hbin]); self.ld(self.hcw[:, :, :, :], d["hcw"][:, :, :, :], w=[self.hcw])
        self.ld(self.hcb[:, :, :], d["hcb"][:, :, :], w=[self.hcb]); self.ld(self.hbout[:, :, :], d["hbout"][:, :, :], w=[self.hbout])
        self.ld(self.hyp[0:64, :, :], d["hyp"][:, :, :], w=[self.hyp]); self.ld(self.hskip[0:1, :, :], d["hskip"][:, :, :], w=[self.hskip])
        self.extra_persistent()
        self.act(self.cs[:, :, :], self.cs[:, :, :], AF.Silu, r=[self.cs], w=[self.cs])
        self.load_fm(d["x"], self.xT, SEQ)
        self.load_fm(d["ctx"], self.cT, CTX)
        S.barrier()
        for li in self.layers:
            self.layer(li)
        self.store_tm(self.xT, self.out, SEQ)
        if self.debug_out:
            self.store_tm(self.cT, self.out_ctx, CTX)
        S.barrier()
        S.emit()
        return nc

    def extra_inputs(self):
        inp = self.inp
        TT = CTX + SEQ
        inp("mla_w_in", [1, D, 704]); inp("mla_w_uq", [1, 384, 1536]); inp("mla_w_ukv", [1, 256, 2048]); inp("mla_w_out", [1, D, D])
        inp("mla_qg", [128, 1, 3]); inp("mla_kvg", [128, 1, 2])
        inp("gdn_w_in", [1, D, 4128]); inp("gdn_w_out", [1, D, D]); inp("gmask", [128, 16, 128]); inp("gblk", [128, 2])
        inp("gdn_cw", [128, 24, 3]); inp("gdn_ab", [16, 2]); inp("gdn_on", [128, 1])
        inp("ropeC", [64, TT]); inp("ropeS", [64, TT]); inp("ropeP", [64, 64]); inp("onesb", [128, 128], BF16)

    def extra_scratch(self):
        self.gscr = self.scr("gscr", [8, 128, SEQ], BF16)
        self.gscr_tok = Tok("gscr")

    def extra_persistent(self):
        pass

    def load_fm(self, src, dstT, T):
        A = self.A
        m0 = A.mark()
        st = [A.alloc(f"ldst{i}", (D,), F32) for i in range(2)]
        for tc in range(T // 128):
            s_ = st[tc % 2]
            self.ld(s_[:, :], src[tc * 128:(tc + 1) * 128, :], w=[s_])
            for half in range(2):
                b = self.psbank()
                for q in range(4):
                    dc = half * 4 + q
                    self.tr(self.PS[:, b, q * 128:(q + 1) * 128], s_[:, dc * 128:(dc + 1) * 128], self.ident[:, :],
                            r=[s_, self.ident], w=[self.PS.k(b)])
                src_ps = self.PS[:, b, :].rearrange("p (q t) -> p q t", q=4)
                if half:
                    self.cp(dstT[:, 4:8, tc * 128:(tc + 1) * 128], src_ps, r=[self.PS.k(b)], w=[dstT])
                else:
                    self.act(dstT[:, 0:4, tc * 128:(tc + 1) * 128], src_ps, AF.Copy, r=[self.PS.k(b)], w=[dstT])
        self.S.barrier()
        A.release(m0)

    def store_tm(self, srcT, dst, T):
        A = self.A
        m0 = A.mark()
        st = [A.alloc(f"stst{i}", (D,), F32) for i in range(2)]
        for tc in range(T // 128):
            s_ = st[tc % 2]
            for half in range(2):
                b = self.psbank()
                for q in range(4):
                    dc = half * 4 + q
                    self.tr(self.PS[:, b, q * 128:(q + 1) * 128], srcT[:, dc, tc * 128:(tc + 1) * 128], self.ident[:, :],
                            r=[srcT, self.ident], w=[self.PS.k(b)])
                if half:
                    self.cp(s_[:, 512:1024], self.PS[:, b, :], r=[self.PS.k(b)], w=[s_])
                else:
                    self.act(s_[:, 0:512], self.PS[:, b, :], AF.Copy, r=[self.PS.k(b)], w=[s_])
            self.ld(dst[tc * 128:(tc + 1) * 128, :], s_[:, :], r=[s_])
        self.S.barrier()
        A.release(m0)

    def layer(self, li):
        kind, j = li % 3, li // 3
        ctx_out = any(l % 3 != 0 for l in range(li + 1, DEPTH))
        self.mark(f"L{li} modulation")
        self.modulation(li)
        self.mark(f"L{li} mixer")
        if kind == 0:
            self.hyena(j, li, self.xT, SEQ, 0)
            if ctx_out:
                self.hyena(j, li, self.cT, CTX, 1)
        elif kind == 1:
            self.mla(j, li, ctx_out)
        else:
            self.gdn(j, li, ctx_out)
        self.mark(f"L{li} ffn")
        self.ffn(li, ctx_out)
        self.mark(f"L{li} end")


def prep_shared(inputs):
    f = lambda a: np.ascontiguousarray(np.asarray(a, np.float32))
    hc = host_consts()
    m = {}
    for k in ("mod_w", "ffn_w_in", "ffn_w_out", "hy_w_in", "hy_w_out", "hy_fw1", "hy_fw2", "hy_fw3", "hy_fw4"):
        m[k] = f(inputs[k])
    m["modb"] = f(np.stack([col_layout(inputs["mod_b"][i], 48) for i in range(DEPTH)], axis=1))
    m["lng"] = f(np.concatenate([col_layout(inputs["ln_g"][i, w], 8) for i in range(DEPTH) for w in range(2)], axis=1))
    m["lnb"] = f(np.concatenate([col_layout(inputs["ln_b"][i, w], 8) for i in range(DEPTH) for w in range(2)], axis=1))
    m["hbin"] = f(np.stack([col_layout(inputs["hy_b_in"][j], 24) for j in range(2)], axis=1))
    m["hcb"] = f(np.stack([col_layout(inputs["hy_conv_b"][j], 24) for j in range(2)], axis=1))
    m["hcw"] = f(np.stack([np.stack([col_layout(inputs["hy_conv_w"][j, k], 24) for k in range(3)], axis=-1) for j in range(2)], axis=1))
    m["hbout"] = f(np.stack([col_layout(inputs["hy_b_out"][j], 8) for j in range(2)], axis=1))
    m["hyp"] = f(np.stack([np.stack([inputs["hy_fb1"][j], inputs["hy_fb2"][j], inputs["hy_fb3"][j], inputs["hy_freq"][j]], axis=-1)
                           for j in range(2)], axis=1))
    m["hskip"] = f(np.asarray(inputs["hy_skip"])[None, :, :])
    m["gdn_cw"] = f(np.stack([col_layout(inputs["gdn_conv_w"][0, k], 24) for k in range(3)], axis=-1))
    m["gdn_ab"] = f(np.stack([np.asarray(inputs["gdn_a_log"][0]).reshape(16), np.asarray(inputs["gdn_dt_bias"][0]).reshape(16)], axis=-1))
    m["gdn_on"] = f(np.asarray(inputs["gdn_o_norm"][0]).reshape(128, 1))
    for k in ("mla_w_in", "mla_w_uq", "mla_w_ukv", "mla_w_out", "gdn_w_in", "gdn_w_out"):
        m[k] = f(inputs[k])
    m["mla_qg"] = f(np.stack([col_layout(inputs["mla_q_norm"][j], 3) for j in range(1)], axis=1))
    m["mla_kvg"] = f(np.stack([col_layout(inputs["mla_kv_norm"][j], 2) for j in range(1)], axis=1))
    for k, v in hc.items():
        m[k] = v
    return m


def prep_core(inputs, b):
    f = lambda a: np.ascontiguousarray(np.asarray(a, np.float32))
    ccol = np.stack([col_layout(inputs["c"][b], 8), col_layout(inputs["c_ctx"], 8)], axis=-1)
    return {"x": f(inputs["x"][b]), "ctx": f(inputs["ctx"][b]), "ccol": f(ccol)}


_PROG_CACHE = {}


def kernel(**inputs):
    if "prog" not in _PROG_CACHE:
        _PROG_CACHE["prog"] = Prog().build()
    nc = _PROG_CACHE["prog"]
    shared = prep_shared(inputs)
    in_maps = []
    for b in range(NCORES):
        m = dict(shared)
        m.update(prep_core(inputs, b))
        in_maps.append(m)
    res = run_bass_kernel_spmd(nc, in_maps, core_ids=list(range(NCORES)))
    return np.stack([np.asarray(r["out"], np.float32) for r in res.results], axis=0)
```
